# Optimizing a Trainium2 kernel written in Bass

```python
import jax, jax.numpy as jnp
from jax import lax
import numpy as np

D_MODEL = 2048
BATCH = 8
SEQ = 2048
DEPTH = 2

CHUNK = 64
N_MIXERS = 2
N_LAYERS_A = (DEPTH + 1) // 2
N_LAYERS_B = DEPTH // 2
D_FF = 4 * D_MODEL
EPS = 1e-6

A_HEADS = 16
A_NOPE = 128
A_ROPE = 64
A_V = D_MODEL // A_HEADS
A_Q_RANK = D_MODEL // 4
A_KV_RANK = D_MODEL // 8
IDX_HEADS = 16
IDX_DIM = 64
IDX_ROPE = 32
TOPK_MAX = 256
Q_BLOCK = 128
ROPE_BASE = 10000.0
A_SCALE = (A_NOPE + A_ROPE) ** -0.5
A_SPLITS = [A_Q_RANK, A_Q_RANK + A_KV_RANK, A_Q_RANK + A_KV_RANK + A_ROPE,
            A_Q_RANK + A_KV_RANK + A_ROPE + IDX_DIM]
A_IN = A_Q_RANK + A_KV_RANK + A_ROPE + IDX_DIM + IDX_HEADS

R_HEAD = 64
R_HEADS = D_MODEL // R_HEAD
R_DECAY_LORA = 96
R_AAA_LORA = 96
R_GATE_LORA = 256
R_GN_EPS = R_HEAD * 1e-5

kernel_name = "hybrid_dsa_rwkv7_adaln_trunk"


def rms_norm(x):
    xf = x.astype(jnp.float32)
    return (xf * lax.rsqrt(jnp.mean(xf * xf, axis=-1, keepdims=True) + EPS)).astype(x.dtype)


def layer_norm(x, eps):
    xf = x.astype(jnp.float32)
    mu = jnp.mean(xf, axis=-1, keepdims=True)
    var = jnp.mean(jnp.square(xf - mu), axis=-1, keepdims=True)
    return ((xf - mu) * lax.rsqrt(var + eps)).astype(x.dtype)


def rope_angles(positions, dim):
    inv = 1.0 / (ROPE_BASE ** (jnp.arange(0, dim, 2, dtype=jnp.float32) / dim))
    ang = positions.astype(jnp.float32)[..., None] * inv
    return jnp.cos(ang), jnp.sin(ang)


def apply_rope(x, cos, sin):
    x1, x2 = jnp.split(x, 2, axis=-1)
    c = cos[:, :, None, :].astype(x.dtype)
    s = sin[:, :, None, :].astype(x.dtype)
    return jnp.concatenate([x1 * c - x2 * s, x2 * c + x1 * s], axis=-1)


def dsa_mixer(h, cos_a, sin_a, cos_i, sin_i, w_in, q_norm_g, kv_norm_g, w_uq, w_qidx,
              kidx_ln_g, kidx_ln_b, w_uk, w_uv, w_o):
    B, S, _ = h.shape
    topk = min(TOPK_MAX, S // 4)
    proj = h @ w_in
    c_q, c_kv, k_rope, k_idx, w_idx = jnp.split(proj, A_SPLITS, axis=-1)
    c_q = rms_norm(c_q) * q_norm_g
    c_kv = rms_norm(c_kv) * kv_norm_g
    q = (c_q @ w_uq).reshape(B, S, A_HEADS, A_NOPE + A_ROPE)
    q_nope = q[..., :A_NOPE]
    q_rope = apply_rope(q[..., A_NOPE:], cos_a, sin_a)
    k_rope = apply_rope(k_rope[:, :, None, :], cos_a, sin_a)[:, :, 0, :]
    keys = jnp.concatenate([c_kv, k_rope], axis=-1)
    q_idx = (c_q @ w_qidx).reshape(B, S, IDX_HEADS, IDX_DIM)
    q_idx = jnp.concatenate([apply_rope(q_idx[..., :IDX_ROPE], cos_i, sin_i), q_idx[..., IDX_ROPE:]], axis=-1)
    k_idx = layer_norm(k_idx, EPS) * kidx_ln_g + kidx_ln_b
    k_idx = jnp.concatenate([apply_rope(k_idx[:, :, None, :IDX_ROPE], cos_i, sin_i)[:, :, 0, :],
                             k_idx[..., IDX_ROPE:]], axis=-1)
    w_idx = w_idx * (IDX_HEADS ** -0.5 * IDX_DIM ** -0.5)

    nb = S // Q_BLOCK

    def to_blocks(t):
        return jnp.moveaxis(t.reshape((B, nb, Q_BLOCK) + t.shape[2:]), 1, 0)

    def block(args):
        bi, qn, qr, qi, wi = args
        t = bi * Q_BLOCK + jnp.arange(Q_BLOCK)
        q_chunk = t // CHUNK
        allowed = (jnp.arange(S)[None, :] // CHUNK) <= q_chunk[:, None]
        rel = jax.nn.relu(jnp.einsum('bqhd,bsd->bqhs', qi, k_idx))
        score = jnp.einsum('bqhs,bqh->bqs', rel, wi).astype(jnp.float32)
        score = jnp.where(allowed[None], score, -jnp.inf)
        _, idx = lax.top_k(score, topk)
        valid = (idx // CHUNK) <= q_chunk[None, :, None]
        sel = jax.vmap(lambda kb, ib: kb[ib])(keys, idx)
        c_sel = sel[..., :A_KV_RANK]
        kr_sel = sel[..., A_KV_RANK:]
        q_lat = jnp.einsum('bqhd,rhd->bqhr', qn, w_uk)
        logits = (jnp.einsum('bqhr,bqkr->bqhk', q_lat, c_sel)
                  + jnp.einsum('bqhd,bqkd->bqhk', qr, kr_sel)).astype(jnp.float32) * A_SCALE
        logits = jnp.where(valid[:, :, None, :], logits, -jnp.inf)
        p = jax.nn.softmax(logits, axis=-1).astype(h.dtype)
        o_lat = jnp.einsum('bqhk,bqkr->bqhr', p, c_sel)
        o = jnp.einsum('bqhr,rhd->bqhd', o_lat, w_uv)
        return o.reshape(B, Q_BLOCK, A_HEADS * A_V)

    out = lax.map(block, (jnp.arange(nb), to_blocks(q_nope), to_blocks(q_rope),
                          to_blocks(q_idx), to_blocks(w_idx)))
    out = jnp.moveaxis(out, 0, 1).reshape(B, S, A_HEADS * A_V)
    return out @ w_o


def rwkv7_mixer(h, mu, w_r, w_k, w_v, w_o, w0, w_w1, w_w2, a0, w_a1, w_a2, w_g1, w_g2,
                k_k, k_a, r_k, gn_g, gn_b):
    B, S, D = h.shape
    h_prev = jnp.pad(h, ((0, 0), (1, 0), (0, 0)))[:, :-1]
    delta = h_prev - h
    xr = h + delta * mu[0]
    xw = h + delta * mu[1]
    xk = h + delta * mu[2]
    xv = h + delta * mu[3]
    xa = h + delta * mu[4]
    xg = h + delta * mu[5]
    r = xr @ w_r
    k = xk @ w_k
    v = xv @ w_v
    w_log = -jax.nn.softplus(-(w0 + jnp.tanh(xw @ w_w1) @ w_w2)) - 0.5
    decay = jnp.exp(-jnp.exp(w_log.astype(jnp.float32)))
    a = jax.nn.sigmoid(a0 + (xa @ w_a1) @ w_a2)
    g = jax.nn.sigmoid(xg @ w_g1) @ w_g2

    def heads(t):
        return t.reshape(B, S, R_HEADS, R_HEAD)

    kk = heads(k * k_k).astype(jnp.float32)
    kk = kk / jnp.maximum(jnp.sqrt(jnp.sum(kk * kk, axis=-1, keepdims=True)), 1e-12)
    k = k * (1 + (a - 1) * k_a)
    rh, kh, vh, ah, dh = heads(r), heads(k), heads(v), heads(a), heads(decay)

    def step(state, inp):
        r_t, k_t, v_t, w_t, kk_t, a_t = inp
        sa = jnp.einsum('bhij,bhj->bhi', state, -kk_t)
        state = (state * w_t[:, :, None, :] + sa[..., None] * (kk_t * a_t)[:, :, None, :]
                 + v_t[..., None] * k_t[:, :, None, :])
        return state, jnp.einsum('bhij,bhj->bhi', state, r_t)

    def seq_first(t):
        return jnp.moveaxis(t.astype(jnp.float32), 1, 0)

    state0 = jnp.zeros((B, R_HEADS, R_HEAD, R_HEAD), jnp.float32)
    _, o = lax.scan(step, state0, (seq_first(rh), seq_first(kh), seq_first(vh),
                                   seq_first(dh), seq_first(kk), seq_first(ah)))
    o = jnp.moveaxis(o, 0, 1)
    o = layer_norm(o, R_GN_EPS).reshape(B, S, D).astype(h.dtype) * gn_g + gn_b
    bonus = jnp.sum(rh * kh * r_k, axis=-1, keepdims=True) * vh
    o = o + bonus.reshape(B, S, D)
    return (o * g) @ w_o


def setup_inputs(seed: int = 0) -> dict:
    key = jax.random.key(seed)
    ks = iter(jax.random.split(key, 48))

    def nrm(shape, scale):
        return jax.random.normal(next(ks), shape, jnp.float32) * scale

    def gain(shape):
        return 1.0 + nrm(shape, 0.02)

    NA, NB, D = N_LAYERS_A, N_LAYERS_B, D_MODEL
    start = jax.random.randint(next(ks), (BATCH, 1), 0, 1024, dtype=jnp.int32)
    positions = start + jnp.arange(SEQ, dtype=jnp.int32)[None, :]
    return {
        "x": nrm((BATCH, SEQ, D), 1.0),
        "c": nrm((BATCH, D), 1.0),
        "positions": positions,
        "ada_w": nrm((DEPTH, D, 6 * D), 0.5 * D ** -0.5),
        "ada_b": nrm((DEPTH, 6 * D), 0.02),
        "mlp_w1": nrm((DEPTH, D, D_FF), D ** -0.5),
        "mlp_w2": nrm((DEPTH, D_FF, D), D_FF ** -0.5),
        "final_g": gain((D,)),
        "a_w_in": nrm((NA, D, A_IN), D ** -0.5),
        "a_q_norm_g": gain((NA, A_Q_RANK)),
        "a_kv_norm_g": gain((NA, A_KV_RANK)),
        "a_w_uq": nrm((NA, A_Q_RANK, A_HEADS * (A_NOPE + A_ROPE)), A_Q_RANK ** -0.5),
        "a_w_qidx": nrm((NA, A_Q_RANK, IDX_HEADS * IDX_DIM), A_Q_RANK ** -0.5),
        "a_kidx_ln_g": gain((NA, IDX_DIM)),
        "a_kidx_ln_b": nrm((NA, IDX_DIM), 0.02),
        "a_w_uk": nrm((NA, A_KV_RANK, A_HEADS, A_NOPE), A_KV_RANK ** -0.5),
        "a_w_uv": nrm((NA, A_KV_RANK, A_HEADS, A_V), A_KV_RANK ** -0.5),
        "a_w_o": nrm((NA, A_HEADS * A_V, D), (A_HEADS * A_V) ** -0.5),
        "b_mu": jax.random.uniform(next(ks), (NB, 6, D), jnp.float32),
        "b_w_r": nrm((NB, D, D), D ** -0.5),
        "b_w_k": nrm((NB, D, D), D ** -0.5),
        "b_w_v": nrm((NB, D, D), D ** -0.5),
        "b_w_o": nrm((NB, D, D), D ** -0.5),
        "b_w0": nrm((NB, D), 1.0) - 0.5,
        "b_w_w1": nrm((NB, D, R_DECAY_LORA), D ** -0.5),
        "b_w_w2": nrm((NB, R_DECAY_LORA, D), 0.5 * R_DECAY_LORA ** -0.5),
        "b_a0": nrm((NB, D), 0.5),
        "b_w_a1": nrm((NB, D, R_AAA_LORA), D ** -0.5),
        "b_w_a2": nrm((NB, R_AAA_LORA, D), 0.5 * R_AAA_LORA ** -0.5),
        "b_w_g1": nrm((NB, D, R_GATE_LORA), D ** -0.5),
        "b_w_g2": nrm((NB, R_GATE_LORA, D), R_GATE_LORA ** -0.5),
        "b_k_k": 0.85 + nrm((NB, D), 0.05),
        "b_k_a": 1.0 + nrm((NB, D), 0.05),
        "b_r_k": nrm((NB, R_HEADS, R_HEAD), 0.1),
        "b_gn_g": gain((NB, D)),
        "b_gn_b": nrm((NB, D), 0.02),
    }


def reference(x, c, positions, ada_w, ada_b, mlp_w1, mlp_w2, final_g,
              a_w_in, a_q_norm_g, a_kv_norm_g, a_w_uq, a_w_qidx, a_kidx_ln_g, a_kidx_ln_b,
              a_w_uk, a_w_uv, a_w_o,
              b_mu, b_w_r, b_w_k, b_w_v, b_w_o, b_w0, b_w_w1, b_w_w2, b_a0, b_w_a1, b_w_a2,
              b_w_g1, b_w_g2, b_k_k, b_k_a, b_r_k, b_gn_g, b_gn_b):
    cos_a, sin_a = rope_angles(positions, A_ROPE)
    cos_i, sin_i = rope_angles(positions, IDX_ROPE)
    c_act = jax.nn.silu(c)
    for i in range(DEPTH):
        mod = c_act @ ada_w[i] + ada_b[i]
        sh1, sc1, g1, sh2, sc2, g2 = [m[:, None, :] for m in jnp.split(mod, 6, axis=-1)]
        hmix = rms_norm(x) * (1 + sc1) + sh1
        j = i // N_MIXERS
        if i % N_MIXERS == 0:
            y = dsa_mixer(hmix, cos_a, sin_a, cos_i, sin_i, a_w_in[j], a_q_norm_g[j], a_kv_norm_g[j],
                          a_w_uq[j], a_w_qidx[j], a_kidx_ln_g[j], a_kidx_ln_b[j],
                          a_w_uk[j], a_w_uv[j], a_w_o[j])
        else:
            y = rwkv7_mixer(hmix, b_mu[j], b_w_r[j], b_w_k[j], b_w_v[j], b_w_o[j], b_w0[j],
                            b_w_w1[j], b_w_w2[j], b_a0[j], b_w_a1[j], b_w_a2[j], b_w_g1[j], b_w_g2[j],
                            b_k_k[j], b_k_a[j], b_r_k[j], b_gn_g[j], b_gn_b[j])
        x = x + g1 * y
        hff = rms_norm(x) * (1 + sc2) + sh2
        x = x + g2 * (jnp.square(jax.nn.relu(hff @ mlp_w1[i])) @ mlp_w2[i])
    return rms_norm(x) * final_g
```

```python
import math
import numpy as np
from contextlib import ExitStack
import concourse.bass as bass
import concourse.mybir as mybir
from concourse.bass_utils import run_bass_kernel_spmd

F32 = mybir.dt.float32
BF16 = mybir.dt.bfloat16
I32 = mybir.dt.int32
ALU = mybir.AluOpType
AF = mybir.ActivationFunctionType
AX = mybir.AxisListType

D = 2048
S = 2048
DFF = 8192
NT = 4
TT = 512
EPS = 1e-6
A_SCALE = 192.0 ** -0.5
NIT = 24
TOPK = 256
NEG = -1.0e30


class Buf:
    def __init__(self, name, dram=False):
        self.name = name
        self.dram = dram
        self.lw = None
        self.lws = {}
        self.rd = {}
        self.dsem = None
        self.dq = None
        self.dcnt = 0


class _RecEng:
    def __getattr__(self, name):
        return lambda *a, **kw: (name, a, kw)


_REC = _RecEng()


def _call(c):
    return lambda eng: getattr(eng, c[0])(*c[1], **c[2])


class KB:
    def __init__(self, nc, es):
        self.nc = nc
        self.es0 = es
        self.engs = {'pe': nc.tensor, 'dve': nc.vector, 'act': nc.scalar, 'pool': nc.gpsimd, 'sp': nc.sync}
        self.sem = {}
        self.cnt = {}
        self.nsem = 0
        self.waited = {e: {} for e in self.engs}
        self.dbufs = []
        self.dfree = {}
        self.rec = None
        for e in self.engs:
            self._newsem(e)

    def newsem(self, name):
        self.nsem += 1
        return self.es0.enter_context(self.nc.semaphore(f"{name}_{self.nsem}"))

    def _newsem(self, e):
        self.sem[e] = self.newsem("s_" + e)
        self.cnt[e] = 0

    def _wait(self, e, tok):
        if tok is None:
            return
        sem, val = tok
        w = self.waited[e]
        if w.get(id(sem), 0) >= val:
            return
        self.engs[e].wait_ge(sem, val)
        w[id(sem)] = val

    def _deps(self, e, reads, writes, skip_dma_waw=False):
        own = self.sem[e]
        for b in reads:
            for t in b.lws.values():
                self._wait(e, t)
            if b.lw is not None:
                if e == 'pe' and b.lw[0] is own:
                    continue
                self._wait(e, b.lw)
        for b in writes:
            if b.lw is not None:
                if b.lw[0] is own and e == 'pe':
                    pass
                elif skip_dma_waw and (b.lw[0] is b.dsem or b.dram):
                    pass
                else:
                    self._wait(e, b.lw)
            for t in b.rd.values():
                if t[0] is own and e == 'pe':
                    continue
                self._wait(e, t)

    def _done(self, tok, reads, writes):
        for b in reads:
            b.rd[id(tok[0])] = tok
        for b in writes:
            b.lw = tok
            b.rd = {}

    def replay(self, chains):
        self.rec = None
        idx = [0] * len(chains)
        while True:
            prog = False
            for ci, ch in enumerate(chains):
                if idx[ci] < len(ch):
                    it = ch[idx[ci]]
                    idx[ci] += 1
                    prog = True
                    if it[0] == 'op':
                        self.op(it[1], it[2], it[3], it[4])
                    else:
                        self.dma(it[1], it[2], it[3], reads=it[4], writes=it[5], **it[6])
            if not prog:
                break

    def op(self, e, fn, reads=(), writes=()):
        if self.rec is not None:
            self.rec.append(('op', e, _call(fn(_REC)), list(reads), list(writes)))
            return None
        self._deps(e, reads, writes)
        ins = fn(self.engs[e])
        if self.cnt[e] >= 30000:
            self._newsem(e)
        self.cnt[e] += 1
        ins.then_inc(self.sem[e], 1)
        tok = (self.sem[e], self.cnt[e])
        self._done(tok, reads, writes)
        return tok

    def dma(self, q, out, in_, reads=(), writes=(), **kw):
        if self.rec is not None:
            self.rec.append(('dma', q, out, in_, list(reads), list(writes), kw))
            return None
        assert len(writes) == 1
        self._deps(q, reads, writes, skip_dma_waw=True)
        dst = writes[0]
        w = reads[0] if dst.dram else dst
        assert not w.dram
        if w.dsem is None:
            fl = self.dfree.setdefault(q, [])
            if fl:
                w.dsem, w.dcnt = fl.pop()
            else:
                w.dsem = self.newsem("d_" + w.name)
                w.dcnt = 0
            w.dq = q
            self.dbufs.append(w)
        assert w.dq == q, (w.name, w.dq, q)
        ins = self.engs[q].dma_start(out=out, in_=in_, **kw)
        w.dcnt += 16
        ins.then_inc(w.dsem, 16)
        tok = (w.dsem, w.dcnt)
        self._done(tok, reads, writes)
        if dst.dram:
            dst.lws[id(tok[0])] = tok
        return tok

    def barrier(self):
        for e in self.engs:
            for e2 in self.engs:
                if e2 != e and self.cnt[e2] > 0:
                    self._wait(e, (self.sem[e2], self.cnt[e2]))
            for b in self.dbufs:
                if b.dcnt > 0:
                    self._wait(e, (b.dsem, b.dcnt))
        for b in self.dbufs:
            self.dfree.setdefault(b.dq, []).append((b.dsem, b.dcnt))
            b.dsem = None
        self.dbufs = []


class Pool:
    def __init__(self, k, es, name, n, shape, dt, psum=False):
        self.items = []
        for i in range(n):
            if psum:
                t = es.enter_context(k.nc.psum_tensor(f"t{_uid()}_{name}{i}", list(shape), dt))
            else:
                t = es.enter_context(k.nc.sbuf_tensor(f"t{_uid()}_{name}{i}", list(shape), dt))
            self.items.append((t, Buf(f"{name}{i}")))
        self.i = 0

    def next(self):
        it = self.items[self.i % len(self.items)]
        self.i += 1
        return it


_UID = [0]


def _uid():
    _UID[0] += 1
    return _UID[0]


def sbt(k, es, name, shape, dt):
    return es.enter_context(k.nc.sbuf_tensor(f"t{_uid()}_" + name, list(shape), dt))


class Ctx:
    pass


def linear(k, wd, Bwd, G, KC, M, rhs_fn, rhs_bufs, N, wpool, pspool, epi, g0=0):
    for g in range(G):
        wt, Bw = wpool.next()
        k.dma('pool', wt[:, 0:KC * M], wd[g0 + g], reads=[Bwd], writes=[Bw])
        pt, Bp = pspool.next()
        for kc in range(KC):
            k.op('pe', lambda e: e.matmul(pt[0:M, 0:N], lhsT=wt[:, kc * M:(kc + 1) * M], rhs=rhs_fn(kc),
                                          start=(kc == 0), stop=(kc == KC - 1)),
                 reads=[Bw] + rhs_bufs, writes=[Bp])
        epi(g, pt, Bp)


def rms_stats(k, C, src_fn, src_bufs, nchunk, N, P, inv_n, ones):
    ps, Bps = C.pstat.next()
    for c in range(nchunk):
        sq, Bsq = C.sqpool.next()
        k.op('act', lambda e: e.activation(out=sq[0:P, 0:N], in_=src_fn(c), func=AF.Square), reads=src_bufs, writes=[Bsq])
        k.op('pe', lambda e: e.matmul(ps[0:P, 0:N], lhsT=ones[0:P, 0:P], rhs=sq[0:P, 0:N], start=(c == 0), stop=(c == nchunk - 1)),
             reads=[Bsq, C.Bconst], writes=[Bps])
    rstd, Brs = C.rspool.next()
    k.op('act', lambda e: e.activation(out=rstd[0:P, 0:N], in_=ps[0:P, 0:N], func=AF.Sqrt, scale=inv_n, bias=C.eps[0:P, :]),
         reads=[Bps, C.Bconst], writes=[Brs])
    k.op('dve', lambda e: e.reciprocal(out=rstd[0:P, 0:N], in_=rstd[0:P, 0:N]), reads=[Brs], writes=[Brs])
    return rstd, Brs


def modnorm(k, C, xt, Bxt, N, out, Bout, mod, scol, hcol):
    rstd, Brs = rms_stats(k, C, lambda c: xt[:, c, 0:N], Bxt, 16, N, 128, 1.0 / D, C.ones_bf)
    for dc in range(16):
        tmp, Btmp = C.tmppool.next()
        k.op('dve', lambda e: e.tensor_tensor(out=tmp[:, 0:N], in0=xt[:, dc, 0:N], in1=rstd[:, 0:N], op=ALU.mult),
             reads=[Bxt[dc], Brs], writes=[Btmp])
        k.op('act', lambda e: e.activation(out=out[:, dc, 0:N], in_=tmp[:, 0:N], func=AF.Identity,
                                           scale=mod[:, scol + dc:scol + dc + 1], bias=mod[:, hcol + dc:hcol + dc + 1]),
             reads=[Btmp, C.Bmod], writes=[Bout])


def mlp_phase(k, C, xin, Bxin, xout, Bxout, w1d, w2d, mod, final=None):
    TL = 2 * TT
    with ExitStack() as es:
        xt = sbt(k, es, "m_xt", [128, 16, TT], F32)
        Bxt = [Buf(f"m_xt{i}") for i in range(16)]
        hT = sbt(k, es, "m_hT", [128, 16, TL], BF16)
        BhT = Buf("m_hT")
        uT = sbt(k, es, "m_uT", [128, 32, TL], BF16)
        BuT = [Buf(f"m_uT{i}") for i in range(32)]
        w1pool = Pool(k, es, "m_w1", 3, [128, 16 * 128], BF16)
        w2pool = Pool(k, es, "m_w2", 2, [128, 32 * 128], BF16)
        rpool = Pool(k, es, "m_r", 3, [128, TT], F32)
        xcp = Pool(k, es, "m_xc", 4, [128, TT], F32)
        C.sqpool = Pool(k, es, "m_sq", 3, [128, TT], BF16)
        C.rspool = Pool(k, es, "m_rs", 2, [128, TT], F32)
        C.tmppool = Pool(k, es, "m_tmp", 3, [128, TT], F32)
        C.pstat = Pool(k, es, "m_pst", 1, [128, TT], F32, psum=True)
        pspool = Pool(k, es, "m_ps", 6, [128, TT], F32, psum=True)
        xin_v = xin.rearrange("(c p) t -> p c t", p=128)
        xout_v = xout.rearrange("(c p) t -> p c t", p=128)
        st = {}

        def prep_stats(tt, sub):
            ts = slice(tt * TL + sub * TT, tt * TL + (sub + 1) * TT)
            for dc in range(16):
                k.dma('sp', xt[:, dc, :], xin_v[:, dc, ts], reads=[Bxin], writes=[Bxt[dc]])
            st[sub] = rms_stats(k, C, lambda c: xt[:, c, :], Bxt, 16, TT, 128, 1.0 / D, C.ones_bf)

        def prep_norm(tt, sub):
            rstd, Brs = st[sub]
            for dc in range(16):
                tmp, Btmp = C.tmppool.next()
                k.op('dve', lambda e: e.tensor_tensor(out=tmp[:], in0=xt[:, dc, :], in1=rstd[:], op=ALU.mult), reads=[Bxt[dc], Brs], writes=[Btmp])
                k.op('act', lambda e: e.activation(out=hT[:, dc, sub * TT:(sub + 1) * TT], in_=tmp[:], func=AF.Identity,
                                                   scale=mod[:, 64 + dc:65 + dc], bias=mod[:, 48 + dc:49 + dc]), reads=[Btmp, C.Bmod], writes=[BhT])
        NTL = S // TL
        for sub in range(2):
            prep_stats(0, sub)
            prep_norm(0, sub)
        for tt in range(NTL):
            for half in range(2):
                for g in range(32):
                    wt, Bw = w1pool.next()
                    k.dma('pool', wt[:], w1d[half * 32 + g], reads=[C.Bw], writes=[Bw])
                    for sub in range(2):
                        pt, Bp = pspool.next()
                        for kc in range(16):
                            k.op('pe', lambda e: e.matmul(pt[:], lhsT=wt[:, kc * 128:(kc + 1) * 128], rhs=hT[:, kc, sub * TT:(sub + 1) * TT],
                                                          start=(kc == 0), stop=(kc == 15)), reads=[Bw, BhT], writes=[Bp])
                        r, Br = rpool.next()
                        k.op('act', lambda e: e.activation(out=r[:], in_=pt[:], func=AF.Relu), reads=[Bp], writes=[Br])
                        k.op('dve', lambda e: e.tensor_tensor(out=uT[:, g, sub * TT:(sub + 1) * TT], in0=r[:], in1=r[:], op=ALU.mult),
                             reads=[Br], writes=[BuT[g]])
                for dc in range(16):
                    if half == 1 and tt + 1 < NTL:
                        if dc == 1:
                            prep_stats(tt + 1, 0)
                        elif dc == 4:
                            prep_norm(tt + 1, 0)
                        elif dc == 8:
                            prep_stats(tt + 1, 1)
                        elif dc == 11:
                            prep_norm(tt + 1, 1)
                    wt, Bw = w2pool.next()
                    k.dma('pool', wt[:], w2d[dc][:, half * 4096:(half + 1) * 4096], reads=[C.Bw], writes=[Bw])
                    for sub in range(2):
                        ts = slice(tt * TL + sub * TT, tt * TL + (sub + 1) * TT)
                        pt, Bp = pspool.next()
                        for kc in range(32):
                            k.op('pe', lambda e: e.matmul(pt[:], lhsT=wt[:, kc * 128:(kc + 1) * 128], rhs=uT[:, kc, sub * TT:(sub + 1) * TT],
                                                          start=(kc == 0), stop=(kc == 31)), reads=[Bw] + BuT, writes=[Bp])
                        xc, Bxc = xcp.next()
                        if half == 0:
                            k.dma('sp', xc[:], xin_v[:, dc, ts], reads=[Bxin], writes=[Bxc])
                        else:
                            k.dma('sp', xc[:], xout_v[:, dc, ts], reads=[Bxout], writes=[Bxc])
                        k.op('dve', lambda e: e.scalar_tensor_tensor(out=xc[:], in0=pt[:], scalar=mod[:, 80 + dc:81 + dc], in1=xc[:],
                                                                     op0=ALU.mult, op1=ALU.add), reads=[Bp, Bxc, C.Bmod], writes=[Bxc])
                        k.dma('sp', xout_v[:, dc, ts], xc[:], reads=[Bxc], writes=[Bxout])
        if final is not None:
            outT, Bout, fing = final
            out_v = outT.rearrange("(c p) t -> p c t", p=128)
            for tt in range(NT):
                ts = slice(tt * TT, (tt + 1) * TT)
                for dc in range(16):
                    k.dma('sp', xt[:, dc, :], xout_v[:, dc, ts], reads=[Bxout], writes=[Bxt[dc]])
                rstd, Brs = rms_stats(k, C, lambda c: xt[:, c, :], Bxt, 16, TT, 128, 1.0 / D, C.ones_bf)
                for dc in range(16):
                    k.op('dve', lambda e: e.scalar_tensor_tensor(out=xt[:, dc, :], in0=xt[:, dc, :], scalar=fing[:, dc:dc + 1], in1=rstd[:],
                                                                 op0=ALU.mult, op1=ALU.mult), reads=[Bxt[dc], Brs, C.Bconst], writes=[Bxt[dc]])
                    k.dma('sp', out_v[:, dc, ts], xt[:, dc, :], reads=[Bxt[dc]], writes=[Bout])
    k.barrier()


def rope_tables(k, C, es, posd, tabs_d, Btabs):
    posi = sbt(k, es, "r_posi", [64, 1, S], I32)
    posf = sbt(k, es, "r_posf", [64, S], F32)
    ang = sbt(k, es, "r_ang", [64, S], F32)
    t1 = sbt(k, es, "r_t1", [64, S], F32)
    ti = sbt(k, es, "r_ti", [64, S], I32)
    Bp, Ba, Bt = Buf("r_pos"), Buf("r_ang"), Buf("r_t")
    k.dma('sp', posi[:], posd.partition_broadcast(64), reads=[C.Bin], writes=[Bp])
    k.op('dve', lambda e: e.tensor_copy(out=posf[:], in_=posi[:, 0, :]), reads=[Bp], writes=[Bp])
    TWO_PI = 2.0 * math.pi
    for ti_, (icol, off, scol) in enumerate([(0, 0.5 * math.pi, None), (0, 0.0, 1), (2, 0.5 * math.pi, None), (2, 0.0, 3)]):
        k.op('dve', lambda e: e.tensor_scalar(out=ang[:], in0=posf[:], scalar1=C.cst[0:64, icol:icol + 1], scalar2=off, op0=ALU.mult, op1=ALU.add),
             reads=[Bp, C.Bconst], writes=[Ba])
        k.op('dve', lambda e: e.tensor_scalar(out=t1[:], in0=ang[:], scalar1=1.0 / TWO_PI, scalar2=None, op0=ALU.mult), reads=[Ba], writes=[Bt])
        k.op('dve', lambda e: e.tensor_copy(out=ti[:], in_=t1[:]), reads=[Bt], writes=[Bt])
        k.op('dve', lambda e: e.tensor_copy(out=t1[:], in_=ti[:]), reads=[Bt], writes=[Bt])
        k.op('dve', lambda e: e.scalar_tensor_tensor(out=ang[:], in0=t1[:], scalar=-TWO_PI, in1=ang[:], op0=ALU.mult, op1=ALU.add),
             reads=[Bt, Ba], writes=[Ba])
        k.op('dve', lambda e: e.tensor_scalar(out=t1[:], in0=ang[:], scalar1=math.pi, scalar2=-TWO_PI, op0=ALU.is_gt, op1=ALU.mult), reads=[Ba], writes=[Bt])
        k.op('dve', lambda e: e.tensor_tensor(out=ang[:], in0=ang[:], in1=t1[:], op=ALU.add), reads=[Ba, Bt], writes=[Ba])
        k.op('dve', lambda e: e.tensor_scalar(out=t1[:], in0=ang[:], scalar1=-math.pi, scalar2=-TWO_PI, op0=ALU.is_lt, op1=ALU.mult), reads=[Ba], writes=[Bt])
        k.op('dve', lambda e: e.tensor_tensor(out=ang[:], in0=ang[:], in1=t1[:], op=ALU.subtract), reads=[Ba, Bt], writes=[Ba])
        k.op('act', lambda e: e.activation(out=t1[:], in_=ang[:], func=AF.Sin), reads=[Ba], writes=[Bt])
        if scol is not None:
            k.op('dve', lambda e: e.tensor_scalar(out=t1[:], in0=t1[:], scalar1=C.cst[0:64, scol:scol + 1], scalar2=None, op0=ALU.mult),
                 reads=[Bt, C.Bconst], writes=[Bt])
        k.dma('sp', tabs_d[ti_], t1[:], reads=[Bt], writes=[Btabs])


def build(dbg=False, stop_after=None):
    nc = bass.Bass("TRN2", target_bir_lowering=False)

    def din(name, shape, dt=F32):
        return nc.dram_tensor(name, list(shape), dt, kind="ExternalInput").ap()

    def dscr(name, shape, dt=F32, out=False):
        return nc.dram_tensor(name, list(shape), dt, kind="ExternalOutput" if (out or dbg) else "Internal").ap()

    xT = din("xT", [D, S])
    cvec = din("cvec", [128, 16])
    posd = din("pos", [1, S], I32)
    cst_d = din("cst", [128, 48])
    ident_d = din("ident", [128, 128])
    ada_w = [din(f"ada_w{i}", [24, 128, 4 * 16 * 128]) for i in range(2)]
    ada_b = [din(f"ada_b{i}", [128, 96]) for i in range(2)]
    w1d = [din(f"w1_{i}", [64, 128, 16 * 128]) for i in range(2)]
    w2d = [din(f"w2_{i}", [16, 128, 64 * 128]) for i in range(2)]
    fing_d = din("fing", [128, 16])
    w_in_a = din("w_in_a", [6, 128, 16 * 128])
    w_in_b = din("w_in_b", [4, 128, 16 * 64])
    w_in_i = din("w_in_i", [128, 16 * 16])
    dsa_vec = din("dsa_vec", [128, 16])
    w_uq_n = din("w_uq_n", [16, 128, 4 * 128])
    w_uq_r = din("w_uq_r", [32, 128, 4 * 64])
    w_qi = din("w_qi", [32, 128, 4 * 64])
    w_ukT = din("w_ukT", [128, 16 * 256])
    w_uv = din("w_uv", [128, 2 * 2048])
    w_o = din("w_o", [16, 128, 16 * 128])

    Wd = {}
    for nm, shp in (("w_w1", [1, 128, 16 * 96]), ("w_a1", [1, 128, 16 * 96]), ("w_g1", [2, 128, 2048]), ("w_v", [16, 128, 2048]),
                    ("w_r", [16, 128, 2048]), ("w_k", [16, 128, 2048]), ("w_o", [16, 128, 2048]), ("w_w2", [96, 2048]), ("w_a2", [96, 2048]),
                    ("w_g2", [128, 4096]), ("rvec", [128, 13 * 16]), ("rconst", [128, 1024])):
        Wd[nm] = din("b_" + nm, shp)
    rk_d = dscr("rk_d", [7, D, S], BF16)
    x1_d = dscr("x1_d", [D, S])
    x2_d = dscr("x2_d", [D, S])
    x3_d = dscr("x3_d", [D, S])
    x4_d = dscr("x4_d", [D, S])
    outT = dscr("outT", [D, S], out=True)
    tabs_d = dscr("tabs_d", [4, 64, S])
    qlat_d = dscr("qlat_d", [16, 128, 2 * 16 * 128], BF16)
    qr_d = dscr("qr_d", [16, 64, 16 * 128], BF16)
    qi_d = dscr("qi_d", [16, 64, 16 * 128], BF16)

    es0 = ExitStack()
    with es0:
        k = KB(nc, es0)
        C = Ctx()
        C.Bin = Buf("inputs", dram=True)
        C.Bw = Buf("weights", dram=True)
        C.Bconst = Buf("const")
        C.Bmod = Buf("mod")
        C.cst = sbt(k, es0, "cst", [128, 48], F32)
        C.eps = sbt(k, es0, "eps", [128, 1], F32)
        C.ident_f = sbt(k, es0, "ident_f", [128, 128], F32)
        C.ident_bf = sbt(k, es0, "ident_bf", [128, 128], BF16)
        C.ones_bf = sbt(k, es0, "ones_bf", [128, 128], BF16)
        C.ones_f = sbt(k, es0, "ones_f", [128, 128], F32)
        C.fing = sbt(k, es0, "fing", [128, 16], F32)
        C.dsav = sbt(k, es0, "dsav", [128, 16], F32)
        mod = [sbt(k, es0, f"mod{i}", [128, 96], F32) for i in range(2)]
        k.dma('sp', C.cst[:], cst_d, reads=[C.Bin], writes=[C.Bconst])
        k.dma('sp', C.ident_f[:], ident_d, reads=[C.Bin], writes=[C.Bconst])
        k.dma('sp', C.fing[:], fing_d, reads=[C.Bin], writes=[C.Bconst])
        k.dma('sp', C.dsav[:], dsa_vec, reads=[C.Bin], writes=[C.Bconst])
        k.op('dve', lambda e: e.memset(C.eps[:], EPS), writes=[C.Bconst])
        k.op('dve', lambda e: e.memset(C.ones_bf[:], 1.0), writes=[C.Bconst])
        k.op('dve', lambda e: e.memset(C.ones_f[:], 1.0), writes=[C.Bconst])
        k.op('dve', lambda e: e.tensor_copy(out=C.ident_bf[:], in_=C.ident_f[:]), reads=[C.Bconst], writes=[C.Bconst])

        cact = sbt(k, es0, "p0_cact", [128, 16], BF16)
        Btabs = Buf("tabs_d", dram=True)
        with ExitStack() as es:
            cv = sbt(k, es, "p0_cv", [128, 16], F32)
            adab = sbt(k, es, "p0_adab", [128, 96], F32)
            Bcv, Bab = Buf("p0_cv"), Buf("p0_adab")
            k.dma('sp', cv[:], cvec, reads=[C.Bin], writes=[Bcv])
            k.op('act', lambda e: e.activation(out=cact[:], in_=cv[:], func=AF.Silu), reads=[Bcv], writes=[Bcv])
            wpool = Pool(k, es, "p0_w", 2, [128, 4 * 16 * 128], BF16)
            psm = es.enter_context(nc.psum_tensor("t_p0_ps", [128, 96], F32))
            Bpsm = Buf("p0_ps")
            for i in range(1):
                k.dma('sp', adab[:], ada_b[i], reads=[C.Bin], writes=[Bab])
                for G in range(24):
                    wt, Bw = wpool.next()
                    k.dma('pool', wt[:], ada_w[i][G], reads=[C.Bw], writes=[Bw])
                    for j in range(4):
                        oc = G * 4 + j
                        for kc in range(16):
                            k.op('pe', lambda e: e.matmul(psm[:, oc:oc + 1], lhsT=wt[:, (j * 16 + kc) * 128:(j * 16 + kc + 1) * 128],
                                                          rhs=cact[:, kc:kc + 1], start=(kc == 0), stop=(kc == 15)),
                                 reads=[Bw, Bcv], writes=[Bpsm])
                k.op('dve', lambda e: e.tensor_tensor(out=mod[i][:], in0=psm[:], in1=adab[:], op=ALU.add), reads=[Bpsm, Bab], writes=[C.Bmod])
                for c0 in (16, 64):
                    k.op('dve', lambda e: e.tensor_scalar(out=mod[i][:, c0:c0 + 16], in0=mod[i][:, c0:c0 + 16], scalar1=1.0, scalar2=None, op0=ALU.add),
                         reads=[C.Bmod], writes=[C.Bmod])
            rope_tables(k, C, es, posd, tabs_d, Btabs)
        k.barrier()

        Bx = [Buf("xT", dram=True), Buf("x1_d", dram=True), Buf("x2_d", dram=True), Buf("x3_d", dram=True), Buf("outT", dram=True), Buf("x4_d", dram=True)]
        C.ada1 = (ada_w[1], ada_b[1], cact, Bcv, mod[1])
        dsa_phase(k, C, nc, xT, Bx[0], x1_d, Bx[1], mod[0], tabs_d, Btabs, w_in_a, w_in_b, w_in_i, w_uq_n, w_uq_r, w_qi, w_ukT, w_uv, w_o,
                  qlat_d, qr_d, qi_d)
        if stop_after == 'dsa':
            k.barrier()
            return nc
        mlp_phase(k, C, x1_d, Bx[1], x2_d, Bx[2], w1d[0], w2d[0], mod[0])
        if stop_after == 'mlp0':
            return nc
        rwkv_phase(k, C, nc, x2_d, Bx[2], x3_d, Bx[3], mod[1], Wd, rk_d)
        if stop_after == 'rwkv':
            return nc
        mlp_phase(k, C, x3_d, Bx[3], x4_d, Bx[5], w1d[1], w2d[1], mod[1], final=(outT, Bx[4], C.fing))
    return nc


def dsa_phase(k, C, nc, xT, Bxin, x1_d, Bxout, mod, tabs_d, Btabs, w_in_a, w_in_b, w_in_i, w_uq_n, w_uq_r, w_qi, w_ukT, w_uv, w_o,
              qlat_d, qr_d, qi_d):
    Bql, Bqr, Bqi = Buf("qlat_d", dram=True), Buf("qr_d", dram=True), Buf("qi_d", dram=True)
    xin_v = xT.rearrange("(c p) t -> p c t", p=128)
    with ExitStack() as esA:
        ckvT = sbt(k, esA, "a_ckvT", [128, 2, S], BF16)
        krT = sbt(k, esA, "a_krT", [64, S], BF16)
        kiT = sbt(k, esA, "a_kiT", [64, S], BF16)
        ckva = sbt(k, esA, "a_ckva", [128, 16, 257], BF16)
        wabs = sbt(k, esA, "a_wabs", [128, 16, 16], F32)
        wsgn = sbt(k, esA, "a_wsgn", [128, 16, 16], F32)
        Bkeys = Buf("a_keys")
        k.op('dve', lambda e: e.memset(ckva[:, :, 256:257], 1.0), writes=[Bkeys])
        with ExitStack() as es:
            xt = sbt(k, es, "a_xt", [128, 16, TT], F32)
            Bxt = [Buf(f"a_xt{i}") for i in range(16)]
            hT = sbt(k, es, "a_hT", [128, 16, TT], BF16)
            BhT = Buf("a_hT")
            tab = sbt(k, es, "a_tab", [64, 4, TT], F32)
            Btab = Buf("a_tab")
            cqraw = sbt(k, es, "a_cqraw", [128, 6, TT], F32)
            Braw = [Buf(f"a_raw{i}") for i in range(6)]
            cqT = sbt(k, es, "a_cqT", [128, 4, TT], BF16)
            BcqT = Buf("a_cqT")
            rawb = sbt(k, es, "a_rawb", [64, 4, TT], F32)
            Brawb = [Buf(f"a_rawb{i}") for i in range(4)]
            xc = sbt(k, es, "a_xc", [64, 2, TT], F32)
            Bxc = Buf("a_xc")
            wini = sbt(k, es, "a_wini", [128, 16, 16], BF16)
            wukT = sbt(k, es, "a_wukT", [128, 16, 256], BF16)
            Bwr = Buf("a_wres")
            k.dma('pool', wini[:], w_in_i.rearrange("p (a b) -> p a b", a=16), reads=[C.Bw], writes=[Bwr])
            k.dma('pool', wukT[:], w_ukT.rearrange("p (a b) -> p a b", a=16), reads=[C.Bw], writes=[Bwr])
            wpA = Pool(k, es, "a_wA", 3, [128, 16 * 128], BF16)
            wpB = Pool(k, es, "a_wB", 2, [128, 16 * 64], BF16)
            wpQ = Pool(k, es, "a_wQ", 4, [128, 4 * 128], BF16)
            qnp = Pool(k, es, "a_qn", 2, [128, TT], BF16)
            rawq = Pool(k, es, "a_rawq", 4, [64, TT], F32)
            stg = Pool(k, es, "a_stg", 4, [128, TT], BF16)
            t64 = Pool(k, es, "a_t64", 4, [64, TT], F32)
            C.sqpool = Pool(k, es, "a_sq", 3, [128, TT], BF16)
            C.rspool = Pool(k, es, "a_rs", 2, [128, TT], F32)
            C.tmppool = Pool(k, es, "a_tmp", 2, [128, TT], F32)
            C.pstat = Pool(k, es, "a_pst", 1, [128, TT], F32, psum=True)
            pspool = Pool(k, es, "a_ps", 5, [128, TT], F32, psum=True)
            pTb = Pool(k, es, "a_pT", 1, [128, 1024], BF16, psum=True)

            def subpool(items):
                p_ = Pool.__new__(Pool)
                p_.items = list(items)
                p_.i = 0
                return p_
            HR = [(subpool(qnp.items[u:u + 1]), subpool(wpQ.items[2 * u:2 * u + 2]), subpool(stg.items[2 * u:2 * u + 2]),
                   subpool(rawq.items[2 * u:2 * u + 2]), subpool(t64.items[2 * u:2 * u + 2]),
                   subpool(pspool.items[0:3] if u == 0 else pspool.items[3:5] + C.pstat.items[0:1])) for u in range(2)]
            adaw1, adab1_d, cact, Bcv, mod1 = C.ada1
            awp = Pool(k, es, "a_adaw", 2, [128, 4 * 16 * 128], BF16)
            adab1 = sbt(k, es, "a_adab1", [128, 96], F32)
            Bab1 = Buf("a_adab1")
            psm1 = es.enter_context(nc.psum_tensor(f"t{_uid()}_a_psm1", [128, 96], F32))
            Bpsm1 = Buf("a_psm1")
            k.dma('sp', adab1[:], adab1_d, reads=[C.Bin], writes=[Bab1])
            ada_ld = {}

            def ada_load(G):
                if G < 24:
                    wt, Bw = awp.next()
                    k.dma('pool', wt[:], adaw1[G], reads=[C.Bw], writes=[Bw])
                    ada_ld[G] = (wt, Bw)

            def ada_step(G):
                if G >= 24:
                    return
                ada_load(G + 1)
                wt, Bw = ada_ld.pop(G)
                for j in range(4):
                    oc = G * 4 + j
                    for kc in range(16):
                        k.op('pe', lambda e: e.matmul(psm1[:, oc:oc + 1], lhsT=wt[:, (j * 16 + kc) * 128:(j * 16 + kc + 1) * 128],
                                                      rhs=cact[:, kc:kc + 1], start=(kc == 0), stop=(kc == 15)),
                             reads=[Bw, Bcv], writes=[Bpsm1])
                if G == 23:
                    k.op('dve', lambda e: e.tensor_tensor(out=mod1[:], in0=psm1[:], in1=adab1[:], op=ALU.add), reads=[Bpsm1, Bab1], writes=[C.Bmod])
                    for c0 in (16, 64):
                        k.op('dve', lambda e: e.tensor_scalar(out=mod1[:, c0:c0 + 16], in0=mod1[:, c0:c0 + 16], scalar1=1.0, scalar2=None, op0=ALU.add),
                             reads=[C.Bmod], writes=[C.Bmod])
            ada_load(0)
            ada_n = [0]

            def rope_combine(a_ap, b_ap, Bab, ct, st, out_ap, Bo_, tp=None):
                tp = tp or t64
                ta, Bta = tp.next()
                tb, Btb = tp.next()
                k.op('dve', lambda e: e.tensor_tensor(out=ta[:], in0=a_ap, in1=tab[:, ct, :], op=ALU.mult), reads=Bab + [Btab], writes=[Bta])
                k.op('dve', lambda e: e.tensor_tensor(out=tb[:], in0=b_ap, in1=tab[:, st, :], op=ALU.mult), reads=Bab + [Btab], writes=[Btb])
                k.op('dve', lambda e: e.tensor_tensor(out=out_ap, in0=ta[:], in1=tb[:], op=ALU.add), reads=[Bta, Btb], writes=[Bo_])

            for tt in range(NT):
                ts = slice(tt * TT, (tt + 1) * TT)
                for dc in range(16):
                    k.dma('sp', xt[:, dc, :], xin_v[:, dc, ts], reads=[Bxin], writes=[Bxt[dc]])
                for j in range(4):
                    k.dma('sp', tab[:, j, :], tabs_d[j][:, ts], reads=[Btabs], writes=[Btab])
                modnorm(k, C, xt, Bxt, TT, hT, BhT, mod, 16, 0)

                def epiA(g, pt, Bp):
                    k.op('act', lambda e: e.copy(out=cqraw[:, g, :], in_=pt[:]), reads=[Bp], writes=[Braw[g]])
                linear(k, w_in_a, C.Bw, 6, 16, 128, lambda kc: hT[:, kc, :], [BhT], TT, wpA, pspool, epiA)
                rq, Brq = rms_stats(k, C, lambda c: cqraw[:, c, :], Braw[0:4], 4, TT, 128, 1.0 / 512, C.ones_bf)
                for g in range(4):
                    k.op('dve', lambda e: e.scalar_tensor_tensor(out=cqT[:, g, :], in0=cqraw[:, g, :], scalar=C.dsav[:, g:g + 1], in1=rq[:],
                                                                 op0=ALU.mult, op1=ALU.mult), reads=[Braw[g], Brq, C.Bconst], writes=[BcqT])
                rkv, Brkv = rms_stats(k, C, lambda c: cqraw[:, 4 + c, :], Braw[4:6], 2, TT, 128, 1.0 / 256, C.ones_bf)
                for g in range(2):
                    k.op('dve', lambda e: e.scalar_tensor_tensor(out=ckvT[:, g, ts], in0=cqraw[:, 4 + g, :], scalar=C.dsav[:, 4 + g:5 + g], in1=rkv[:],
                                                                 op0=ALU.mult, op1=ALU.mult), reads=[Braw[4 + g], Brkv, C.Bconst], writes=[Bkeys])
                pT, BpT = pTb.next()
                for kb in range(4):
                    for rc in range(2):
                        j = kb * 2 + rc
                        k.op('pe', lambda e: e.transpose(out=pT[:, j * 128:(j + 1) * 128], in_=ckvT[:, rc, tt * TT + kb * 128: tt * TT + (kb + 1) * 128],
                                                         identity=C.ident_bf[:]), reads=[Bkeys, C.Bconst], writes=[BpT])
                k.op('act', lambda e: e.copy(out=ckva[:, tt * 4:(tt + 1) * 4, 0:256], in_=pT[:].rearrange("p (a b) -> p a b", a=4)),
                     reads=[BpT], writes=[Bkeys])

                def epiB(g, pt, Bp):
                    k.op('act', lambda e: e.copy(out=rawb[:, g, :], in_=pt[0:64, :]), reads=[Bp], writes=[Brawb[g]])
                linear(k, w_in_b, C.Bw, 4, 16, 64, lambda kc: hT[:, kc, :], [BhT], TT, wpB, pspool, epiB)
                rope_combine(rawb[:, 0, :], rawb[:, 1, :], [Brawb[0], Brawb[1]], 0, 1, krT[:, ts], Bkeys)
                pm, Bpm = pspool.next()
                k.op('pe', lambda e: e.matmul(pm[0:64, :], lhsT=C.ones_f[0:64, 0:64], rhs=rawb[:, 2, :], start=True, stop=True),
                     reads=[Brawb[2], C.Bconst], writes=[Bpm])
                for j in range(2):
                    k.op('dve', lambda e: e.scalar_tensor_tensor(out=xc[:, j, :], in0=pm[0:64, :], scalar=-1.0 / 64, in1=rawb[:, 2 + j, :],
                                                                 op0=ALU.mult, op1=ALU.add), reads=[Bpm, Brawb[2 + j]], writes=[Bxc])
                rl, Brl = rms_stats(k, C, lambda c: xc[:, 0, :], [Bxc], 1, TT, 64, 1.0 / 64, C.ones_bf)
                lns = []
                for j in range(2):
                    tq, Btq = t64.next()
                    k.op('dve', lambda e: e.tensor_tensor(out=tq[:], in0=xc[:, j, :], in1=rl[0:64, :], op=ALU.mult), reads=[Bxc, Brl], writes=[Btq])
                    k.op('act', lambda e: e.activation(out=tq[:], in_=tq[:], func=AF.Identity, scale=C.dsav[0:64, 6 + 2 * j:7 + 2 * j],
                                                       bias=C.dsav[0:64, 7 + 2 * j:8 + 2 * j]), reads=[Btq, C.Bconst], writes=[Btq])
                    lns.append((tq, Btq))
                rope_combine(lns[0][0][:], lns[1][0][:], [lns[0][1], lns[1][1]], 2, 3, kiT[:, ts], Bkeys)

                for tb in range(4):
                    pw, Bpw = pspool.next()
                    for kc in range(16):
                        k.op('pe', lambda e: e.matmul(pw[:, 0:16], lhsT=hT[:, kc, tb * 128:(tb + 1) * 128], rhs=wini[:, kc, :],
                                                      start=(kc == 0), stop=(kc == 15)), reads=[BhT, Bwr], writes=[Bpw])
                    k.op('act', lambda e: e.activation(out=wabs[:, tt * 4 + tb, :], in_=pw[:, 0:16], func=AF.Abs, scale=1.0 / 32),
                         reads=[Bpw], writes=[Bkeys])
                    k.op('act', lambda e: e.activation(out=wsgn[:, tt * 4 + tb, :], in_=pw[:, 0:16], func=AF.Sign), reads=[Bpw], writes=[Bkeys])

                def head(h, R):
                    qnp_, wpQ_, stg_, rawq_, t64_, psp_ = R
                    qn, Bqn = qnp_.next()

                    def epiQ(g, pt, Bp):
                        k.op('act', lambda e: e.copy(out=qn[:], in_=pt[:]), reads=[Bp], writes=[Bqn])
                    linear(k, w_uq_n, C.Bw, 1, 4, 128, lambda kc: cqT[:, kc, :], [BcqT], TT, wpQ_, psp_, epiQ, g0=h)
                    for rc in range(2):
                        pl, Bpl = psp_.next()
                        k.op('pe', lambda e: e.matmul(pl[:], lhsT=wukT[:, h, rc * 128:(rc + 1) * 128], rhs=qn[:], start=True, stop=True),
                             reads=[Bwr, Bqn], writes=[Bpl])
                        sg, Bsg = stg_.next()
                        k.op('dve', lambda e: e.tensor_copy(out=sg[:], in_=pl[:]), reads=[Bpl], writes=[Bsg])
                        c0 = (rc * 16 + h) * 128
                        k.dma('sp', qlat_d[tt * 4:(tt + 1) * 4, :, c0:c0 + 128].rearrange("b r q -> r b q"),
                              sg[:].rearrange("p (b q) -> p b q", b=4), reads=[Bsg], writes=[Bql])
                    for (wd_, tc_, td_, dst, Bdst) in ((w_uq_r, 0, 1, qr_d, Bqr), (w_qi, 2, 3, qi_d, Bqi)):
                        rr = []

                        def epiR(g, pt, Bp):
                            r_, Br_ = rawq_.next()
                            k.op('act', lambda e: e.copy(out=r_[:], in_=pt[0:64, :]), reads=[Bp], writes=[Br_])
                            rr.append((r_, Br_))
                        linear(k, wd_, C.Bw, 2, 4, 64, lambda kc: cqT[:, kc, :], [BcqT], TT, wpQ_, psp_, epiR, g0=2 * h)
                        sg, Bsg = stg_.next()
                        rope_combine(rr[0][0][:], rr[1][0][:], [rr[0][1], rr[1][1]], tc_, td_, sg[0:64, :], Bsg, t64_)
                        k.dma('sp', dst[tt * 4:(tt + 1) * 4, :, h * 128:(h + 1) * 128].rearrange("b r q -> r b q"),
                              sg[0:64, :].rearrange("p (b q) -> p b q", b=4), reads=[Bsg], writes=[Bdst])

                for h0 in range(0, 16, 2):
                    chains = []
                    for u in range(2):
                        ada_step(ada_n[0])
                        ada_n[0] += 1
                        k.rec = []
                        head(h0 + u, HR[u])
                        chains.append(k.rec)
                        k.rec = None
                    k.replay(chains)
        k.barrier()
        with ExitStack() as es:
            wuv = sbt(k, es, "b_wuv", [128, 2, 2048], BF16)
            Bwuv = Buf("b_wuv")
            k.dma('pool', wuv[:], w_uv.rearrange("p (a b) -> p a b", a=2), reads=[C.Bw], writes=[Bwuv])
            qlp = Pool(k, es, "b_ql", 2, [128, 2, 2048], BF16)
            qrp = Pool(k, es, "b_qr", 2, [64, 2048], BF16)
            qip = Pool(k, es, "b_qi", 2, [64, 16, 128], BF16)
            acc = sbt(k, es, "b_acc", [128, S], F32)
            Bacc = [Buf(f"b_acc{i}") for i in range(4)]
            junk = sbt(k, es, "b_junk", [128, S], BF16)
            Bjunk = Buf("b_junk")
            mask = sbt(k, es, "b_mask", [128, S], BF16)
            Bmask = Buf("b_mask")
            maskT = sbt(k, es, "b_maskT", [128, 16, 128], BF16)
            BmaskT = Buf("b_maskT")
            relup = Pool(k, es, "b_relu", 3, [128, 512], F32)
            PTp = Pool(k, es, "b_PT", 3, [128, 4, 128], BF16)
            olatp = Pool(k, es, "b_olat", 2, [128, 4, 256], BF16)
            olatTp = Pool(k, es, "b_olatT", 2, [128, 8, 128], BF16)
            oT = sbt(k, es, "b_oT", [128, 16, TT], BF16)
            BoT = Buf("b_oT")
            sm = sbt(k, es, "b_sm", [128, 8], F32)
            Bsm = Buf("b_sm")
            Wt = sbt(k, es, "b_W", [128, NIT], F32)
            mid = sbt(k, es, "b_mid", [128, NIT], F32)
            cnt = sbt(k, es, "b_cnt", [128, NIT], F32)
            gg = sbt(k, es, "b_g", [128, NIT], F32)
            BW, Bmid, Bcnt, Bg = Buf("b_W"), Buf("b_mid"), Buf("b_cnt"), Buf("b_g")
            rsp = Pool(k, es, "b_rs", 2, [128, 4], F32)
            wop = Pool(k, es, "b_wo", 3, [128, 16 * 128], BF16)
            xcp = Pool(k, es, "b_xc", 3, [128, TT], F32)
            ygp = Pool(k, es, "b_yg", 2, [128, TT], F32)
            pA = Pool(k, es, "b_pA", 2, [128, 512], F32, psum=True)
            pPV = Pool(k, es, "b_pPV", 4, [128, 512], F32, psum=True)
            pTb = Pool(k, es, "b_pT", 1, [128, 1024], BF16, psum=True)
            pO = Pool(k, es, "b_pO", 1, [128, 512], F32, psum=True)
            Q = {}

            def idx_stage(qt):
                nk = 128 * (qt + 1)
                nblk = (nk + 511) // 512
                qlatT, Bq1 = qlp.next()
                qrT, Bq2 = qrp.next()
                qiT, Bq3 = qip.next()
                Q[qt] = (qlatT, Bq1, qrT, Bq2)
                k.dma('sp', qiT[:], qi_d[qt].rearrange("p (a b) -> p a b", a=16), reads=[Bqi], writes=[Bq3])
                k.dma('sp', qlatT[:], qlat_d[qt].rearrange("p (a b) -> p a b", a=2), reads=[Bql], writes=[Bq1])
                k.dma('sp', qrT[:], qr_d[qt], reads=[Bqr], writes=[Bq2])
                for h in range(16):
                    for kb in range(nblk):
                        w = min(512, nk - kb * 512)
                        cs = slice(kb * 512, kb * 512 + w)
                        pt, Bp = pA.next()
                        k.op('pe', lambda e: e.matmul(pt[:, 0:w], lhsT=qiT[:, h, :], rhs=kiT[:, cs], start=True, stop=True),
                             reads=[Bq3, Bkeys], writes=[Bp])
                        r, Br = relup.next()
                        k.op('act', lambda e: e.activation(out=r[:, 0:w], in_=pt[:, 0:w], func=AF.Relu, scale=wabs[:, qt, h:h + 1]),
                             reads=[Bp, Bkeys], writes=[Br])
                        if h == 0:
                            k.op('dve', lambda e: e.tensor_scalar(out=acc[:, cs], in0=r[:, 0:w], scalar1=wsgn[:, qt, 0:1], scalar2=None, op0=ALU.mult),
                                 reads=[Br, Bkeys], writes=[Bacc[kb]])
                        else:
                            k.op('dve', lambda e: e.scalar_tensor_tensor(out=acc[:, cs], in0=r[:, 0:w], scalar=wsgn[:, qt, h:h + 1], in1=acc[:, cs],
                                                                         op0=ALU.mult, op1=ALU.add), reads=[Br, Bkeys, Bacc[kb]], writes=[Bacc[kb]])

            def bisect_stage(qt):
                nk = 128 * (qt + 1)
                nblk = (nk + 511) // 512
                Ba = Bacc[0:nblk]
                if qt >= 2:
                    k.op('dve', lambda e: e.tensor_reduce(out=sm[:, 0:1], in_=acc[:, 0:nk], axis=AX.X, op=ALU.max), reads=Ba, writes=[Bsm])
                    k.op('dve', lambda e: e.tensor_reduce(out=sm[:, 1:2], in_=acc[:, 0:nk], axis=AX.X, op=ALU.min), reads=Ba, writes=[Bsm])
                k.op('dve', lambda e: e.memset(acc[0:64, nk - 64:nk], NEG), reads=[Bsm], writes=[Bacc[nblk - 1]])
                if qt >= 2:
                    k.op('dve', lambda e: e.tensor_tensor(out=sm[:, 2:3], in0=sm[:, 0:1], in1=sm[:, 1:2], op=ALU.subtract), reads=[Bsm], writes=[Bsm])
                    k.op('dve', lambda e: e.tensor_scalar(out=Wt[:], in0=C.cst[:, 8:8 + NIT], scalar1=sm[:, 2:3], scalar2=None, op0=ALU.mult),
                         reads=[Bsm, C.Bconst], writes=[BW])
                    k.op('dve', lambda e: e.tensor_tensor(out=mid[:, 0:1], in0=sm[:, 1:2], in1=Wt[:, 0:1], op=ALU.add), reads=[Bsm, BW], writes=[Bmid])
                    for j in range(NIT):
                        k.op('dve', lambda e: e.tensor_scalar(out=junk[:, 0:nk], in0=acc[:, 0:nk], scalar1=mid[:, j:j + 1], scalar2=None,
                                                              op0=ALU.is_ge, op1=ALU.add, accum_out=cnt[:, j:j + 1]),
                             reads=Ba + [Bmid], writes=[Bjunk, Bcnt])
                        k.op('dve', lambda e: e.scalar_tensor_tensor(out=gg[:, j:j + 1], in0=cnt[:, j:j + 1], scalar=TOPK - 0.5, in1=Wt[:, j:j + 1],
                                                                     op0=ALU.is_ge, op1=ALU.mult), reads=[Bcnt, BW], writes=[Bg])
                        if j < NIT - 1:
                            k.op('dve', lambda e: e.scalar_tensor_tensor(out=mid[:, j + 1:j + 2], in0=gg[:, j:j + 1], scalar=Wt[:, j + 1:j + 2],
                                                                         in1=mid[:, j:j + 1], op0=ALU.subtract, op1=ALU.add),
                                 reads=[Bg, BW, Bmid], writes=[Bmid])
                        k.op('dve', lambda e: e.tensor_tensor(out=sm[:, 1:2], in0=sm[:, 1:2], in1=gg[:, j:j + 1], op=ALU.add), reads=[Bsm, Bg], writes=[Bsm])
                    k.op('dve', lambda e: e.tensor_scalar(out=mask[:, 0:nk], in0=acc[:, 0:nk], scalar1=sm[:, 1:2], scalar2=None, op0=ALU.is_ge),
                         reads=Ba + [Bsm], writes=[Bmask])
                else:
                    k.op('dve', lambda e: e.tensor_scalar(out=mask[:, 0:nk], in0=acc[:, 0:nk], scalar1=-1.0e29, scalar2=None, op0=ALU.is_ge),
                         reads=Ba, writes=[Bmask])

            def transp_stage(qt):
                nkc = qt + 1
                for c0 in range(0, nkc, 8):
                    n = min(8, nkc - c0)
                    pT, BpT = pTb.next()
                    for j in range(n):
                        k.op('pe', lambda e: e.transpose(out=pT[:, j * 128:(j + 1) * 128], in_=mask[:, (c0 + j) * 128:(c0 + j + 1) * 128],
                                                         identity=C.ident_bf[:]), reads=[Bmask, C.Bconst], writes=[BpT])
                    k.op('act', lambda e: e.copy(out=maskT[:, c0:c0 + n, :], in_=pT[:, 0:n * 128].rearrange("p (a b) -> p a b", a=n)),
                         reads=[BpT], writes=[BmaskT])

            def attn_stage(qt):
                nkc = qt + 1
                qlatT, Bq1, qrT, Bq2 = Q.pop(qt)
                qs = (qt % 4) * 128
                for hg in range(4):
                    pv = [pPV.next() for _ in range(4)]

                    def scores(kc):
                        ks = slice(kc * 128, (kc + 1) * 128)
                        pt, Bp = pA.next()
                        k.op('pe', lambda e: e.matmul(pt[:], lhsT=ckvT[:, 0, ks], rhs=qlatT[:, 0, hg * 512:(hg + 1) * 512], start=True, stop=False),
                             reads=[Bkeys, Bq1], writes=[Bp])
                        k.op('pe', lambda e: e.matmul(pt[:], lhsT=ckvT[:, 1, ks], rhs=qlatT[:, 1, hg * 512:(hg + 1) * 512], start=False, stop=False),
                             reads=[Bkeys, Bq1], writes=[Bp])
                        k.op('pe', lambda e: e.matmul(pt[:], lhsT=krT[:, ks], rhs=qrT[:, hg * 512:(hg + 1) * 512], start=False, stop=True),
                             reads=[Bkeys, Bq2], writes=[Bp])
                        P, BP = PTp.next()
                        k.op('act', lambda e: e.activation(out=P[:], in_=pt[:].rearrange("p (a b) -> p a b", a=4), func=AF.Exp, scale=A_SCALE),
                             reads=[Bp], writes=[BP])
                        k.op('pool', lambda e: e.tensor_tensor(out=P[:], in0=P[:], in1=maskT[:, kc:kc + 1, :].to_broadcast([128, 4, 128]), op=ALU.mult),
                             reads=[BP, BmaskT], writes=[BP])
                        return P, BP
                    nxt = scores(0)
                    for kc in range(nkc):
                        P, BP = nxt
                        if kc + 1 < nkc:
                            nxt = scores(kc + 1)
                        for j in range(4):
                            k.op('pe', lambda e: e.matmul(pv[j][0][:, 0:257], lhsT=P[:, j, :], rhs=ckva[:, kc, :], start=(kc == 0), stop=(kc == nkc - 1)),
                                 reads=[BP, Bkeys], writes=[pv[j][1]])
                    rs, Brs = rsp.next()
                    ol, Bol = olatp.next()
                    for j in range(4):
                        k.op('act', lambda e: e.activation(out=rs[:, j:j + 1], in_=pv[j][0][:, 256:257], func=AF.Ln), reads=[pv[j][1]], writes=[Brs])
                    k.op('act', lambda e: e.activation(out=rs[:, 0:4], in_=rs[:, 0:4], func=AF.Exp, scale=-1.0), reads=[Brs], writes=[Brs])
                    for j in range(4):
                        k.op('act', lambda e: e.activation(out=ol[:, j, :], in_=pv[j][0][:, 0:256], func=AF.Identity, scale=rs[:, j:j + 1]),
                             reads=[pv[j][1], Brs], writes=[Bol])
                    pT, BpT = pTb.next()
                    for j in range(4):
                        for rc in range(2):
                            jj = j * 2 + rc
                            k.op('pe', lambda e: e.transpose(out=pT[:, jj * 128:(jj + 1) * 128], in_=ol[:, j, rc * 128:(rc + 1) * 128],
                                                             identity=C.ident_bf[:]), reads=[Bol, C.Bconst], writes=[BpT])
                    olT, BolT = olatTp.next()
                    k.op('act', lambda e: e.copy(out=olT[:], in_=pT[:].rearrange("p (a b) -> p a b", a=8)), reads=[BpT], writes=[BolT])
                    po, Bpo = pO.next()
                    for j in range(4):
                        for rc in range(2):
                            hh = hg * 4 + j
                            k.op('pe', lambda e: e.matmul(po[:, j * 128:(j + 1) * 128], lhsT=wuv[:, rc, hh * 128:(hh + 1) * 128], rhs=olT[:, j * 2 + rc, :],
                                                          start=(rc == 0), stop=(rc == 1)), reads=[Bwuv, BolT], writes=[Bpo])
                    k.op('act', lambda e: e.copy(out=oT[:, hg * 4:(hg + 1) * 4, qs:qs + 128], in_=po[:].rearrange("p (a b) -> p a b", a=4)),
                         reads=[Bpo], writes=[BoT])
                if qt % 4 == 3:
                    tsl = slice((qt // 4) * TT, (qt // 4 + 1) * TT)

                    def epiO(g, pt, Bp):
                        xc_, Bxc_ = xcp.next()
                        k.dma('sp', xc_[:], xin_v[:, g, tsl], reads=[Bxin], writes=[Bxc_])
                        yg, Byg = ygp.next()
                        k.op('act', lambda e: e.activation(out=yg[:], in_=pt[:], func=AF.Identity, scale=mod[:, 32 + g:33 + g]),
                             reads=[Bp, C.Bmod], writes=[Byg])
                        k.op('pool', lambda e: e.tensor_tensor(out=xc_[:], in0=yg[:], in1=xc_[:], op=ALU.add), reads=[Byg, Bxc_], writes=[Bxc_])
                        k.dma('sp', x1_d.rearrange("(c p) t -> p c t", p=128)[:, g, tsl], xc_[:], reads=[Bxc_], writes=[Bxout])
                    linear(k, w_o, C.Bw, 16, 16, 128, lambda kc: oT[:, kc, :], [BoT], TT, wop, pA, epiO)

            idx_stage(0)
            bisect_stage(0)
            transp_stage(0)
            for qt in range(16):
                if qt + 1 < 16:
                    idx_stage(qt + 1)
                    bisect_stage(qt + 1)
                attn_stage(qt)
                if qt + 1 < 16:
                    transp_stage(qt + 1)
    k.barrier()


def _wl(w, M=128):
    K, N = w.shape
    return np.ascontiguousarray(w.reshape(K // 128, 128, N // M, M).transpose(2, 1, 0, 3)).reshape(N // M, 128, (K // 128) * M)


def _cols(w, cols, M):
    return _wl(np.ascontiguousarray(w[:, cols]), M)


def _pc(v):
    return np.ascontiguousarray(v.reshape(-1, 128).T)


def shared_inputs(inp):
    f = np.float32
    sh = {}
    for i in range(2):
        aw = inp["ada_w"][i]
        g = _wl(aw)
        sh[f"ada_w{i}"] = np.ascontiguousarray(g.reshape(24, 4, 128, 2048).transpose(0, 2, 1, 3)).reshape(24, 128, 4 * 2048)
        sh[f"ada_b{i}"] = _pc(inp["ada_b"][i])
        sh[f"w1_{i}"] = _wl(inp["mlp_w1"][i])
        sh[f"w2_{i}"] = _wl(inp["mlp_w2"][i])
    sh["fing"] = _pc(inp["final_g"])
    w_in = inp["a_w_in"][0]
    sh["w_in_a"] = _wl(w_in[:, 0:768])
    r = np.arange
    kr = list(r(768, 832)); kr_sw = list(r(800, 832)) + list(r(768, 800))
    ki = list(r(832, 896)); ki_sw = list(r(848, 864)) + list(r(832, 848)) + list(r(864, 896))
    sh["w_in_b"] = np.concatenate([_cols(w_in, c, 64) for c in (kr, kr_sw, ki, ki_sw)], 0)
    sh["w_in_i"] = np.ascontiguousarray(w_in[:, 896:912].reshape(16, 128, 16).transpose(1, 0, 2)).reshape(128, 256)
    dv = np.zeros((128, 16), f)
    dv[:, 0:4] = _pc(inp["a_q_norm_g"][0]); dv[:, 4:6] = _pc(inp["a_kv_norm_g"][0])
    perm = list(r(16, 32)) + list(r(0, 16)) + list(r(32, 64))
    dv[0:64, 6] = inp["a_kidx_ln_g"][0]; dv[0:64, 7] = inp["a_kidx_ln_b"][0]
    dv[0:64, 8] = inp["a_kidx_ln_g"][0][perm]; dv[0:64, 9] = inp["a_kidx_ln_b"][0][perm]
    sh["dsa_vec"] = dv
    wuq = inp["a_w_uq"][0]
    sh["w_uq_n"] = np.concatenate([_cols(wuq, list(r(h * 192, h * 192 + 128)), 128) for h in range(16)], 0)
    gs = []
    for h in range(16):
        b = h * 192 + 128
        gs.append(_cols(wuq, list(r(b, b + 64)), 64))
        gs.append(_cols(wuq, list(r(b + 32, b + 64)) + list(r(b, b + 32)), 64))
    sh["w_uq_r"] = np.concatenate(gs, 0)
    wqi = inp["a_w_qidx"][0]
    gs = []
    for h in range(16):
        b = h * 64
        gs.append(_cols(wqi, list(r(b, b + 64)), 64))
        gs.append(_cols(wqi, list(r(b + 16, b + 32)) + list(r(b, b + 16)) + list(r(b + 32, b + 64)), 64))
    sh["w_qi"] = np.concatenate(gs, 0)
    sh["w_ukT"] = np.ascontiguousarray(inp["a_w_uk"][0].transpose(2, 1, 0)).reshape(128, 16 * 256)
    sh["w_uv"] = np.ascontiguousarray(inp["a_w_uv"][0].reshape(2, 128, 2048).transpose(1, 0, 2)).reshape(128, 4096)
    sh["w_o"] = _wl(inp["a_w_o"][0])
    sh["b_w_w1"] = _wl(inp["b_w_w1"][0], 96)
    sh["b_w_a1"] = _wl(inp["b_w_a1"][0], 96)
    sh["b_w_g1"] = _wl(inp["b_w_g1"][0])
    for nm in ("v", "r", "k", "o"):
        sh["b_w_" + nm] = _wl(inp["b_w_" + nm][0])
    sh["b_w_w2"] = inp["b_w_w2"][0]
    sh["b_w_a2"] = inp["b_w_a2"][0]
    sh["b_w_g2"] = np.ascontiguousarray(inp["b_w_g2"][0].reshape(2, 128, 2048).transpose(1, 0, 2)).reshape(128, 4096)
    vecs = [inp["b_mu"][0][i] for i in range(6)] + [inp["b_w0"][0], inp["b_a0"][0], inp["b_k_k"][0], inp["b_k_a"][0],
                                                    inp["b_r_k"][0].reshape(-1), inp["b_gn_g"][0], inp["b_gn_b"][0]]
    sh["b_rvec"] = np.ascontiguousarray(np.stack([_pc(v) for v in vecs], 0).transpose(1, 0, 2)).reshape(128, 13 * 16)
    ii = np.arange(128)
    same = (ii[:, None] // 64) == (ii[None, :] // 64)
    Ms = ((ii[:, None] < ii[None, :]) & same).astype(f)
    Mi = ((ii[:, None] <= ii[None, :]) & same).astype(f)
    rmask = np.ones((128, 512), f); rmask[:, ::64] = 0.0
    sh["b_rconst"] = np.concatenate([Ms, Mi, Ms.T, same.astype(f), rmask], 1)
    cst = np.zeros((128, 48), f)
    p = np.arange(64)
    cst[0:64, 0] = 1.0 / (10000.0 ** ((2.0 * (p % 32)) / 64.0)).astype(f)
    cst[0:64, 1] = np.where(p < 32, -1.0, 1.0)
    cst[0:32, 2] = 1.0 / (10000.0 ** ((2.0 * (p[0:32] % 16)) / 32.0)).astype(f)
    cst[0:16, 3] = -1.0; cst[16:32, 3] = 1.0
    cst[:, 8:8 + NIT] = 2.0 ** -(np.arange(NIT, dtype=f) + 1.0)
    sh["cst"] = cst
    sh["ident"] = np.eye(128, dtype=f)
    return {k_: np.ascontiguousarray(v, dtype=f) for k_, v in sh.items()}


def core_inputs(inp, b, sh):
    m = dict(sh)
    m["xT"] = np.ascontiguousarray(inp["x"][b].T)
    m["cvec"] = _pc(inp["c"][b])
    m["pos"] = np.ascontiguousarray(inp["positions"][b].reshape(1, S).astype(np.int32))
    return m


_NC = None


def kernel(**inputs):
    global _NC
    inp = {k_: np.asarray(v) for k_, v in inputs.items()}
    if _NC is None:
        _NC = build()
    sh = shared_inputs(inp)
    in_maps = [core_inputs(inp, b, sh) for b in range(8)]
    res = run_bass_kernel_spmd(_NC, in_maps, core_ids=list(range(8)))
    out = np.stack([np.ascontiguousarray(res.results[b]["outT"].T) for b in range(8)], 0)
    return out.astype(np.float32)


def rwkv_phase(k, C, nc, xin, Bxin, xout, Bxout, mod, Wd, rk_d):
    Brk = Buf("rk_d", dram=True)
    xin_v = xin.rearrange("(c p) t -> p c t", p=128)
    xout_v = xout.rearrange("(c p) t -> p c t", p=128)
    rk_v = [rk_d[j].rearrange("(c p) t -> p c t", p=128) for j in range(7)]
    NEG_E = -math.exp(-0.5)
    with ExitStack() as esA:
        gam = sbt(k, esA, "r_gam", [128, 16, 32], F32)
        Bgam = Buf("r_gam")
        rcf = sbt(k, esA, "r_rcf", [128, 1024], F32)
        rcb = sbt(k, esA, "r_rcb", [128, 512], BF16)
        rv = sbt(k, esA, "r_rv", [128, 13, 16], F32)
        gne = sbt(k, esA, "r_gne", [128, 1], F32)
        Brc = Buf("r_rc")
        k.dma('sp', rcf[:], Wd['rconst'], reads=[C.Bin], writes=[Brc])
        k.dma('sp', rv[:], Wd['rvec'].rearrange("p (a b) -> p a b", a=13), reads=[C.Bin], writes=[Brc])
        k.op('dve', lambda e: e.tensor_copy(out=rcb[:], in_=rcf[:, 0:512]), reads=[Brc], writes=[Brc])
        k.op('dve', lambda e: e.memset(gne[:], 64e-5), writes=[Brc])
        omk = sbt(k, esA, "r_omk", [128, 16], F32)
        k.op('dve', lambda e: e.tensor_scalar(out=omk[:], in0=rv[:, 9, :], scalar1=-1.0, scalar2=1.0, op0=ALU.mult, op1=ALU.add), reads=[Brc], writes=[Brc])
        Ms_b, Mi_b, MsT_b, bd_b = rcb[:, 0:128], rcb[:, 128:256], rcb[:, 256:384], rcb[:, 384:512]
        rmask = rcf[:, 512:1024]
        with ExitStack() as es:
            ww2 = sbt(k, es, "ra_ww2", [96, 2048], BF16)
            wa2 = sbt(k, es, "ra_wa2", [96, 2048], BF16)
            wg2 = sbt(k, es, "ra_wg2", [128, 2, 2048], BF16)
            Bwr = Buf("ra_wres")
            k.dma('pool', ww2[:], Wd['w_w2'], reads=[C.Bw], writes=[Bwr])
            k.dma('pool', wa2[:], Wd['w_a2'], reads=[C.Bw], writes=[Bwr])
            k.dma('pool', wg2[:], Wd['w_g2'].rearrange("p (a b) -> p a b", a=2), reads=[C.Bw], writes=[Bwr])
            xt = sbt(k, es, "ra_xt", [128, 16, TT], F32)
            xtb = xt[:].bitcast(BF16)
            Bxt = [Buf(f"ra_xt{i}") for i in range(16)]
            hx = sbt(k, es, "ra_hx", [128, 16, TT + 1], BF16)
            Bhx = Buf("ra_hx")
            vT = sbt(k, es, "ra_vT", [128, 16, TT], BF16)
            BvT = Buf("ra_vT")
            rT = sbt(k, es, "ra_rT", [128, 16, TT], BF16)
            BrT = Buf("ra_rT")
            hw = sbt(k, es, "ra_hw", [96, TT], BF16)
            ha = sbt(k, es, "ra_ha", [96, TT], BF16)
            hg = sbt(k, es, "ra_hg", [128, 2, TT], BF16)
            Bhw, Bha, Bhg = Buf("ra_hw"), Buf("ra_ha"), Buf("ra_hg")
            wp = Pool(k, es, "ra_w", 3, [128, 16 * 128], BF16)
            C.sqpool = Pool(k, es, "ra_sq", 3, [128, TT], BF16)
            C.rspool = Pool(k, es, "ra_rs", 2, [128, TT], F32)
            C.tmppool = Pool(k, es, "ra_tmp", 3, [128, TT], F32)
            pspool = Pool(k, es, "ra_ps", 8, [128, TT], F32, psum=True)
            C.pstat = pspool
            pkpool = Pool.__new__(Pool)
            pkpool.items = pspool.items[6:8]
            pkpool.i = 0
            tf = {}

            def T(name, dt=F32):
                if name not in tf:
                    tf[name] = (sbt(k, es, "ra_t_" + name, [128, TT], dt), Buf("ra_t_" + name))
                return tf[name]
            stgpp = [Pool(k, es, f"ra_stg{p_}", 6, [128, TT], BF16) for p_ in range(2)]
            k.op('dve', lambda e: e.memset(hx[:, :, 0:1], 0.0), writes=[Bhx])

            def mix(c):
                for dc in range(16):
                    k.op('dve', lambda e: e.scalar_tensor_tensor(out=xtb[:, dc, TT:2 * TT], in0=xtb[:, dc, 0:TT], scalar=rv[:, c, dc:dc + 1],
                                                                 in1=hx[:, dc, 1:TT + 1], op0=ALU.mult, op1=ALU.add),
                         reads=[Bxt[dc], Bhx, Brc], writes=[Bxt[dc]])
            xm = lambda kc: xtb[:, kc, TT:2 * TT]

            for tt in range(NT):
                ts = slice(tt * TT, (tt + 1) * TT)
                if tt > 0:
                    k.op('dve', lambda e: e.tensor_copy(out=hx[:, :, 0:1], in_=hx[:, :, TT:TT + 1]), reads=[Bhx] + Bxt, writes=[Bhx])
                for dc in range(16):
                    k.dma('sp', xt[:, dc, :], xin_v[:, dc, ts], reads=[Bxin], writes=[Bxt[dc]])
                modnorm(k, C, xt, Bxt, TT, hx[:, :, 1:TT + 1], Bhx, mod, 16, 0)
                for dc in range(16):
                    k.op('dve', lambda e: e.tensor_tensor(out=xtb[:, dc, 0:TT], in0=hx[:, dc, 0:TT], in1=hx[:, dc, 1:TT + 1], op=ALU.subtract),
                         reads=[Bhx, Bxt[dc]], writes=[Bxt[dc]])
                mix(1)
                linear(k, Wd['w_w1'], C.Bw, 1, 16, 96, xm, Bxt, TT, wp, pspool,
                       lambda g, pt, Bp: k.op('act', lambda e: e.activation(out=hw[:], in_=pt[0:96, :], func=AF.Tanh), reads=[Bp], writes=[Bhw]))
                mix(4)
                linear(k, Wd['w_a1'], C.Bw, 1, 16, 96, xm, Bxt, TT, wp, pspool,
                       lambda g, pt, Bp: k.op('act', lambda e: e.copy(out=ha[:], in_=pt[0:96, :]), reads=[Bp], writes=[Bha]))
                mix(5)
                linear(k, Wd['w_g1'], C.Bw, 2, 16, 128, xm, Bxt, TT, wp, pspool,
                       lambda g, pt, Bp: k.op('act', lambda e: e.activation(out=hg[:, g, :], in_=pt[:], func=AF.Sigmoid), reads=[Bp], writes=[Bhg]))
                mix(3)
                linear(k, Wd['w_v'], C.Bw, 16, 16, 128, xm, Bxt, TT, wp, pspool,
                       lambda g, pt, Bp: k.op('act', lambda e: e.copy(out=vT[:, g, :], in_=pt[:]), reads=[Bp], writes=[BvT]))
                k.dma('sp', rk_v[4][:, :, ts], vT[:], reads=[BvT], writes=[Brk])
                mix(0)
                linear(k, Wd['w_r'], C.Bw, 16, 16, 128, xm, Bxt, TT, wp, pspool,
                       lambda g, pt, Bp: k.op('act', lambda e: e.copy(out=rT[:, g, :], in_=pt[:]), reads=[Bp], writes=[BrT]))
                mix(2)

                pend = []

                def epiK(fc, pk, Bpk):
                    k.rec = []
                    epiK_body(fc, pk, Bpk, str(fc % 2))
                    pend.append(k.rec)
                    k.rec = None
                    if len(pend) == 2:
                        k.replay(pend)
                        pend.clear()

                def epiK_body(fc, pk, Bpk, pr):
                    fcs = slice(fc * 128, (fc + 1) * 128)
                    stgp = stgpp[fc % 2]
                    bA, bB, bC = pspool.items[3 * (fc % 2)], pspool.items[3 * (fc % 2) + 1], pspool.items[3 * (fc % 2) + 2]
                    kf, Bkf = T("kf" + pr)
                    k.op('act', lambda e: e.copy(out=kf[:], in_=pk[:]), reads=[Bpk], writes=[Bkf])
                    pz, Bpz = bA
                    k.op('pe', lambda e: e.matmul(pz[:], lhsT=ww2[0:96, fcs], rhs=hw[0:96, :], start=True, stop=True), reads=[Bwr, Bhw], writes=[Bpz])
                    pa, Bpa = bB
                    k.op('pe', lambda e: e.matmul(pa[:], lhsT=wa2[0:96, fcs], rhs=ha[0:96, :], start=True, stop=True), reads=[Bwr, Bha], writes=[Bpa])
                    pg, Bpg = bC
                    for kc in range(2):
                        k.op('pe', lambda e, kc=kc: e.matmul(pg[:], lhsT=wg2[:, kc, fcs], rhs=hg[:, kc, :], start=(kc == 0), stop=(kc == 1)),
                             reads=[Bwr, Bhg], writes=[Bpg])
                    a, Ba = T("a" + pr); sg, Bsg = T("sg" + pr); lw, Blw = sg, Bsg; g_, Bg_ = T("g" + pr); E1, BE1 = T("E1" + pr)
                    E2, BE2 = T("E2" + pr, BF16); gp, Bgp = T("gp" + pr); E3, BE3 = T("E3" + pr, BF16)
                    kkr, Bkkr = T("kkr" + pr, BF16); nr, Bnr = T("nr" + pr); kk, Bkk = T("kk" + pr, BF16)
                    t_, Bt_ = T("t" + pr, BF16); kp, Bkp = T("kp" + pr); b_, Bb_ = T("b" + pr, BF16); sq, Bsq = T("sq" + pr, BF16)
                    rkb, Brkb = T("rkb" + pr, BF16)
                    k.op('act', lambda e: e.activation(out=a[:], in_=pa[:], func=AF.Sigmoid, bias=rv[:, 7, fc:fc + 1]), reads=[Bpa, Brc], writes=[Ba])
                    k.op('act', lambda e: e.activation(out=sg[:], in_=pz[:], func=AF.Sigmoid, bias=rv[:, 6, fc:fc + 1]), reads=[Bpz, Brc], writes=[Bsg])
                    k.op('dve', lambda e: e.tensor_tensor_scan(out=g_[:], data0=rmask, data1=lw[:], initial=0.0, op0=ALU.mult, op1=ALU.add),
                         reads=[Blw, Brc], writes=[Bg_])
                    k.op('act', lambda e: e.activation(out=E1[:], in_=g_[:], func=AF.Exp, scale=NEG_E), reads=[Bg_], writes=[BE1])
                    k.op('act', lambda e: e.activation(out=E2[:], in_=g_[:], func=AF.Exp, scale=-NEG_E), reads=[Bg_], writes=[BE2])
                    k.op('dve', lambda e: e.tensor_tensor(out=gp[:], in0=g_[:], in1=lw[:], op=ALU.subtract), reads=[Bg_, Blw], writes=[Bgp])
                    k.op('act', lambda e: e.activation(out=E3[:], in_=gp[:], func=AF.Exp, scale=NEG_E), reads=[Bgp], writes=[BE3])
                    k.op('dve', lambda e: e.tensor_copy(out=gam[:, fc, tt * 8:(tt + 1) * 8], in_=E1[:].rearrange("p (a b) -> p a b", a=8)[:, :, 63]),
                         reads=[BE1], writes=[Bgam])
                    k.op('act', lambda e: e.activation(out=kkr[:], in_=pk[:], func=AF.Identity, scale=rv[:, 8, fc:fc + 1]),
                         reads=[Bpk, Brc], writes=[Bkkr])
                    k.op('act', lambda e: e.activation(out=sq[:], in_=kkr[:], func=AF.Square), reads=[Bkkr], writes=[Bsq])
                    pss, Bpss = bA
                    k.op('pe', lambda e: e.matmul(pss[:], lhsT=bd_b, rhs=sq[:], start=True, stop=True), reads=[Bsq, Brc], writes=[Bpss])
                    k.op('dve', lambda e: e.tensor_scalar(out=nr[:], in0=pss[:], scalar1=1e-24, scalar2=None, op0=ALU.max), reads=[Bpss], writes=[Bnr])
                    k.op('act', lambda e: e.activation(out=nr[:], in_=nr[:], func=AF.Ln), reads=[Bnr], writes=[Bnr])
                    k.op('act', lambda e: e.activation(out=nr[:], in_=nr[:], func=AF.Exp, scale=-0.5), reads=[Bnr], writes=[Bnr])
                    k.op('dve', lambda e: e.tensor_tensor(out=kk[:], in0=kkr[:], in1=nr[:], op=ALU.mult), reads=[Bkkr, Bnr], writes=[Bkk])
                    k.op('act', lambda e: e.activation(out=t_[:], in_=a[:], func=AF.Identity, scale=rv[:, 9, fc:fc + 1], bias=omk[:, fc:fc + 1]),
                         reads=[Ba, Brc], writes=[Bt_])
                    k.op('dve', lambda e: e.tensor_tensor(out=kp[:], in0=t_[:], in1=kf[:], op=ALU.mult), reads=[Bt_, Bkf], writes=[Bkp])
                    k.op('dve', lambda e: e.tensor_tensor(out=b_[:], in0=kk[:], in1=a[:], op=ALU.mult), reads=[Bkk, Ba], writes=[Bb_])
                    outs = []
                    sA, BsA = stgp.next()
                    k.op('dve', lambda e: e.scalar_tensor_tensor(out=sA[:], in0=kk[:], scalar=-1.0, in1=E3[:], op0=ALU.mult, op1=ALU.mult),
                         reads=[Bkk, BE3], writes=[BsA])
                    outs.append((0, sA, BsA))
                    sK, BsK = stgp.next()
                    k.op('dve', lambda e: e.tensor_tensor(out=sK[:], in0=kp[:], in1=E2[:], op=ALU.mult), reads=[Bkp, BE2], writes=[BsK])
                    outs.append((1, sK, BsK))
                    sB, BsB = stgp.next()
                    k.op('dve', lambda e: e.tensor_tensor(out=sB[:], in0=b_[:], in1=E2[:], op=ALU.mult), reads=[Bb_, BE2], writes=[BsB])
                    outs.append((2, sB, BsB))
                    sR, BsR = stgp.next()
                    k.op('dve', lambda e: e.tensor_tensor(out=sR[:], in0=rT[:, fc, :], in1=E1[:], op=ALU.mult), reads=[BrT, BE1], writes=[BsR])
                    outs.append((3, sR, BsR))
                    k.op('dve', lambda e: e.scalar_tensor_tensor(out=rkb[:], in0=rT[:, fc, :], scalar=rv[:, 10, fc:fc + 1], in1=kp[:],
                                                                 op0=ALU.mult, op1=ALU.mult), reads=[BrT, Bkp, Brc], writes=[Brkb])
                    pc, Bpc = bB
                    k.op('pe', lambda e: e.matmul(pc[:], lhsT=bd_b, rhs=rkb[:], start=True, stop=True), reads=[Brkb, Brc], writes=[Bpc])
                    sN, BsN = stgp.next()
                    k.op('dve', lambda e: e.tensor_tensor(out=sN[:], in0=pc[:], in1=vT[:, fc, :], op=ALU.mult), reads=[Bpc, BvT], writes=[BsN])
                    outs.append((5, sN, BsN))
                    sG, BsG = stgp.next()
                    k.op('act', lambda e: e.copy(out=sG[:], in_=pg[:]), reads=[Bpg], writes=[BsG])
                    outs.append((6, sG, BsG))
                    for (j, s_, Bs_) in outs:
                        k.dma('sp', rk_v[j][:, fc, ts], s_[:], reads=[Bs_], writes=[Brk])
                linear(k, Wd['w_k'], C.Bw, 16, 16, 128, xm, Bxt, TT, wp, pkpool, epiK)
        k.barrier()
        with ExitStack() as es:
            ldp = [Pool(k, es, f"rb_ld{j}", 2, [128, 16, 128], BF16) for j in range(7)]
            tokp = [sbt(k, es, f"rb_tok{j}", [128, 2048], BF16) for j in range(4)]
            Btok = [Buf(f"rb_tok{j}") for j in range(4)]
            RES = []
            ArkT = sbt(k, es, "rb_Ark", [128, 32, 128], BF16)
            ArbT = sbt(k, es, "rb_Arb", [128, 32, 128], BF16)
            BArk, BArb = Buf("rb_Ark"), Buf("rb_Arb")

            U0 = sbt(k, es, "rb_U0", [128, 2048], F32)
            BU0 = Buf("rb_U0")
            WT = sbt(k, es, "rb_WT", [128, 16, 128], BF16)
            BWT = Buf("rb_WT")
            Utok = sbt(k, es, "rb_Utok", [128, 2048], BF16)
            BUtok = Buf("rb_Utok")
            otok = sbt(k, es, "rb_otok", [128, 32, 64], F32)
            Botok = Buf("rb_otok")
            oc_ = sbt(k, es, "rb_oc", [128, 32, 64], F32)
            onb = sbt(k, es, "rb_on", [128, 2048], BF16)
            Bon = Buf("rb_on")
            st2 = sbt(k, es, "rb_st2", [128, 4, 32], F32)
            Bst2 = Buf("rb_st2")
            ST_f = sbt(k, es, "rb_STf", [128, 16, 64], F32)
            ST_b = sbt(k, es, "rb_STb", [128, 16, 128], BF16)
            stmp = sbt(k, es, "rb_stmp", [128, 16, 64], F32)
            BSTf, BSTb, Bstmp = Buf("rb_STf"), Buf("rb_STb"), Buf("rb_stmp")
            ev = Pool(k, es, "rb_ev", 2, [128, 8, 128], F32)
            ogT = sbt(k, es, "rb_ogT", [128, 16, TT], BF16)
            BogT = Buf("rb_ogT")
            wop = Pool(k, es, "rb_wo", 3, [128, 16 * 128], BF16)
            xcp = Pool(k, es, "rb_xc", 2, [128, TT], F32)
            pp = Pool(k, es, "rb_pp", 8, [128, 512], F32, psum=True)
            for u in range(2):
                g5_ = [Pool(k, es, f"rb_g{j}_{u}", 1, [128, 4, 128], BF16) for j in range(3)]
                pmp_ = Pool(k, es, f"rb_pm{u}", 4, [128, 4, 128], BF16)
                ttp_ = Pool(k, es, f"rb_tt{u}", 2, [128, 4, 128], BF16)
                Yp_ = Pool(k, es, f"rb_Y{u}", 1, [128, 4, 64], BF16)
                pps = Pool.__new__(Pool)
                pps.items = pp.items[4 * u:4 * u + 4]
                pps.i = 0
                RES.append((g5_, pmp_, ttp_, Yp_, pps))
            k.op('dve', lambda e: e.memset(ST_f[:], 0.0), writes=[BSTf])
            k.op('dve', lambda e: e.memset(ST_b[:], 0.0), writes=[BSTb])
            k.op('dve', lambda e: e.memset(Utok[:], 0.0), writes=[BUtok])
            def load_tile(ti_):
                X_ = []
                for j in range(7):
                    t_, B_ = ldp[j].next()
                    k.dma('sp', t_[:], rk_v[j][:, :, ti_ * 128:(ti_ + 1) * 128], reads=[Brk], writes=[B_])
                    X_.append((t_, B_))
                return X_
            Xn = load_tile(0)
            for ti in range(16):
                tc = slice(ti * 128, (ti + 1) * 128)
                X = Xn
                if ti + 1 < 16:
                    Xn = load_tile(ti + 1)
                (At, BAt), (Kt, BKt), (Bt, BBt), (Rt, BRt), (Vt, BVt), (BON, BBON), (GATE, BGATE) = X
                for j, (src, Bsrc) in enumerate([(Vt, BVt), (Kt, BKt), (Bt, BBt), (At, BAt)]):
                    for half in range(2):
                        pt, Bp = pp.next()
                        ptb = pt[:].bitcast(BF16)
                        for q in range(8):
                            k.op('pe', lambda e: e.transpose(out=ptb[:, q * 128:(q + 1) * 128], in_=src[:, half * 8 + q, :], identity=C.ident_bf[:]),
                                 reads=[Bsrc, C.Bconst], writes=[Bp])
                        k.op('act', lambda e: e.copy(out=tokp[j][:, half * 1024:(half + 1) * 1024], in_=ptb[:, 0:1024]), reads=[Bp], writes=[Btok[j]])
                Vtok, Ktok, Btk, Atok = tokp
                BVtok, BKtok, BBtok, BAtok = Btok
                Ark4 = ArkT[:].rearrange("p (f q) t -> p f q t", q=2)
                Arb4 = ArbT[:].rearrange("p (f q) t -> p f q t", q=2)
                U04 = U0[:].rearrange("p (f q i) -> p f q i", q=2, i=64)
                def group(gq, R):
                    g5, pmp, ttp, Yp, pp = R
                    par, gg = gq // 4, gq % 4
                    fs = slice(par * 64, par * 64 + 64)
                    fcl = [gg * 4 + j for j in range(4)]
                    hs = [2 * fc + par for fc in fcl]
                    M0, BM0 = g5[0].next()
                    P0, BP0 = g5[1].next()
                    Aak, BAak = g5[2].next()
                    prods = [((Bt, BBt), (At, BAt), Ms_b, M0[:], BM0), ((At, BAt), (Bt, BBt), MsT_b, P0[:], BP0),
                             ((Kt, BKt), (At, BAt), Ms_b, Aak[:], BAak), ((Kt, BKt), (Rt, BRt), Mi_b, Ark4[:, gg * 4:(gg + 1) * 4, par, :], BArk),
                             ((Bt, BBt), (Rt, BRt), Mi_b, Arb4[:, gg * 4:(gg + 1) * 4, par, :], BArb)]
                    for ((L, BL), (R_, BR), msk, dst, Bdst) in prods:
                        pt, Bp = pp.next()
                        for j, fc in enumerate(fcl):
                            k.op('pe', lambda e: e.matmul(pt[:, j * 128:(j + 1) * 128], lhsT=L[fs, fc, :], rhs=R_[fs, fc, :], start=True, stop=True),
                                 reads=[BL, BR], writes=[Bp])
                        k.op('dve', lambda e: e.tensor_tensor(out=dst, in0=pt[:].rearrange("p (a b) -> p a b", a=4),
                                                              in1=msk.unsqueeze(1).to_broadcast([128, 4, 128]), op=ALU.mult), reads=[Bp, Brc], writes=[Bdst])
                    pt, Bp = pp.next()
                    for j, h in enumerate(hs):
                        k.op('pe', lambda e: e.matmul(pt[:, j * 64:(j + 1) * 64], lhsT=Aak[:, j, :], rhs=Vtok[:, h * 64:(h + 1) * 64], start=True, stop=True),
                             reads=[BAak, BVtok], writes=[Bp])
                    Yg, BYg = Yp.next()
                    k.op('act', lambda e: e.copy(out=Yg[:], in_=pt[:, 0:256].rearrange("p (a b) -> p a b", a=4)), reads=[Bp], writes=[BYg])
                    TTt, BTT = ttp.next()
                    k.op('dve', lambda e: e.tensor_tensor(out=TTt[:], in0=M0[:], in1=C.ident_bf[:].unsqueeze(1).to_broadcast([128, 4, 128]), op=ALU.add),
                         reads=[BM0, C.Bconst], writes=[BTT])
                    Pk, BPk, Mk, BMk = P0, BP0, M0, BM0
                    for lev in range(1, 6):
                        pt, Bp = pp.next()
                        for j in range(4):
                            k.op('pe', lambda e: e.matmul(pt[:, j * 128:(j + 1) * 128], lhsT=Mk[:, j, :], rhs=Pk[:, j, :], start=True, stop=True),
                                 reads=[BMk, BPk], writes=[Bp])
                        Pn, BPn = pmp.next()
                        k.op('act', lambda e: e.copy(out=Pn[:], in_=pt[:].rearrange("p (a b) -> p a b", a=4)), reads=[Bp], writes=[BPn])
                        if lev < 5:
                            pt2, Bp2 = pp.next()
                            for j in range(4):
                                k.op('pe', lambda e: e.matmul(pt2[:, j * 128:(j + 1) * 128], lhsT=Pk[:, j, :], rhs=Mk[:, j, :], start=True, stop=True),
                                     reads=[BMk, BPk], writes=[Bp2])
                            Mn, BMn = pmp.next()
                            k.op('act', lambda e: e.copy(out=Mn[:], in_=pt2[:].rearrange("p (a b) -> p a b", a=4)), reads=[Bp2], writes=[BMn])
                        pt3, Bp3 = pp.next()
                        for j in range(4):
                            k.op('pe', lambda e: e.matmul(pt3[:, j * 128:(j + 1) * 128], lhsT=Pn[:, j, :], rhs=TTt[:, j, :], start=True, stop=True),
                                 reads=[BPn, BTT], writes=[Bp3])
                        TTn, BTTn = ttp.next()
                        k.op('dve', lambda e: e.tensor_tensor(out=TTn[:], in0=pt3[:].rearrange("p (a b) -> p a b", a=4), in1=TTt[:], op=ALU.add),
                             reads=[Bp3, BTT], writes=[BTTn])
                        TTt, BTT = TTn, BTTn
                        Pk, BPk = Pn, BPn
                        if lev < 5:
                            Mk, BMk = Mn, BMn
                    pt, Bp = pp.next()
                    for j in range(4):
                        k.op('pe', lambda e: e.matmul(pt[:, j * 64:(j + 1) * 64], lhsT=TTt[:, j, :], rhs=Yg[:, j, :], start=True, stop=True),
                             reads=[BTT, BYg], writes=[Bp])
                    k.op('act', lambda e: e.copy(out=U04[:, gg * 4:(gg + 1) * 4, par, :], in_=pt[:, 0:256].rearrange("p (a b) -> p a b", a=4)),
                         reads=[Bp], writes=[BU0])
                    pt, Bp = pp.next()
                    for j, h in enumerate(hs):
                        k.op('pe', lambda e: e.matmul(pt[fs, j * 128:(j + 1) * 128], lhsT=Atok[:, h * 64:(h + 1) * 64], rhs=TTt[:, j, :],
                                                      start=True, stop=True), reads=[BAtok, BTT], writes=[Bp])
                    k.op('act', lambda e: e.copy(out=WT[fs, gg * 4:(gg + 1) * 4, :], in_=pt[fs, :].rearrange("p (a b) -> p a b", a=4)), reads=[Bp], writes=[BWT])
                for gq in range(0, 8, 2):
                    chains = []
                    for u in range(2):
                        k.rec = []
                        group(gq + u, RES[u])
                        chains.append(k.rec)
                        k.rec = None
                    k.replay(chains)
                for c in range(2):
                    cs = slice(64 * c, 64 * c + 64)
                    pu = [pp.next() for _ in range(4)]
                    for fc in range(16):
                        k.op('pe', lambda e: e.matmul(pu[fc // 4][0][cs, (fc % 4) * 128:(fc % 4 + 1) * 128], lhsT=WT[:, fc, cs], rhs=ST_b[:, fc, :],
                                                      start=True, stop=True), reads=[BWT, BSTb], writes=[pu[fc // 4][1]])
                    for q in range(4):
                        k.op('dve', lambda e: e.tensor_tensor(out=Utok[cs, q * 512:(q + 1) * 512], in0=pu[q][0][cs, :], in1=U0[cs, q * 512:(q + 1) * 512], op=ALU.add),
                             reads=[pu[q][1], BU0], writes=[BUtok])
                    po = [pp.next() for _ in range(4)]
                    for fc in range(16):
                        k.op('pe', lambda e: e.matmul(po[fc // 4][0][cs, (fc % 4) * 128:(fc % 4 + 1) * 128], lhsT=Rt[:, fc, cs], rhs=ST_b[:, fc, :],
                                                      start=True, stop=False), reads=[BRt, BSTb], writes=[po[fc // 4][1]])
                        for par in range(2):
                            h = 2 * fc + par
                            hc = slice(h * 64, (h + 1) * 64)
                            o_ap = po[fc // 4][0][cs, (h % 8) * 64:(h % 8 + 1) * 64]
                            k.op('pe', lambda e: e.matmul(o_ap, lhsT=ArkT[:, h, cs], rhs=Vtok[:, hc], start=False, stop=False),
                                 reads=[BArk, BVtok], writes=[po[fc // 4][1]])
                            k.op('pe', lambda e: e.matmul(o_ap, lhsT=ArbT[:, h, cs], rhs=Utok[:, hc], start=False, stop=(par == 1)),
                                 reads=[BArb, BUtok], writes=[po[fc // 4][1]])
                    for q in range(4):
                        k.op('act', lambda e: e.copy(out=otok[cs, q * 8:(q + 1) * 8, :], in_=po[q][0][cs, :].rearrange("p (a b) -> p a b", a=8)),
                             reads=[po[q][1]], writes=[Botok])
                    pS = [pp.next() for _ in range(2)]
                    for h in range(32):
                        fc, fs = h // 2, slice((h % 2) * 64, (h % 2) * 64 + 64)
                        s_ap = pS[fc // 8][0][fs, (fc % 8) * 64:(fc % 8 + 1) * 64]
                        hc = slice(h * 64, (h + 1) * 64)
                        k.op('pe', lambda e: e.matmul(s_ap, lhsT=Ktok[cs, hc], rhs=Vtok[cs, hc], start=True, stop=False),
                             reads=[BKtok, BVtok], writes=[pS[fc // 8][1]])
                        k.op('pe', lambda e: e.matmul(s_ap, lhsT=Btk[cs, hc], rhs=Utok[cs, hc], start=False, stop=True),
                             reads=[BBtok, BUtok], writes=[pS[fc // 8][1]])
                    ci = ti * 2 + c
                    for q in range(2):
                        k.op('dve', lambda e: e.tensor_tensor(out=stmp[:, q * 8:(q + 1) * 8, :], in0=pS[q][0][:].rearrange("p (a b) -> p a b", a=8),
                                                              in1=ST_f[:, q * 8:(q + 1) * 8, :], op=ALU.add), reads=[pS[q][1], BSTf], writes=[Bstmp])
                    k.op('dve', lambda e: e.tensor_tensor(out=ST_f[:], in0=stmp[:], in1=gam[:, :, ci:ci + 1].to_broadcast([128, 16, 64]), op=ALU.mult),
                         reads=[Bstmp, Bgam], writes=[BSTf])
                    k.op('act', lambda e: e.copy(out=ST_b[0:64, :, 0:64], in_=ST_f[0:64, :, :]), reads=[BSTf], writes=[BSTb])
                    k.op('act', lambda e: e.copy(out=ST_b[64:128, :, 64:128], in_=ST_f[64:128, :, :]), reads=[BSTf], writes=[BSTb])
                k.op('dve', lambda e: e.tensor_reduce(out=st2[:, 0, :], in_=otok[:], axis=AX.X, op=ALU.add), reads=[Botok], writes=[Bst2])
                k.op('dve', lambda e: e.tensor_scalar(out=st2[:, 1, :], in0=st2[:, 0, :], scalar1=-1.0 / 64, scalar2=None, op0=ALU.mult), reads=[Bst2], writes=[Bst2])
                k.op('dve', lambda e: e.tensor_tensor(out=oc_[:], in0=otok[:], in1=st2[:, 1, :].unsqueeze(2).to_broadcast([128, 32, 64]), op=ALU.add),
                     reads=[Botok, Bst2], writes=[Bon])
                k.op('act', lambda e: e.activation(out=otok[:], in_=oc_[:], func=AF.Square), reads=[Bon], writes=[Botok])
                k.op('dve', lambda e: e.tensor_reduce(out=st2[:, 2, :], in_=otok[:], axis=AX.X, op=ALU.add), reads=[Botok], writes=[Bst2])
                k.op('act', lambda e: e.activation(out=st2[:, 3, :], in_=st2[:, 2, :], func=AF.Sqrt, scale=1.0 / 64, bias=gne[:]), reads=[Bst2, Brc], writes=[Bst2])
                k.op('dve', lambda e: e.reciprocal(out=st2[:, 3, :], in_=st2[:, 3, :]), reads=[Bst2], writes=[Bst2])
                k.op('dve', lambda e: e.tensor_tensor(out=onb[:].rearrange("p (a b) -> p a b", a=32), in0=oc_[:],
                                                      in1=st2[:, 3, :].unsqueeze(2).to_broadcast([128, 32, 64]), op=ALU.mult), reads=[Bon, Bst2], writes=[Bon])
                tsub = slice((ti % 4) * 128, (ti % 4 + 1) * 128)
                for half in range(2):
                    pt, Bp = pp.next()
                    ptb = pt[:].bitcast(BF16)
                    for q in range(8):
                        fc = half * 8 + q
                        k.op('pe', lambda e: e.transpose(out=ptb[:, q * 128:(q + 1) * 128], in_=onb[:, fc * 128:(fc + 1) * 128], identity=C.ident_bf[:]),
                             reads=[Bon, C.Bconst], writes=[Bp])
                    e_, Be_ = ev.next()
                    hs_ = slice(half * 8, half * 8 + 8)
                    k.op('dve', lambda e: e.tensor_tensor(out=e_[:], in0=ptb[:, 0:1024].rearrange("p (a b) -> p a b", a=8),
                                                          in1=rv[:, 11, hs_].unsqueeze(2).to_broadcast([128, 8, 128]), op=ALU.mult), reads=[Bp, Brc], writes=[Be_])
                    k.op('dve', lambda e: e.tensor_tensor(out=e_[:], in0=e_[:], in1=rv[:, 12, hs_].unsqueeze(2).to_broadcast([128, 8, 128]), op=ALU.add),
                         reads=[Be_, Brc], writes=[Be_])
                    k.op('dve', lambda e: e.tensor_tensor(out=e_[:], in0=e_[:], in1=BON[:, hs_, :], op=ALU.add), reads=[Be_, BBON], writes=[Be_])
                    k.op('dve', lambda e: e.tensor_tensor(out=ogT[:, hs_, tsub], in0=e_[:], in1=GATE[:, hs_, :], op=ALU.mult), reads=[Be_, BGATE], writes=[BogT])
                if ti % 4 == 3:
                    tsl = slice((ti // 4) * TT, (ti // 4 + 1) * TT)

                    def epiO(g, pt, Bp):
                        xc_, Bxc_ = xcp.next()
                        k.dma('sp', xc_[:], xin_v[:, g, tsl], reads=[Bxin], writes=[Bxc_])
                        k.op('dve', lambda e: e.scalar_tensor_tensor(out=xc_[:], in0=pt[:], scalar=mod[:, 32 + g:33 + g], in1=xc_[:],
                                                                     op0=ALU.mult, op1=ALU.add), reads=[Bp, Bxc_, C.Bmod], writes=[Bxc_])
                        k.dma('sp', xout_v[:, g, tsl], xc_[:], reads=[Bxc_], writes=[Bxout])
                    linear(k, Wd['w_o'], C.Bw, 16, 16, 128, lambda kc: ogT[:, kc, :], [BogT], TT, wop, pp, epiO)
    k.barrier()
```

```python
import math
import numpy as np
from contextlib import ExitStack
import concourse.bass as bass
import concourse.mybir as mybir
from concourse.bass_utils import run_bass_kernel_spmd

F32 = mybir.dt.float32
BF16 = mybir.dt.bfloat16
I32 = mybir.dt.int32
ALU = mybir.AluOpType
AF = mybir.ActivationFunctionType
AX = mybir.AxisListType

D = 2048
S = 2048
DFF = 8192
NT = 4
TT = 512
EPS = 1e-6
A_SCALE = 192.0 ** -0.5
NIT = 22
TOPK = 256
NEG = -1.0e30


class Buf:
    def __init__(self, name, dram=False):
        self.name = name
        self.dram = dram
        self.lw = None
        self.lws = {}
        self.rd = {}
        self.dsem = None
        self.dq = None
        self.dcnt = 0


class _RecEng:
    def __getattr__(self, name):
        return lambda *a, **kw: (name, a, kw)


_REC = _RecEng()


def _call(c):
    return lambda eng: getattr(eng, c[0])(*c[1], **c[2])


class KB:
    def __init__(self, nc, es):
        self.nc = nc
        self.es0 = es
        self.engs = {'pe': nc.tensor, 'dve': nc.vector, 'act': nc.scalar, 'pool': nc.gpsimd, 'sp': nc.sync}
        self.sem = {}
        self.cnt = {}
        self.nsem = 0
        self.waited = {e: {} for e in self.engs}
        self.dbufs = []
        self.dfree = {}
        self.rec = None
        for e in self.engs:
            self._newsem(e)

    def newsem(self, name):
        self.nsem += 1
        return self.es0.enter_context(self.nc.semaphore(f"{name}_{self.nsem}"))

    def _newsem(self, e):
        self.sem[e] = self.newsem("s_" + e)
        self.cnt[e] = 0

    def _wait(self, e, tok):
        if tok is None:
            return
        sem, val = tok
        w = self.waited[e]
        if w.get(id(sem), 0) >= val:
            return
        self.engs[e].wait_ge(sem, val)
        w[id(sem)] = val

    def _deps(self, e, reads, writes, skip_dma_waw=False):
        own = self.sem[e]
        for b in reads:
            for t in b.lws.values():
                self._wait(e, t)
            if b.lw is not None:
                if e == 'pe' and b.lw[0] is own:
                    continue
                self._wait(e, b.lw)
        for b in writes:
            if b.lw is not None:
                if b.lw[0] is own and e == 'pe':
                    pass
                elif skip_dma_waw and (b.lw[0] is b.dsem or b.dram):
                    pass
                else:
                    self._wait(e, b.lw)
            for t in b.rd.values():
                if t[0] is own and e == 'pe':
                    continue
                self._wait(e, t)

    def _done(self, tok, reads, writes):
        for b in reads:
            b.rd[id(tok[0])] = tok
        for b in writes:
            b.lw = tok
            b.rd = {}

    def replay(self, chains):
        self.rec = None
        idx = [0] * len(chains)
        while True:
            prog = False
            for ci, ch in enumerate(chains):
                if idx[ci] < len(ch):
                    it = ch[idx[ci]]
                    idx[ci] += 1
                    prog = True
                    if it[0] == 'op':
                        self.op(it[1], it[2], it[3], it[4])
                    else:
                        self.dma(it[1], it[2], it[3], reads=it[4], writes=it[5], **it[6])
            if not prog:
                break

    def op(self, e, fn, reads=(), writes=()):
        if self.rec is not None:
            self.rec.append(('op', e, _call(fn(_REC)), list(reads), list(writes)))
            return None
        self._deps(e, reads, writes)
        ins = fn(self.engs[e])
        if self.cnt[e] >= 30000:
            self._newsem(e)
        self.cnt[e] += 1
        ins.then_inc(self.sem[e], 1)
        tok = (self.sem[e], self.cnt[e])
        self._done(tok, reads, writes)
        return tok

    def dma(self, q, out, in_, reads=(), writes=(), **kw):
        if self.rec is not None:
            self.rec.append(('dma', q, out, in_, list(reads), list(writes), kw))
            return None
        assert len(writes) == 1
        self._deps(q, reads, writes, skip_dma_waw=True)
        dst = writes[0]
        w = reads[0] if dst.dram else dst
        assert not w.dram
        if w.dsem is None:
            fl = self.dfree.setdefault(q, [])
            if fl:
                w.dsem, w.dcnt = fl.pop()
            else:
                w.dsem = self.newsem("d_" + w.name)
                w.dcnt = 0
            w.dq = q
            self.dbufs.append(w)
        assert w.dq == q, (w.name, w.dq, q)
        ins = self.engs[q].dma_start(out=out, in_=in_, **kw)
        w.dcnt += 16
        ins.then_inc(w.dsem, 16)
        tok = (w.dsem, w.dcnt)
        self._done(tok, reads, writes)
        if dst.dram:
            dst.lws[id(tok[0])] = tok
        return tok

    def barrier(self):
        for e in self.engs:
            for e2 in self.engs:
                if e2 != e and self.cnt[e2] > 0:
                    self._wait(e, (self.sem[e2], self.cnt[e2]))
            for b in self.dbufs:
                if b.dcnt > 0:
                    self._wait(e, (b.dsem, b.dcnt))
        for b in self.dbufs:
            self.dfree.setdefault(b.dq, []).append((b.dsem, b.dcnt))
            b.dsem = None
        self.dbufs = []


class Pool:
    def __init__(self, k, es, name, n, shape, dt, psum=False):
        self.items = []
        for i in range(n):
            if psum:
                t = es.enter_context(k.nc.psum_tensor(f"t{_uid()}_{name}{i}", list(shape), dt))
            else:
                t = es.enter_context(k.nc.sbuf_tensor(f"t{_uid()}_{name}{i}", list(shape), dt))
            self.items.append((t, Buf(f"{name}{i}")))
        self.i = 0

    def next(self):
        it = self.items[self.i % len(self.items)]
        self.i += 1
        return it


_UID = [0]


def _uid():
    _UID[0] += 1
    return _UID[0]


def sbt(k, es, name, shape, dt):
    return es.enter_context(k.nc.sbuf_tensor(f"t{_uid()}_" + name, list(shape), dt))


class Ctx:
    pass


def linear(k, wd, Bwd, G, KC, M, rhs_fn, rhs_bufs, N, wpool, pspool, epi, g0=0):
    for g in range(G):
        wt, Bw = wpool.next()
        k.dma('pool', wt[:, 0:KC * M], wd[g0 + g], reads=[Bwd], writes=[Bw])
        pt, Bp = pspool.next()
        for kc in range(KC):
            k.op('pe', lambda e: e.matmul(pt[0:M, 0:N], lhsT=wt[:, kc * M:(kc + 1) * M], rhs=rhs_fn(kc),
                                          start=(kc == 0), stop=(kc == KC - 1)),
                 reads=[Bw] + rhs_bufs, writes=[Bp])
        epi(g, pt, Bp)


def rms_stats(k, C, src_fn, src_bufs, nchunk, N, P, inv_n, ones):
    ps, Bps = C.pstat.next()
    for c in range(nchunk):
        sq, Bsq = C.sqpool.next()
        k.op('act', lambda e: e.activation(out=sq[0:P, 0:N], in_=src_fn(c), func=AF.Square), reads=src_bufs, writes=[Bsq])
        k.op('pe', lambda e: e.matmul(ps[0:P, 0:N], lhsT=ones[0:P, 0:P], rhs=sq[0:P, 0:N], start=(c == 0), stop=(c == nchunk - 1)),
             reads=[Bsq, C.Bconst], writes=[Bps])
    rstd, Brs = C.rspool.next()
    k.op('act', lambda e: e.activation(out=rstd[0:P, 0:N], in_=ps[0:P, 0:N], func=AF.Sqrt, scale=inv_n, bias=C.eps[0:P, :]),
         reads=[Bps, C.Bconst], writes=[Brs])
    k.op('dve', lambda e: e.reciprocal(out=rstd[0:P, 0:N], in_=rstd[0:P, 0:N]), reads=[Brs], writes=[Brs])
    return rstd, Brs


def modnorm(k, C, xt, Bxt, N, out, Bout, mod, scol, hcol):
    rstd, Brs = rms_stats(k, C, lambda c: xt[:, c, 0:N], Bxt, 16, N, 128, 1.0 / D, C.ones_bf)
    for dc in range(16):
        tmp, Btmp = C.tmppool.next()
        k.op('dve', lambda e: e.tensor_tensor(out=tmp[:, 0:N], in0=xt[:, dc, 0:N], in1=rstd[:, 0:N], op=ALU.mult),
             reads=[Bxt[dc], Brs], writes=[Btmp])
        k.op('act', lambda e: e.activation(out=out[:, dc, 0:N], in_=tmp[:, 0:N], func=AF.Identity,
                                           scale=mod[:, scol + dc:scol + dc + 1], bias=mod[:, hcol + dc:hcol + dc + 1]),
             reads=[Btmp, C.Bmod], writes=[Bout])


def mlp_phase(k, C, xin, Bxin, xout, Bxout, w1d, w2d, mod, final=None):
    TL = 2 * TT
    with ExitStack() as es:
        xt = sbt(k, es, "m_xt", [128, 16, TT], F32)
        Bxt = [Buf(f"m_xt{i}") for i in range(16)]
        hT = sbt(k, es, "m_hT", [128, 16, TL], BF16)
        BhT = Buf("m_hT")
        uT = sbt(k, es, "m_uT", [128, 32, TL], BF16)
        BuT = [Buf(f"m_uT{i}") for i in range(32)]
        w1pool = Pool(k, es, "m_w1", 3, [128, 16 * 128], BF16)
        w2pool = Pool(k, es, "m_w2", 2, [128, 32 * 128], BF16)
        rpool = Pool(k, es, "m_r", 3, [128, TT], F32)
        xcp = Pool(k, es, "m_xc", 4, [128, TT], F32)
        C.sqpool = Pool(k, es, "m_sq", 3, [128, TT], BF16)
        C.rspool = Pool(k, es, "m_rs", 2, [128, TT], F32)
        C.tmppool = Pool(k, es, "m_tmp", 3, [128, TT], F32)
        C.pstat = Pool(k, es, "m_pst", 1, [128, TT], F32, psum=True)
        pspool = Pool(k, es, "m_ps", 6, [128, TT], F32, psum=True)
        xin_v = xin.rearrange("(c p) t -> p c t", p=128)
        xout_v = xout.rearrange("(c p) t -> p c t", p=128)
        st = {}

        def prep_stats(tt, sub):
            ts = slice(tt * TL + sub * TT, tt * TL + (sub + 1) * TT)
            for dc in range(16):
                k.dma('sp', xt[:, dc, :], xin_v[:, dc, ts], reads=[Bxin], writes=[Bxt[dc]])
            st[sub] = rms_stats(k, C, lambda c: xt[:, c, :], Bxt, 16, TT, 128, 1.0 / D, C.ones_bf)

        def prep_norm(tt, sub):
            rstd, Brs = st[sub]
            for dc in range(16):
                tmp, Btmp = C.tmppool.next()
                k.op('dve', lambda e: e.tensor_tensor(out=tmp[:], in0=xt[:, dc, :], in1=rstd[:], op=ALU.mult), reads=[Bxt[dc], Brs], writes=[Btmp])
                k.op('act', lambda e: e.activation(out=hT[:, dc, sub * TT:(sub + 1) * TT], in_=tmp[:], func=AF.Identity,
                                                   scale=mod[:, 64 + dc:65 + dc], bias=mod[:, 48 + dc:49 + dc]), reads=[Btmp, C.Bmod], writes=[BhT])
        NTL = S // TL
        for sub in range(2):
            prep_stats(0, sub)
            prep_norm(0, sub)
        for tt in range(NTL):
            for half in range(2):
                for g in range(32):
                    wt, Bw = w1pool.next()
                    k.dma('pool', wt[:], w1d[half * 32 + g], reads=[C.Bw], writes=[Bw])
                    for sub in range(2):
                        pt, Bp = pspool.next()
                        for kc in range(16):
                            k.op('pe', lambda e: e.matmul(pt[:], lhsT=wt[:, kc * 128:(kc + 1) * 128], rhs=hT[:, kc, sub * TT:(sub + 1) * TT],
                                                          start=(kc == 0), stop=(kc == 15)), reads=[Bw, BhT], writes=[Bp])
                        r, Br = rpool.next()
                        k.op('act', lambda e: e.activation(out=r[:], in_=pt[:], func=AF.Relu), reads=[Bp], writes=[Br])
                        k.op('dve', lambda e: e.tensor_tensor(out=uT[:, g, sub * TT:(sub + 1) * TT], in0=r[:], in1=r[:], op=ALU.mult),
                             reads=[Br], writes=[BuT[g]])
                for dc in range(16):
                    if half == 1 and tt + 1 < NTL:
                        if dc == 1:
                            prep_stats(tt + 1, 0)
                        elif dc == 4:
                            prep_norm(tt + 1, 0)
                        elif dc == 8:
                            prep_stats(tt + 1, 1)
                        elif dc == 11:
                            prep_norm(tt + 1, 1)
                    wt, Bw = w2pool.next()
                    k.dma('pool', wt[:], w2d[dc][:, half * 4096:(half + 1) * 4096], reads=[C.Bw], writes=[Bw])
                    for sub in range(2):
                        ts = slice(tt * TL + sub * TT, tt * TL + (sub + 1) * TT)
                        pt, Bp = pspool.next()
                        for kc in range(32):
                            k.op('pe', lambda e: e.matmul(pt[:], lhsT=wt[:, kc * 128:(kc + 1) * 128], rhs=uT[:, kc, sub * TT:(sub + 1) * TT],
                                                          start=(kc == 0), stop=(kc == 31)), reads=[Bw] + BuT, writes=[Bp])
                        xc, Bxc = xcp.next()
                        if half == 0:
                            k.dma('sp', xc[:], xin_v[:, dc, ts], reads=[Bxin], writes=[Bxc])
                        else:
                            k.dma('sp', xc[:], xout_v[:, dc, ts], reads=[Bxout], writes=[Bxc])
                        k.op('dve', lambda e: e.scalar_tensor_tensor(out=xc[:], in0=pt[:], scalar=mod[:, 80 + dc:81 + dc], in1=xc[:],
                                                                     op0=ALU.mult, op1=ALU.add), reads=[Bp, Bxc, C.Bmod], writes=[Bxc])
                        k.dma('sp', xout_v[:, dc, ts], xc[:], reads=[Bxc], writes=[Bxout])
        if final is not None:
            outT, Bout, fing = final
            out_v = outT.rearrange("(c p) t -> p c t", p=128)
            for tt in range(NT):
                ts = slice(tt * TT, (tt + 1) * TT)
                for dc in range(16):
                    k.dma('sp', xt[:, dc, :], xout_v[:, dc, ts], reads=[Bxout], writes=[Bxt[dc]])
                rstd, Brs = rms_stats(k, C, lambda c: xt[:, c, :], Bxt, 16, TT, 128, 1.0 / D, C.ones_bf)
                for dc in range(16):
                    k.op('dve', lambda e: e.scalar_tensor_tensor(out=xt[:, dc, :], in0=xt[:, dc, :], scalar=fing[:, dc:dc + 1], in1=rstd[:],
                                                                 op0=ALU.mult, op1=ALU.mult), reads=[Bxt[dc], Brs, C.Bconst], writes=[Bxt[dc]])
                    k.dma('sp', out_v[:, dc, ts], xt[:, dc, :], reads=[Bxt[dc]], writes=[Bout])
    k.barrier()


def rope_tables(k, C, es, posd, tabs_d, Btabs):
    posi = sbt(k, es, "r_posi", [64, 1, S], I32)
    posf = sbt(k, es, "r_posf", [64, S], F32)
    ang = sbt(k, es, "r_ang", [64, S], F32)
    t1 = sbt(k, es, "r_t1", [64, S], F32)
    ti = sbt(k, es, "r_ti", [64, S], I32)
    Bp, Ba, Bt = Buf("r_pos"), Buf("r_ang"), Buf("r_t")
    k.dma('sp', posi[:], posd.partition_broadcast(64), reads=[C.Bin], writes=[Bp])
    k.op('dve', lambda e: e.tensor_copy(out=posf[:], in_=posi[:, 0, :]), reads=[Bp], writes=[Bp])
    TWO_PI = 2.0 * math.pi
    for ti_, (icol, off, scol) in enumerate([(0, 0.5 * math.pi, None), (0, 0.0, 1), (2, 0.5 * math.pi, None), (2, 0.0, 3)]):
        k.op('dve', lambda e: e.tensor_scalar(out=ang[:], in0=posf[:], scalar1=C.cst[0:64, icol:icol + 1], scalar2=off, op0=ALU.mult, op1=ALU.add),
             reads=[Bp, C.Bconst], writes=[Ba])
        k.op('dve', lambda e: e.tensor_scalar(out=t1[:], in0=ang[:], scalar1=1.0 / TWO_PI, scalar2=None, op0=ALU.mult), reads=[Ba], writes=[Bt])
        k.op('dve', lambda e: e.tensor_copy(out=ti[:], in_=t1[:]), reads=[Bt], writes=[Bt])
        k.op('dve', lambda e: e.tensor_copy(out=t1[:], in_=ti[:]), reads=[Bt], writes=[Bt])
        k.op('dve', lambda e: e.scalar_tensor_tensor(out=ang[:], in0=t1[:], scalar=-TWO_PI, in1=ang[:], op0=ALU.mult, op1=ALU.add),
             reads=[Bt, Ba], writes=[Ba])
        k.op('dve', lambda e: e.tensor_scalar(out=t1[:], in0=ang[:], scalar1=math.pi, scalar2=-TWO_PI, op0=ALU.is_gt, op1=ALU.mult), reads=[Ba], writes=[Bt])
        k.op('dve', lambda e: e.tensor_tensor(out=ang[:], in0=ang[:], in1=t1[:], op=ALU.add), reads=[Ba, Bt], writes=[Ba])
        k.op('dve', lambda e: e.tensor_scalar(out=t1[:], in0=ang[:], scalar1=-math.pi, scalar2=-TWO_PI, op0=ALU.is_lt, op1=ALU.mult), reads=[Ba], writes=[Bt])
        k.op('dve', lambda e: e.tensor_tensor(out=ang[:], in0=ang[:], in1=t1[:], op=ALU.subtract), reads=[Ba, Bt], writes=[Ba])
        k.op('act', lambda e: e.activation(out=t1[:], in_=ang[:], func=AF.Sin), reads=[Ba], writes=[Bt])
        if scol is not None:
            k.op('dve', lambda e: e.tensor_scalar(out=t1[:], in0=t1[:], scalar1=C.cst[0:64, scol:scol + 1], scalar2=None, op0=ALU.mult),
                 reads=[Bt, C.Bconst], writes=[Bt])
        k.dma('sp', tabs_d[ti_], t1[:], reads=[Bt], writes=[Btabs])


def build(dbg=False, stop_after=None):
    nc = bass.Bass("TRN2", target_bir_lowering=False)

    def din(name, shape, dt=F32):
        return nc.dram_tensor(name, list(shape), dt, kind="ExternalInput").ap()

    def dscr(name, shape, dt=F32, out=False):
        return nc.dram_tensor(name, list(shape), dt, kind="ExternalOutput" if (out or dbg) else "Internal").ap()

    xT = din("xT", [D, S])
    cvec = din("cvec", [128, 16])
    posd = din("pos", [1, S], I32)
    cst_d = din("cst", [128, 48])
    ident_d = din("ident", [128, 128])
    ada_w = [din(f"ada_w{i}", [24, 128, 4 * 16 * 128]) for i in range(2)]
    ada_b = [din(f"ada_b{i}", [128, 96]) for i in range(2)]
    w1d = [din(f"w1_{i}", [64, 128, 16 * 128]) for i in range(2)]
    w2d = [din(f"w2_{i}", [16, 128, 64 * 128]) for i in range(2)]
    fing_d = din("fing", [128, 16])
    w_in_a = din("w_in_a", [6, 128, 16 * 128])
    w_in_b = din("w_in_b", [4, 128, 16 * 64])
    w_in_i = din("w_in_i", [128, 16 * 16])
    dsa_vec = din("dsa_vec", [128, 16])
    w_uq_n = din("w_uq_n", [16, 128, 4 * 128])
    w_uq_r = din("w_uq_r", [32, 128, 4 * 64])
    w_qi = din("w_qi", [32, 128, 4 * 64])
    w_ukT = din("w_ukT", [128, 16 * 256])
    w_uv = din("w_uv", [128, 2 * 2048])
    w_o = din("w_o", [16, 128, 16 * 128])

    Wd = {}
    for nm, shp in (("w_w1", [1, 128, 16 * 96]), ("w_a1", [1, 128, 16 * 96]), ("w_g1", [2, 128, 2048]), ("w_v", [16, 128, 2048]),
                    ("w_r", [16, 128, 2048]), ("w_k", [16, 128, 2048]), ("w_o", [16, 128, 2048]), ("w_w2", [96, 2048]), ("w_a2", [96, 2048]),
                    ("w_g2", [128, 4096]), ("rvec", [128, 13 * 16]), ("rconst", [128, 1024])):
        Wd[nm] = din("b_" + nm, shp)
    rk_d = dscr("rk_d", [7, D, S], BF16)
    x1_d = dscr("x1_d", [D, S])
    x2_d = dscr("x2_d", [D, S])
    x3_d = dscr("x3_d", [D, S])
    x4_d = dscr("x4_d", [D, S])
    outT = dscr("outT", [D, S], out=True)
    tabs_d = dscr("tabs_d", [4, 64, S])
    qlat_d = dscr("qlat_d", [16, 128, 2 * 16 * 128], BF16)
    qr_d = dscr("qr_d", [16, 64, 16 * 128], BF16)
    qi_d = dscr("qi_d", [16, 64, 16 * 128], BF16)

    es0 = ExitStack()
    with es0:
        k = KB(nc, es0)
        C = Ctx()
        C.Bin = Buf("inputs", dram=True)
        C.Bw = Buf("weights", dram=True)
        C.Bconst = Buf("const")
        C.Bmod = Buf("mod")
        C.cst = sbt(k, es0, "cst", [128, 48], F32)
        C.eps = sbt(k, es0, "eps", [128, 1], F32)
        C.ident_f = sbt(k, es0, "ident_f", [128, 128], F32)
        C.ident_bf = sbt(k, es0, "ident_bf", [128, 128], BF16)
        C.ones_bf = sbt(k, es0, "ones_bf", [128, 128], BF16)
        C.ones_f = sbt(k, es0, "ones_f", [128, 128], F32)
        C.fing = sbt(k, es0, "fing", [128, 16], F32)
        C.dsav = sbt(k, es0, "dsav", [128, 16], F32)
        mod = [sbt(k, es0, f"mod{i}", [128, 96], F32) for i in range(2)]
        k.dma('sp', C.cst[:], cst_d, reads=[C.Bin], writes=[C.Bconst])
        k.dma('sp', C.ident_f[:], ident_d, reads=[C.Bin], writes=[C.Bconst])
        k.dma('sp', C.fing[:], fing_d, reads=[C.Bin], writes=[C.Bconst])
        k.dma('sp', C.dsav[:], dsa_vec, reads=[C.Bin], writes=[C.Bconst])
        k.op('dve', lambda e: e.memset(C.eps[:], EPS), writes=[C.Bconst])
        k.op('dve', lambda e: e.memset(C.ones_bf[:], 1.0), writes=[C.Bconst])
        k.op('dve', lambda e: e.memset(C.ones_f[:], 1.0), writes=[C.Bconst])
        k.op('dve', lambda e: e.tensor_copy(out=C.ident_bf[:], in_=C.ident_f[:]), reads=[C.Bconst], writes=[C.Bconst])

        cact = sbt(k, es0, "p0_cact", [128, 16], BF16)
        Btabs = Buf("tabs_d", dram=True)
        with ExitStack() as es:
            cv = sbt(k, es, "p0_cv", [128, 16], F32)
            adab = sbt(k, es, "p0_adab", [128, 96], F32)
            Bcv, Bab = Buf("p0_cv"), Buf("p0_adab")
            k.dma('sp', cv[:], cvec, reads=[C.Bin], writes=[Bcv])
            k.op('act', lambda e: e.activation(out=cact[:], in_=cv[:], func=AF.Silu), reads=[Bcv], writes=[Bcv])
            wpool = Pool(k, es, "p0_w", 2, [128, 4 * 16 * 128], BF16)
            psm = es.enter_context(nc.psum_tensor("t_p0_ps", [128, 96], F32))
            Bpsm = Buf("p0_ps")
            for i in range(1):
                k.dma('sp', adab[:], ada_b[i], reads=[C.Bin], writes=[Bab])
                for G in range(24):
                    wt, Bw = wpool.next()
                    k.dma('pool', wt[:], ada_w[i][G], reads=[C.Bw], writes=[Bw])
                    for j in range(4):
                        oc = G * 4 + j
                        for kc in range(16):
                            k.op('pe', lambda e: e.matmul(psm[:, oc:oc + 1], lhsT=wt[:, (j * 16 + kc) * 128:(j * 16 + kc + 1) * 128],
                                                          rhs=cact[:, kc:kc + 1], start=(kc == 0), stop=(kc == 15)),
                                 reads=[Bw, Bcv], writes=[Bpsm])
                k.op('dve', lambda e: e.tensor_tensor(out=mod[i][:], in0=psm[:], in1=adab[:], op=ALU.add), reads=[Bpsm, Bab], writes=[C.Bmod])
                for c0 in (16, 64):
                    k.op('dve', lambda e: e.tensor_scalar(out=mod[i][:, c0:c0 + 16], in0=mod[i][:, c0:c0 + 16], scalar1=1.0, scalar2=None, op0=ALU.add),
                         reads=[C.Bmod], writes=[C.Bmod])
            rope_tables(k, C, es, posd, tabs_d, Btabs)
        k.barrier()

        Bx = [Buf("xT", dram=True), Buf("x1_d", dram=True), Buf("x2_d", dram=True), Buf("x3_d", dram=True), Buf("outT", dram=True), Buf("x4_d", dram=True)]
        C.ada1 = (ada_w[1], ada_b[1], cact, Bcv, mod[1])
        dsa_phase(k, C, nc, xT, Bx[0], x1_d, Bx[1], mod[0], tabs_d, Btabs, w_in_a, w_in_b, w_in_i, w_uq_n, w_uq_r, w_qi, w_ukT, w_uv, w_o,
                  qlat_d, qr_d, qi_d)
        if stop_after == 'dsa':
            k.barrier()
            return nc
        mlp_phase(k, C, x1_d, Bx[1], x2_d, Bx[2], w1d[0], w2d[0], mod[0])
        if stop_after == 'mlp0':
            return nc
        rwkv_phase(k, C, nc, x2_d, Bx[2], x3_d, Bx[3], mod[1], Wd, rk_d)
        if stop_after == 'rwkv':
            return nc
        mlp_phase(k, C, x3_d, Bx[3], x4_d, Bx[5], w1d[1], w2d[1], mod[1], final=(outT, Bx[4], C.fing))
    return nc


def dsa_phase(k, C, nc, xT, Bxin, x1_d, Bxout, mod, tabs_d, Btabs, w_in_a, w_in_b, w_in_i, w_uq_n, w_uq_r, w_qi, w_ukT, w_uv, w_o,
              qlat_d, qr_d, qi_d):
    Bql, Bqr, Bqi = Buf("qlat_d", dram=True), Buf("qr_d", dram=True), Buf("qi_d", dram=True)
    xin_v = xT.rearrange("(c p) t -> p c t", p=128)
    with ExitStack() as esA:
        ckvT = sbt(k, esA, "a_ckvT", [128, 2, S], BF16)
        krT = sbt(k, esA, "a_krT", [64, S], BF16)
        kiT = sbt(k, esA, "a_kiT", [64, S], BF16)
        ckva = sbt(k, esA, "a_ckva", [128, 16, 257], BF16)
        wabs = sbt(k, esA, "a_wabs", [128, 16, 16], F32)
        wsgn = sbt(k, esA, "a_wsgn", [128, 16, 16], F32)
        Bkeys = Buf("a_keys")
        k.op('dve', lambda e: e.memset(ckva[:, :, 256:257], 1.0), writes=[Bkeys])
        with ExitStack() as es:
            xt = sbt(k, es, "a_xt", [128, 16, TT], F32)
            Bxt = [Buf(f"a_xt{i}") for i in range(16)]
            hT = sbt(k, es, "a_hT", [128, 16, TT], BF16)
            BhT = Buf("a_hT")
            tab = sbt(k, es, "a_tab", [64, 4, TT], F32)
            Btab = Buf("a_tab")
            cqraw = sbt(k, es, "a_cqraw", [128, 6, TT], F32)
            Braw = [Buf(f"a_raw{i}") for i in range(6)]
            cqT = sbt(k, es, "a_cqT", [128, 4, TT], BF16)
            BcqT = Buf("a_cqT")
            rawb = sbt(k, es, "a_rawb", [64, 4, TT], F32)
            Brawb = [Buf(f"a_rawb{i}") for i in range(4)]
            xc = sbt(k, es, "a_xc", [64, 2, TT], F32)
            Bxc = Buf("a_xc")
            wini = sbt(k, es, "a_wini", [128, 16, 16], BF16)
            wukT = sbt(k, es, "a_wukT", [128, 16, 256], BF16)
            Bwr = Buf("a_wres")
            k.dma('pool', wini[:], w_in_i.rearrange("p (a b) -> p a b", a=16), reads=[C.Bw], writes=[Bwr])
            k.dma('pool', wukT[:], w_ukT.rearrange("p (a b) -> p a b", a=16), reads=[C.Bw], writes=[Bwr])
            wpA = Pool(k, es, "a_wA", 3, [128, 16 * 128], BF16)
            wpB = Pool(k, es, "a_wB", 2, [128, 16 * 64], BF16)
            wpQ = Pool(k, es, "a_wQ", 4, [128, 4 * 128], BF16)
            qnp = Pool(k, es, "a_qn", 2, [128, TT], BF16)
            rawq = Pool(k, es, "a_rawq", 4, [64, TT], F32)
            stg = Pool(k, es, "a_stg", 4, [128, TT], BF16)
            t64 = Pool(k, es, "a_t64", 4, [64, TT], F32)
            C.sqpool = Pool(k, es, "a_sq", 3, [128, TT], BF16)
            C.rspool = Pool(k, es, "a_rs", 2, [128, TT], F32)
            C.tmppool = Pool(k, es, "a_tmp", 2, [128, TT], F32)
            C.pstat = Pool(k, es, "a_pst", 1, [128, TT], F32, psum=True)
            pspool = Pool(k, es, "a_ps", 5, [128, TT], F32, psum=True)
            pTb = Pool(k, es, "a_pT", 1, [128, 1024], BF16, psum=True)

            def subpool(items):
                p_ = Pool.__new__(Pool)
                p_.items = list(items)
                p_.i = 0
                return p_
            HR = [(subpool(qnp.items[u:u + 1]), subpool(wpQ.items[2 * u:2 * u + 2]), subpool(stg.items[2 * u:2 * u + 2]),
                   subpool(rawq.items[2 * u:2 * u + 2]), subpool(t64.items[2 * u:2 * u + 2]),
                   subpool(pspool.items[0:3] if u == 0 else pspool.items[3:5] + C.pstat.items[0:1])) for u in range(2)]
            adaw1, adab1_d, cact, Bcv, mod1 = C.ada1
            awp = Pool(k, es, "a_adaw", 2, [128, 4 * 16 * 128], BF16)
            adab1 = sbt(k, es, "a_adab1", [128, 96], F32)
            Bab1 = Buf("a_adab1")
            psm1 = es.enter_context(nc.psum_tensor(f"t{_uid()}_a_psm1", [128, 96], F32))
            Bpsm1 = Buf("a_psm1")
            k.dma('sp', adab1[:], adab1_d, reads=[C.Bin], writes=[Bab1])
            ada_ld = {}

            def ada_load(G):
                if G < 24:
                    wt, Bw = awp.next()
                    k.dma('pool', wt[:], adaw1[G], reads=[C.Bw], writes=[Bw])
                    ada_ld[G] = (wt, Bw)

            def ada_step(G):
                if G >= 24:
                    return
                ada_load(G + 1)
                wt, Bw = ada_ld.pop(G)
                for j in range(4):
                    oc = G * 4 + j
                    for kc in range(16):
                        k.op('pe', lambda e: e.matmul(psm1[:, oc:oc + 1], lhsT=wt[:, (j * 16 + kc) * 128:(j * 16 + kc + 1) * 128],
                                                      rhs=cact[:, kc:kc + 1], start=(kc == 0), stop=(kc == 15)),
                             reads=[Bw, Bcv], writes=[Bpsm1])
                if G == 23:
                    k.op('dve', lambda e: e.tensor_tensor(out=mod1[:], in0=psm1[:], in1=adab1[:], op=ALU.add), reads=[Bpsm1, Bab1], writes=[C.Bmod])
                    for c0 in (16, 64):
                        k.op('dve', lambda e: e.tensor_scalar(out=mod1[:, c0:c0 + 16], in0=mod1[:, c0:c0 + 16], scalar1=1.0, scalar2=None, op0=ALU.add),
                             reads=[C.Bmod], writes=[C.Bmod])
            ada_load(0)
            ada_n = [0]

            def rope_combine(a_ap, b_ap, Bab, ct, st, out_ap, Bo_, tp=None):
                tp = tp or t64
                ta, Bta = tp.next()
                tb, Btb = tp.next()
                k.op('dve', lambda e: e.tensor_tensor(out=ta[:], in0=a_ap, in1=tab[:, ct, :], op=ALU.mult), reads=Bab + [Btab], writes=[Bta])
                k.op('dve', lambda e: e.tensor_tensor(out=tb[:], in0=b_ap, in1=tab[:, st, :], op=ALU.mult), reads=Bab + [Btab], writes=[Btb])
                k.op('dve', lambda e: e.tensor_tensor(out=out_ap, in0=ta[:], in1=tb[:], op=ALU.add), reads=[Bta, Btb], writes=[Bo_])

            for tt in range(NT):
                ts = slice(tt * TT, (tt + 1) * TT)
                for dc in range(16):
                    k.dma('sp', xt[:, dc, :], xin_v[:, dc, ts], reads=[Bxin], writes=[Bxt[dc]])
                for j in range(4):
                    k.dma('sp', tab[:, j, :], tabs_d[j][:, ts], reads=[Btabs], writes=[Btab])
                modnorm(k, C, xt, Bxt, TT, hT, BhT, mod, 16, 0)

                def epiA(g, pt, Bp):
                    k.op('act', lambda e: e.copy(out=cqraw[:, g, :], in_=pt[:]), reads=[Bp], writes=[Braw[g]])
                linear(k, w_in_a, C.Bw, 6, 16, 128, lambda kc: hT[:, kc, :], [BhT], TT, wpA, pspool, epiA)
                rq, Brq = rms_stats(k, C, lambda c: cqraw[:, c, :], Braw[0:4], 4, TT, 128, 1.0 / 512, C.ones_bf)
                for g in range(4):
                    k.op('dve', lambda e: e.scalar_tensor_tensor(out=cqT[:, g, :], in0=cqraw[:, g, :], scalar=C.dsav[:, g:g + 1], in1=rq[:],
                                                                 op0=ALU.mult, op1=ALU.mult), reads=[Braw[g], Brq, C.Bconst], writes=[BcqT])
                rkv, Brkv = rms_stats(k, C, lambda c: cqraw[:, 4 + c, :], Braw[4:6], 2, TT, 128, 1.0 / 256, C.ones_bf)
                for g in range(2):
                    k.op('dve', lambda e: e.scalar_tensor_tensor(out=ckvT[:, g, ts], in0=cqraw[:, 4 + g, :], scalar=C.dsav[:, 4 + g:5 + g], in1=rkv[:],
                                                                 op0=ALU.mult, op1=ALU.mult), reads=[Braw[4 + g], Brkv, C.Bconst], writes=[Bkeys])
                pT, BpT = pTb.next()
                for kb in range(4):
                    for rc in range(2):
                        j = kb * 2 + rc
                        k.op('pe', lambda e: e.transpose(out=pT[:, j * 128:(j + 1) * 128], in_=ckvT[:, rc, tt * TT + kb * 128: tt * TT + (kb + 1) * 128],
                                                         identity=C.ident_bf[:]), reads=[Bkeys, C.Bconst], writes=[BpT])
                k.op('act', lambda e: e.copy(out=ckva[:, tt * 4:(tt + 1) * 4, 0:256], in_=pT[:].rearrange("p (a b) -> p a b", a=4)),
                     reads=[BpT], writes=[Bkeys])

                def epiB(g, pt, Bp):
                    k.op('act', lambda e: e.copy(out=rawb[:, g, :], in_=pt[0:64, :]), reads=[Bp], writes=[Brawb[g]])
                linear(k, w_in_b, C.Bw, 4, 16, 64, lambda kc: hT[:, kc, :], [BhT], TT, wpB, pspool, epiB)
                rope_combine(rawb[:, 0, :], rawb[:, 1, :], [Brawb[0], Brawb[1]], 0, 1, krT[:, ts], Bkeys)
                pm, Bpm = pspool.next()
                k.op('pe', lambda e: e.matmul(pm[0:64, :], lhsT=C.ones_f[0:64, 0:64], rhs=rawb[:, 2, :], start=True, stop=True),
                     reads=[Brawb[2], C.Bconst], writes=[Bpm])
                for j in range(2):
                    k.op('dve', lambda e: e.scalar_tensor_tensor(out=xc[:, j, :], in0=pm[0:64, :], scalar=-1.0 / 64, in1=rawb[:, 2 + j, :],
                                                                 op0=ALU.mult, op1=ALU.add), reads=[Bpm, Brawb[2 + j]], writes=[Bxc])
                rl, Brl = rms_stats(k, C, lambda c: xc[:, 0, :], [Bxc], 1, TT, 64, 1.0 / 64, C.ones_bf)
                lns = []
                for j in range(2):
                    tq, Btq = t64.next()
                    k.op('dve', lambda e: e.tensor_tensor(out=tq[:], in0=xc[:, j, :], in1=rl[0:64, :], op=ALU.mult), reads=[Bxc, Brl], writes=[Btq])
                    k.op('act', lambda e: e.activation(out=tq[:], in_=tq[:], func=AF.Identity, scale=C.dsav[0:64, 6 + 2 * j:7 + 2 * j],
                                                       bias=C.dsav[0:64, 7 + 2 * j:8 + 2 * j]), reads=[Btq, C.Bconst], writes=[Btq])
                    lns.append((tq, Btq))
                rope_combine(lns[0][0][:], lns[1][0][:], [lns[0][1], lns[1][1]], 2, 3, kiT[:, ts], Bkeys)

                for tb in range(4):
                    pw, Bpw = pspool.next()
                    for kc in range(16):
                        k.op('pe', lambda e: e.matmul(pw[:, 0:16], lhsT=hT[:, kc, tb * 128:(tb + 1) * 128], rhs=wini[:, kc, :],
                                                      start=(kc == 0), stop=(kc == 15)), reads=[BhT, Bwr], writes=[Bpw])
                    k.op('act', lambda e: e.activation(out=wabs[:, tt * 4 + tb, :], in_=pw[:, 0:16], func=AF.Abs, scale=1.0 / 32),
                         reads=[Bpw], writes=[Bkeys])
                    k.op('act', lambda e: e.activation(out=wsgn[:, tt * 4 + tb, :], in_=pw[:, 0:16], func=AF.Sign), reads=[Bpw], writes=[Bkeys])

                def head(h, R):
                    qnp_, wpQ_, stg_, rawq_, t64_, psp_ = R
                    qn, Bqn = qnp_.next()

                    def epiQ(g, pt, Bp):
                        k.op('act', lambda e: e.copy(out=qn[:], in_=pt[:]), reads=[Bp], writes=[Bqn])
                    linear(k, w_uq_n, C.Bw, 1, 4, 128, lambda kc: cqT[:, kc, :], [BcqT], TT, wpQ_, psp_, epiQ, g0=h)
                    for rc in range(2):
                        pl, Bpl = psp_.next()
                        k.op('pe', lambda e: e.matmul(pl[:], lhsT=wukT[:, h, rc * 128:(rc + 1) * 128], rhs=qn[:], start=True, stop=True),
                             reads=[Bwr, Bqn], writes=[Bpl])
                        sg, Bsg = stg_.next()
                        k.op('dve', lambda e: e.tensor_copy(out=sg[:], in_=pl[:]), reads=[Bpl], writes=[Bsg])
                        c0 = (rc * 16 + h) * 128
                        k.dma('sp', qlat_d[tt * 4:(tt + 1) * 4, :, c0:c0 + 128].rearrange("b r q -> r b q"),
                              sg[:].rearrange("p (b q) -> p b q", b=4), reads=[Bsg], writes=[Bql])
                    for (wd_, tc_, td_, dst, Bdst) in ((w_uq_r, 0, 1, qr_d, Bqr), (w_qi, 2, 3, qi_d, Bqi)):
                        rr = []

                        def epiR(g, pt, Bp):
                            r_, Br_ = rawq_.next()
                            k.op('act', lambda e: e.copy(out=r_[:], in_=pt[0:64, :]), reads=[Bp], writes=[Br_])
                            rr.append((r_, Br_))
                        linear(k, wd_, C.Bw, 2, 4, 64, lambda kc: cqT[:, kc, :], [BcqT], TT, wpQ_, psp_, epiR, g0=2 * h)
                        sg, Bsg = stg_.next()
                        rope_combine(rr[0][0][:], rr[1][0][:], [rr[0][1], rr[1][1]], tc_, td_, sg[0:64, :], Bsg, t64_)
                        k.dma('sp', dst[tt * 4:(tt + 1) * 4, :, h * 128:(h + 1) * 128].rearrange("b r q -> r b q"),
                              sg[0:64, :].rearrange("p (b q) -> p b q", b=4), reads=[Bsg], writes=[Bdst])

                for h0 in range(0, 16, 2):
                    chains = []
                    for u in range(2):
                        ada_step(ada_n[0])
                        ada_n[0] += 1
                        k.rec = []
                        head(h0 + u, HR[u])
                        chains.append(k.rec)
                        k.rec = None
                    k.replay(chains)
        k.barrier()
        with ExitStack() as es:
            wuv = sbt(k, es, "b_wuv", [128, 2, 2048], BF16)
            Bwuv = Buf("b_wuv")
            k.dma('pool', wuv[:], w_uv.rearrange("p (a b) -> p a b", a=2), reads=[C.Bw], writes=[Bwuv])
            qlp = Pool(k, es, "b_ql", 2, [128, 2, 2048], BF16)
            qrp = Pool(k, es, "b_qr", 2, [64, 2048], BF16)
            qip = Pool(k, es, "b_qi", 2, [64, 16, 128], BF16)
            acc = sbt(k, es, "b_acc", [128, S], F32)
            Bacc = [Buf(f"b_acc{i}") for i in range(4)]
            junk = sbt(k, es, "b_junk", [128, S], BF16)
            Bjunk = Buf("b_junk")
            mask = sbt(k, es, "b_mask", [128, S], BF16)
            Bmask = Buf("b_mask")
            maskT = sbt(k, es, "b_maskT", [128, 16, 128], BF16)
            BmaskT = Buf("b_maskT")
            relup = Pool(k, es, "b_relu", 3, [128, 512], F32)
            PTp = Pool(k, es, "b_PT", 3, [128, 4, 128], BF16)
            olatp = Pool(k, es, "b_olat", 2, [128, 4, 256], BF16)
            olatTp = Pool(k, es, "b_olatT", 2, [128, 8, 128], BF16)
            oT = sbt(k, es, "b_oT", [128, 16, TT], BF16)
            BoT = Buf("b_oT")
            sm = sbt(k, es, "b_sm", [128, 8], F32)
            Bsm = Buf("b_sm")
            Wt = sbt(k, es, "b_W", [128, NIT], F32)
            mid = sbt(k, es, "b_mid", [128, NIT], F32)
            cnt = sbt(k, es, "b_cnt", [128, NIT], F32)
            gg = sbt(k, es, "b_g", [128, NIT], F32)
            BW, Bmid, Bcnt, Bg = Buf("b_W"), Buf("b_mid"), Buf("b_cnt"), Buf("b_g")
            rsp = Pool(k, es, "b_rs", 2, [128, 4], F32)
            wop = Pool(k, es, "b_wo", 3, [128, 16 * 128], BF16)
            xcp = Pool(k, es, "b_xc", 3, [128, TT], F32)
            ygp = Pool(k, es, "b_yg", 2, [128, TT], F32)
            pA = Pool(k, es, "b_pA", 2, [128, 512], F32, psum=True)
            pPV = Pool(k, es, "b_pPV", 4, [128, 512], F32, psum=True)
            pTb = Pool(k, es, "b_pT", 1, [128, 1024], BF16, psum=True)
            pO = Pool(k, es, "b_pO", 1, [128, 512], F32, psum=True)
            Q = {}

            def idx_stage(qt):
                nk = 128 * (qt + 1)
                nblk = (nk + 511) // 512
                qlatT, Bq1 = qlp.next()
                qrT, Bq2 = qrp.next()
                qiT, Bq3 = qip.next()
                Q[qt] = (qlatT, Bq1, qrT, Bq2)
                k.dma('sp', qiT[:], qi_d[qt].rearrange("p (a b) -> p a b", a=16), reads=[Bqi], writes=[Bq3])
                k.dma('sp', qlatT[:], qlat_d[qt].rearrange("p (a b) -> p a b", a=2), reads=[Bql], writes=[Bq1])
                k.dma('sp', qrT[:], qr_d[qt], reads=[Bqr], writes=[Bq2])
                for h in range(16):
                    for kb in range(nblk):
                        w = min(512, nk - kb * 512)
                        cs = slice(kb * 512, kb * 512 + w)
                        pt, Bp = pA.next()
                        k.op('pe', lambda e: e.matmul(pt[:, 0:w], lhsT=qiT[:, h, :], rhs=kiT[:, cs], start=True, stop=True),
                             reads=[Bq3, Bkeys], writes=[Bp])
                        r, Br = relup.next()
                        k.op('act', lambda e: e.activation(out=r[:, 0:w], in_=pt[:, 0:w], func=AF.Relu, scale=wabs[:, qt, h:h + 1]),
                             reads=[Bp, Bkeys], writes=[Br])
                        if h == 0:
                            k.op('dve', lambda e: e.tensor_scalar(out=acc[:, cs], in0=r[:, 0:w], scalar1=wsgn[:, qt, 0:1], scalar2=None, op0=ALU.mult),
                                 reads=[Br, Bkeys], writes=[Bacc[kb]])
                        else:
                            k.op('dve', lambda e: e.scalar_tensor_tensor(out=acc[:, cs], in0=r[:, 0:w], scalar=wsgn[:, qt, h:h + 1], in1=acc[:, cs],
                                                                         op0=ALU.mult, op1=ALU.add), reads=[Br, Bkeys, Bacc[kb]], writes=[Bacc[kb]])

            def bisect_stage(qt):
                nk = 128 * (qt + 1)
                nblk = (nk + 511) // 512
                Ba = Bacc[0:nblk]
                if qt >= 2:
                    k.op('dve', lambda e: e.tensor_reduce(out=sm[:, 0:1], in_=acc[:, 0:nk], axis=AX.X, op=ALU.max), reads=Ba, writes=[Bsm])
                    k.op('dve', lambda e: e.tensor_reduce(out=sm[:, 1:2], in_=acc[:, 0:nk], axis=AX.X, op=ALU.min), reads=Ba, writes=[Bsm])
                k.op('dve', lambda e: e.memset(acc[0:64, nk - 64:nk], NEG), reads=[Bsm], writes=[Bacc[nblk - 1]])
                if qt >= 2:
                    k.op('dve', lambda e: e.tensor_tensor(out=sm[:, 2:3], in0=sm[:, 0:1], in1=sm[:, 1:2], op=ALU.subtract), reads=[Bsm], writes=[Bsm])
                    k.op('dve', lambda e: e.tensor_scalar(out=Wt[:], in0=C.cst[:, 8:8 + NIT], scalar1=sm[:, 2:3], scalar2=None, op0=ALU.mult),
                         reads=[Bsm, C.Bconst], writes=[BW])
                    k.op('dve', lambda e: e.tensor_tensor(out=mid[:, 0:1], in0=sm[:, 1:2], in1=Wt[:, 0:1], op=ALU.add), reads=[Bsm, BW], writes=[Bmid])
                    for j in range(NIT):
                        k.op('dve', lambda e: e.tensor_scalar(out=junk[:, 0:nk], in0=acc[:, 0:nk], scalar1=mid[:, j:j + 1], scalar2=None,
                                                              op0=ALU.is_ge, op1=ALU.add, accum_out=cnt[:, j:j + 1]),
                             reads=Ba + [Bmid], writes=[Bjunk, Bcnt])
                        k.op('dve', lambda e: e.scalar_tensor_tensor(out=gg[:, j:j + 1], in0=cnt[:, j:j + 1], scalar=TOPK - 0.5, in1=Wt[:, j:j + 1],
                                                                     op0=ALU.is_ge, op1=ALU.mult), reads=[Bcnt, BW], writes=[Bg])
                        if j < NIT - 1:
                            k.op('dve', lambda e: e.scalar_tensor_tensor(out=mid[:, j + 1:j + 2], in0=gg[:, j:j + 1], scalar=Wt[:, j + 1:j + 2],
                                                                         in1=mid[:, j:j + 1], op0=ALU.subtract, op1=ALU.add),
                                 reads=[Bg, BW, Bmid], writes=[Bmid])
                        k.op('dve', lambda e: e.tensor_tensor(out=sm[:, 1:2], in0=sm[:, 1:2], in1=gg[:, j:j + 1], op=ALU.add), reads=[Bsm, Bg], writes=[Bsm])
                    k.op('dve', lambda e: e.tensor_scalar(out=mask[:, 0:nk], in0=acc[:, 0:nk], scalar1=sm[:, 1:2], scalar2=None, op0=ALU.is_ge),
                         reads=Ba + [Bsm], writes=[Bmask])
                else:
                    k.op('dve', lambda e: e.tensor_scalar(out=mask[:, 0:nk], in0=acc[:, 0:nk], scalar1=-1.0e29, scalar2=None, op0=ALU.is_ge),
                         reads=Ba, writes=[Bmask])

            def transp_stage(qt):
                nkc = qt + 1
                for c0 in range(0, nkc, 8):
                    n = min(8, nkc - c0)
                    pT, BpT = pTb.next()
                    for j in range(n):
                        k.op('pe', lambda e: e.transpose(out=pT[:, j * 128:(j + 1) * 128], in_=mask[:, (c0 + j) * 128:(c0 + j + 1) * 128],
                                                         identity=C.ident_bf[:]), reads=[Bmask, C.Bconst], writes=[BpT])
                    k.op('act', lambda e: e.copy(out=maskT[:, c0:c0 + n, :], in_=pT[:, 0:n * 128].rearrange("p (a b) -> p a b", a=n)),
                         reads=[BpT], writes=[BmaskT])

            def attn_stage(qt):
                nkc = qt + 1
                qlatT, Bq1, qrT, Bq2 = Q.pop(qt)
                qs = (qt % 4) * 128
                for hg in range(4):
                    pv = [pPV.next() for _ in range(4)]

                    def scores(kc):
                        ks = slice(kc * 128, (kc + 1) * 128)
                        pt, Bp = pA.next()
                        k.op('pe', lambda e: e.matmul(pt[:], lhsT=ckvT[:, 0, ks], rhs=qlatT[:, 0, hg * 512:(hg + 1) * 512], start=True, stop=False),
                             reads=[Bkeys, Bq1], writes=[Bp])
                        k.op('pe', lambda e: e.matmul(pt[:], lhsT=ckvT[:, 1, ks], rhs=qlatT[:, 1, hg * 512:(hg + 1) * 512], start=False, stop=False),
                             reads=[Bkeys, Bq1], writes=[Bp])
                        k.op('pe', lambda e: e.matmul(pt[:], lhsT=krT[:, ks], rhs=qrT[:, hg * 512:(hg + 1) * 512], start=False, stop=True),
                             reads=[Bkeys, Bq2], writes=[Bp])
                        P, BP = PTp.next()
                        k.op('act', lambda e: e.activation(out=P[:], in_=pt[:].rearrange("p (a b) -> p a b", a=4), func=AF.Exp, scale=A_SCALE),
                             reads=[Bp], writes=[BP])
                        k.op('pool', lambda e: e.tensor_tensor(out=P[:], in0=P[:], in1=maskT[:, kc:kc + 1, :].to_broadcast([128, 4, 128]), op=ALU.mult),
                             reads=[BP, BmaskT], writes=[BP])
                        return P, BP
                    nxt = scores(0)
                    for kc in range(nkc):
                        P, BP = nxt
                        if kc + 1 < nkc:
                            nxt = scores(kc + 1)
                        for j in range(4):
                            k.op('pe', lambda e: e.matmul(pv[j][0][:, 0:257], lhsT=P[:, j, :], rhs=ckva[:, kc, :], start=(kc == 0), stop=(kc == nkc - 1)),
                                 reads=[BP, Bkeys], writes=[pv[j][1]])
                    rs, Brs = rsp.next()
                    ol, Bol = olatp.next()
                    for j in range(4):
                        k.op('act', lambda e: e.activation(out=rs[:, j:j + 1], in_=pv[j][0][:, 256:257], func=AF.Ln), reads=[pv[j][1]], writes=[Brs])
                    k.op('act', lambda e: e.activation(out=rs[:, 0:4], in_=rs[:, 0:4], func=AF.Exp, scale=-1.0), reads=[Brs], writes=[Brs])
                    for j in range(4):
                        k.op('act', lambda e: e.activation(out=ol[:, j, :], in_=pv[j][0][:, 0:256], func=AF.Identity, scale=rs[:, j:j + 1]),
                             reads=[pv[j][1], Brs], writes=[Bol])
                    pT, BpT = pTb.next()
                    for j in range(4):
                        for rc in range(2):
                            jj = j * 2 + rc
                            k.op('pe', lambda e: e.transpose(out=pT[:, jj * 128:(jj + 1) * 128], in_=ol[:, j, rc * 128:(rc + 1) * 128],
                                                             identity=C.ident_bf[:]), reads=[Bol, C.Bconst], writes=[BpT])
                    olT, BolT = olatTp.next()
                    k.op('act', lambda e: e.copy(out=olT[:], in_=pT[:].rearrange("p (a b) -> p a b", a=8)), reads=[BpT], writes=[BolT])
                    po, Bpo = pO.next()
                    for j in range(4):
                        for rc in range(2):
                            hh = hg * 4 + j
                            k.op('pe', lambda e: e.matmul(po[:, j * 128:(j + 1) * 128], lhsT=wuv[:, rc, hh * 128:(hh + 1) * 128], rhs=olT[:, j * 2 + rc, :],
                                                          start=(rc == 0), stop=(rc == 1)), reads=[Bwuv, BolT], writes=[Bpo])
                    k.op('act', lambda e: e.copy(out=oT[:, hg * 4:(hg + 1) * 4, qs:qs + 128], in_=po[:].rearrange("p (a b) -> p a b", a=4)),
                         reads=[Bpo], writes=[BoT])
                if qt % 4 == 3:
                    tsl = slice((qt // 4) * TT, (qt // 4 + 1) * TT)

                    def epiO(g, pt, Bp):
                        xc_, Bxc_ = xcp.next()
                        k.dma('sp', xc_[:], xin_v[:, g, tsl], reads=[Bxin], writes=[Bxc_])
                        yg, Byg = ygp.next()
                        k.op('act', lambda e: e.activation(out=yg[:], in_=pt[:], func=AF.Identity, scale=mod[:, 32 + g:33 + g]),
                             reads=[Bp, C.Bmod], writes=[Byg])
                        k.op('pool', lambda e: e.tensor_tensor(out=xc_[:], in0=yg[:], in1=xc_[:], op=ALU.add), reads=[Byg, Bxc_], writes=[Bxc_])
                        k.dma('sp', x1_d.rearrange("(c p) t -> p c t", p=128)[:, g, tsl], xc_[:], reads=[Bxc_], writes=[Bxout])
                    linear(k, w_o, C.Bw, 16, 16, 128, lambda kc: oT[:, kc, :], [BoT], TT, wop, pA, epiO)

            idx_stage(0)
            bisect_stage(0)
            transp_stage(0)
            for qt in range(16):
                if qt + 1 < 16:
                    idx_stage(qt + 1)
                    bisect_stage(qt + 1)
                attn_stage(qt)
                if qt + 1 < 16:
                    transp_stage(qt + 1)
    k.barrier()


def _wl(w, M=128):
    K, N = w.shape
    return np.ascontiguousarray(w.reshape(K // 128, 128, N // M, M).transpose(2, 1, 0, 3)).reshape(N // M, 128, (K // 128) * M)


def _cols(w, cols, M):
    return _wl(np.ascontiguousarray(w[:, cols]), M)


def _pc(v):
    return np.ascontiguousarray(v.reshape(-1, 128).T)


def shared_inputs(inp):
    f = np.float32
    sh = {}
    for i in range(2):
        aw = inp["ada_w"][i]
        g = _wl(aw)
        sh[f"ada_w{i}"] = np.ascontiguousarray(g.reshape(24, 4, 128, 2048).transpose(0, 2, 1, 3)).reshape(24, 128, 4 * 2048)
        sh[f"ada_b{i}"] = _pc(inp["ada_b"][i])
        sh[f"w1_{i}"] = _wl(inp["mlp_w1"][i])
        sh[f"w2_{i}"] = _wl(inp["mlp_w2"][i])
    sh["fing"] = _pc(inp["final_g"])
    w_in = inp["a_w_in"][0]
    sh["w_in_a"] = _wl(w_in[:, 0:768])
    r = np.arange
    kr = list(r(768, 832)); kr_sw = list(r(800, 832)) + list(r(768, 800))
    ki = list(r(832, 896)); ki_sw = list(r(848, 864)) + list(r(832, 848)) + list(r(864, 896))
    sh["w_in_b"] = np.concatenate([_cols(w_in, c, 64) for c in (kr, kr_sw, ki, ki_sw)], 0)
    sh["w_in_i"] = np.ascontiguousarray(w_in[:, 896:912].reshape(16, 128, 16).transpose(1, 0, 2)).reshape(128, 256)
    dv = np.zeros((128, 16), f)
    dv[:, 0:4] = _pc(inp["a_q_norm_g"][0]); dv[:, 4:6] = _pc(inp["a_kv_norm_g"][0])
    perm = list(r(16, 32)) + list(r(0, 16)) + list(r(32, 64))
    dv[0:64, 6] = inp["a_kidx_ln_g"][0]; dv[0:64, 7] = inp["a_kidx_ln_b"][0]
    dv[0:64, 8] = inp["a_kidx_ln_g"][0][perm]; dv[0:64, 9] = inp["a_kidx_ln_b"][0][perm]
    sh["dsa_vec"] = dv
    wuq = inp["a_w_uq"][0]
    sh["w_uq_n"] = np.concatenate([_cols(wuq, list(r(h * 192, h * 192 + 128)), 128) for h in range(16)], 0)
    gs = []
    for h in range(16):
        b = h * 192 + 128
        gs.append(_cols(wuq, list(r(b, b + 64)), 64))
        gs.append(_cols(wuq, list(r(b + 32, b + 64)) + list(r(b, b + 32)), 64))
    sh["w_uq_r"] = np.concatenate(gs, 0)
    wqi = inp["a_w_qidx"][0]
    gs = []
    for h in range(16):
        b = h * 64
        gs.append(_cols(wqi, list(r(b, b + 64)), 64))
        gs.append(_cols(wqi, list(r(b + 16, b + 32)) + list(r(b, b + 16)) + list(r(b + 32, b + 64)), 64))
    sh["w_qi"] = np.concatenate(gs, 0)
    sh["w_ukT"] = np.ascontiguousarray(inp["a_w_uk"][0].transpose(2, 1, 0)).reshape(128, 16 * 256)
    sh["w_uv"] = np.ascontiguousarray(inp["a_w_uv"][0].reshape(2, 128, 2048).transpose(1, 0, 2)).reshape(128, 4096)
    sh["w_o"] = _wl(inp["a_w_o"][0])
    sh["b_w_w1"] = _wl(inp["b_w_w1"][0], 96)
    sh["b_w_a1"] = _wl(inp["b_w_a1"][0], 96)
    sh["b_w_g1"] = _wl(inp["b_w_g1"][0])
    for nm in ("v", "r", "k", "o"):
        sh["b_w_" + nm] = _wl(inp["b_w_" + nm][0])
    sh["b_w_w2"] = inp["b_w_w2"][0]
    sh["b_w_a2"] = inp["b_w_a2"][0]
    sh["b_w_g2"] = np.ascontiguousarray(inp["b_w_g2"][0].reshape(2, 128, 2048).transpose(1, 0, 2)).reshape(128, 4096)
    vecs = [inp["b_mu"][0][i] for i in range(6)] + [inp["b_w0"][0], inp["b_a0"][0], inp["b_k_k"][0], inp["b_k_a"][0],
                                                    inp["b_r_k"][0].reshape(-1), inp["b_gn_g"][0], inp["b_gn_b"][0]]
    sh["b_rvec"] = np.ascontiguousarray(np.stack([_pc(v) for v in vecs], 0).transpose(1, 0, 2)).reshape(128, 13 * 16)
    ii = np.arange(128)
    same = (ii[:, None] // 64) == (ii[None, :] // 64)
    Ms = ((ii[:, None] < ii[None, :]) & same).astype(f)
    Mi = ((ii[:, None] <= ii[None, :]) & same).astype(f)
    rmask = np.ones((128, 512), f); rmask[:, ::64] = 0.0
    sh["b_rconst"] = np.concatenate([Ms, Mi, Ms.T, same.astype(f), rmask], 1)
    cst = np.zeros((128, 48), f)
    p = np.arange(64)
    cst[0:64, 0] = 1.0 / (10000.0 ** ((2.0 * (p % 32)) / 64.0)).astype(f)
    cst[0:64, 1] = np.where(p < 32, -1.0, 1.0)
    cst[0:32, 2] = 1.0 / (10000.0 ** ((2.0 * (p[0:32] % 16)) / 32.0)).astype(f)
    cst[0:16, 3] = -1.0; cst[16:32, 3] = 1.0
    cst[:, 8:8 + NIT] = 2.0 ** -(np.arange(NIT, dtype=f) + 1.0)
    sh["cst"] = cst
    sh["ident"] = np.eye(128, dtype=f)
    return {k_: np.ascontiguousarray(v, dtype=f) for k_, v in sh.items()}


def core_inputs(inp, b, sh):
    m = dict(sh)
    m["xT"] = np.ascontiguousarray(inp["x"][b].T)
    m["cvec"] = _pc(inp["c"][b])
    m["pos"] = np.ascontiguousarray(inp["positions"][b].reshape(1, S).astype(np.int32))
    return m


_NC = None


def kernel(**inputs):
    global _NC
    inp = {k_: np.asarray(v) for k_, v in inputs.items()}
    if _NC is None:
        _NC = build()
    sh = shared_inputs(inp)
    in_maps = [core_inputs(inp, b, sh) for b in range(8)]
    res = run_bass_kernel_spmd(_NC, in_maps, core_ids=list(range(8)))
    out = np.stack([np.ascontiguousarray(res.results[b]["outT"].T) for b in range(8)], 0)
    return out.astype(np.float32)


def rwkv_phase(k, C, nc, xin, Bxin, xout, Bxout, mod, Wd, rk_d):
    Brk = Buf("rk_d", dram=True)
    xin_v = xin.rearrange("(c p) t -> p c t", p=128)
    xout_v = xout.rearrange("(c p) t -> p c t", p=128)
    rk_v = [rk_d[j].rearrange("(c p) t -> p c t", p=128) for j in range(7)]
    NEG_E = -math.exp(-0.5)
    with ExitStack() as esA:
        gam = sbt(k, esA, "r_gam", [128, 16, 32], F32)
        Bgam = Buf("r_gam")
        rcf = sbt(k, esA, "r_rcf", [128, 1024], F32)
        rcb = sbt(k, esA, "r_rcb", [128, 512], BF16)
        rv = sbt(k, esA, "r_rv", [128, 13, 16], F32)
        gne = sbt(k, esA, "r_gne", [128, 1], F32)
        Brc = Buf("r_rc")
        k.dma('sp', rcf[:], Wd['rconst'], reads=[C.Bin], writes=[Brc])
        k.dma('sp', rv[:], Wd['rvec'].rearrange("p (a b) -> p a b", a=13), reads=[C.Bin], writes=[Brc])
        k.op('dve', lambda e: e.tensor_copy(out=rcb[:], in_=rcf[:, 0:512]), reads=[Brc], writes=[Brc])
        k.op('dve', lambda e: e.memset(gne[:], 64e-5), writes=[Brc])
        omk = sbt(k, esA, "r_omk", [128, 16], F32)
        k.op('dve', lambda e: e.tensor_scalar(out=omk[:], in0=rv[:, 9, :], scalar1=-1.0, scalar2=1.0, op0=ALU.mult, op1=ALU.add), reads=[Brc], writes=[Brc])
        Ms_b, Mi_b, MsT_b, bd_b = rcb[:, 0:128], rcb[:, 128:256], rcb[:, 256:384], rcb[:, 384:512]
        rmask = rcf[:, 512:1024]
        with ExitStack() as es:
            ww2 = sbt(k, es, "ra_ww2", [96, 2048], BF16)
            wa2 = sbt(k, es, "ra_wa2", [96, 2048], BF16)
            wg2 = sbt(k, es, "ra_wg2", [128, 2, 2048], BF16)
            Bwr = Buf("ra_wres")
            k.dma('pool', ww2[:], Wd['w_w2'], reads=[C.Bw], writes=[Bwr])
            k.dma('pool', wa2[:], Wd['w_a2'], reads=[C.Bw], writes=[Bwr])
            k.dma('pool', wg2[:], Wd['w_g2'].rearrange("p (a b) -> p a b", a=2), reads=[C.Bw], writes=[Bwr])
            xt = sbt(k, es, "ra_xt", [128, 16, TT], F32)
            xtb = xt[:].bitcast(BF16)
            Bxt = [Buf(f"ra_xt{i}") for i in range(16)]
            hx = sbt(k, es, "ra_hx", [128, 16, TT + 1], BF16)
            Bhx = Buf("ra_hx")
            vT = sbt(k, es, "ra_vT", [128, 16, TT], BF16)
            BvT = Buf("ra_vT")
            rT = sbt(k, es, "ra_rT", [128, 16, TT], BF16)
            BrT = Buf("ra_rT")
            hw = sbt(k, es, "ra_hw", [96, TT], BF16)
            ha = sbt(k, es, "ra_ha", [96, TT], BF16)
            hg = sbt(k, es, "ra_hg", [128, 2, TT], BF16)
            Bhw, Bha, Bhg = Buf("ra_hw"), Buf("ra_ha"), Buf("ra_hg")
            wp = Pool(k, es, "ra_w", 3, [128, 16 * 128], BF16)
            C.sqpool = Pool(k, es, "ra_sq", 3, [128, TT], BF16)
            C.rspool = Pool(k, es, "ra_rs", 2, [128, TT], F32)
            C.tmppool = Pool(k, es, "ra_tmp", 3, [128, TT], F32)
            pspool = Pool(k, es, "ra_ps", 8, [128, TT], F32, psum=True)
            C.pstat = pspool
            pkpool = Pool.__new__(Pool)
            pkpool.items = pspool.items[6:8]
            pkpool.i = 0
            tf = {}

            def T(name, dt=F32):
                if name not in tf:
                    tf[name] = (sbt(k, es, "ra_t_" + name, [128, TT], dt), Buf("ra_t_" + name))
                return tf[name]
            stgpp = [Pool(k, es, f"ra_stg{p_}", 6, [128, TT], BF16) for p_ in range(2)]
            k.op('dve', lambda e: e.memset(hx[:, :, 0:1], 0.0), writes=[Bhx])

            def mix(c):
                for dc in range(16):
                    k.op('dve', lambda e: e.scalar_tensor_tensor(out=xtb[:, dc, TT:2 * TT], in0=xtb[:, dc, 0:TT], scalar=rv[:, c, dc:dc + 1],
                                                                 in1=hx[:, dc, 1:TT + 1], op0=ALU.mult, op1=ALU.add),
                         reads=[Bxt[dc], Bhx, Brc], writes=[Bxt[dc]])
            xm = lambda kc: xtb[:, kc, TT:2 * TT]

            for tt in range(NT):
                ts = slice(tt * TT, (tt + 1) * TT)
                if tt > 0:
                    k.op('dve', lambda e: e.tensor_copy(out=hx[:, :, 0:1], in_=hx[:, :, TT:TT + 1]), reads=[Bhx] + Bxt, writes=[Bhx])
                for dc in range(16):
                    k.dma('sp', xt[:, dc, :], xin_v[:, dc, ts], reads=[Bxin], writes=[Bxt[dc]])
                modnorm(k, C, xt, Bxt, TT, hx[:, :, 1:TT + 1], Bhx, mod, 16, 0)
                for dc in range(16):
                    k.op('dve', lambda e: e.tensor_tensor(out=xtb[:, dc, 0:TT], in0=hx[:, dc, 0:TT], in1=hx[:, dc, 1:TT + 1], op=ALU.subtract),
                         reads=[Bhx, Bxt[dc]], writes=[Bxt[dc]])
                mix(1)
                linear(k, Wd['w_w1'], C.Bw, 1, 16, 96, xm, Bxt, TT, wp, pspool,
                       lambda g, pt, Bp: k.op('act', lambda e: e.activation(out=hw[:], in_=pt[0:96, :], func=AF.Tanh), reads=[Bp], writes=[Bhw]))
                mix(4)
                linear(k, Wd['w_a1'], C.Bw, 1, 16, 96, xm, Bxt, TT, wp, pspool,
                       lambda g, pt, Bp: k.op('act', lambda e: e.copy(out=ha[:], in_=pt[0:96, :]), reads=[Bp], writes=[Bha]))
                mix(5)
                linear(k, Wd['w_g1'], C.Bw, 2, 16, 128, xm, Bxt, TT, wp, pspool,
                       lambda g, pt, Bp: k.op('act', lambda e: e.activation(out=hg[:, g, :], in_=pt[:], func=AF.Sigmoid), reads=[Bp], writes=[Bhg]))
                mix(3)
                linear(k, Wd['w_v'], C.Bw, 16, 16, 128, xm, Bxt, TT, wp, pspool,
                       lambda g, pt, Bp: k.op('act', lambda e: e.copy(out=vT[:, g, :], in_=pt[:]), reads=[Bp], writes=[BvT]))
                k.dma('sp', rk_v[4][:, :, ts], vT[:], reads=[BvT], writes=[Brk])
                mix(0)
                linear(k, Wd['w_r'], C.Bw, 16, 16, 128, xm, Bxt, TT, wp, pspool,
                       lambda g, pt, Bp: k.op('act', lambda e: e.copy(out=rT[:, g, :], in_=pt[:]), reads=[Bp], writes=[BrT]))
                mix(2)

                pend = []

                def epiK(fc, pk, Bpk):
                    k.rec = []
                    epiK_body(fc, pk, Bpk, str(fc % 2))
                    pend.append(k.rec)
                    k.rec = None
                    if len(pend) == 2:
                        k.replay(pend)
                        pend.clear()

                def epiK_body(fc, pk, Bpk, pr):
                    fcs = slice(fc * 128, (fc + 1) * 128)
                    stgp = stgpp[fc % 2]
                    bA, bB, bC = pspool.items[3 * (fc % 2)], pspool.items[3 * (fc % 2) + 1], pspool.items[3 * (fc % 2) + 2]
                    kf, Bkf = T("kf" + pr)
                    k.op('act', lambda e: e.copy(out=kf[:], in_=pk[:]), reads=[Bpk], writes=[Bkf])
                    pz, Bpz = bA
                    k.op('pe', lambda e: e.matmul(pz[:], lhsT=ww2[0:96, fcs], rhs=hw[0:96, :], start=True, stop=True), reads=[Bwr, Bhw], writes=[Bpz])
                    pa, Bpa = bB
                    k.op('pe', lambda e: e.matmul(pa[:], lhsT=wa2[0:96, fcs], rhs=ha[0:96, :], start=True, stop=True), reads=[Bwr, Bha], writes=[Bpa])
                    pg, Bpg = bC
                    for kc in range(2):
                        k.op('pe', lambda e, kc=kc: e.matmul(pg[:], lhsT=wg2[:, kc, fcs], rhs=hg[:, kc, :], start=(kc == 0), stop=(kc == 1)),
                             reads=[Bwr, Bhg], writes=[Bpg])
                    a, Ba = T("a" + pr); sg, Bsg = T("sg" + pr); lw, Blw = sg, Bsg; g_, Bg_ = T("g" + pr); E1, BE1 = T("E1" + pr)
                    E2, BE2 = T("E2" + pr, BF16); gp, Bgp = T("gp" + pr); E3, BE3 = T("E3" + pr, BF16)
                    kkr, Bkkr = T("kkr" + pr, BF16); nr, Bnr = T("nr" + pr); kk, Bkk = T("kk" + pr, BF16)
                    t_, Bt_ = T("t" + pr, BF16); kp, Bkp = T("kp" + pr); b_, Bb_ = T("b" + pr, BF16); sq, Bsq = T("sq" + pr, BF16)
                    rkb, Brkb = T("rkb" + pr, BF16)
                    k.op('act', lambda e: e.activation(out=a[:], in_=pa[:], func=AF.Sigmoid, bias=rv[:, 7, fc:fc + 1]), reads=[Bpa, Brc], writes=[Ba])
                    k.op('act', lambda e: e.activation(out=sg[:], in_=pz[:], func=AF.Sigmoid, bias=rv[:, 6, fc:fc + 1]), reads=[Bpz, Brc], writes=[Bsg])
                    k.op('dve', lambda e: e.tensor_tensor_scan(out=g_[:], data0=rmask, data1=lw[:], initial=0.0, op0=ALU.mult, op1=ALU.add),
                         reads=[Blw, Brc], writes=[Bg_])
                    k.op('act', lambda e: e.activation(out=E1[:], in_=g_[:], func=AF.Exp, scale=NEG_E), reads=[Bg_], writes=[BE1])
                    k.op('act', lambda e: e.activation(out=E2[:], in_=g_[:], func=AF.Exp, scale=-NEG_E), reads=[Bg_], writes=[BE2])
                    k.op('dve', lambda e: e.tensor_tensor(out=gp[:], in0=g_[:], in1=lw[:], op=ALU.subtract), reads=[Bg_, Blw], writes=[Bgp])
                    k.op('act', lambda e: e.activation(out=E3[:], in_=gp[:], func=AF.Exp, scale=NEG_E), reads=[Bgp], writes=[BE3])
                    k.op('dve', lambda e: e.tensor_copy(out=gam[:, fc, tt * 8:(tt + 1) * 8], in_=E1[:].rearrange("p (a b) -> p a b", a=8)[:, :, 63]),
                         reads=[BE1], writes=[Bgam])
                    k.op('act', lambda e: e.activation(out=kkr[:], in_=pk[:], func=AF.Identity, scale=rv[:, 8, fc:fc + 1]),
                         reads=[Bpk, Brc], writes=[Bkkr])
                    k.op('act', lambda e: e.activation(out=sq[:], in_=kkr[:], func=AF.Square), reads=[Bkkr], writes=[Bsq])
                    pss, Bpss = bA
                    k.op('pe', lambda e: e.matmul(pss[:], lhsT=bd_b, rhs=sq[:], start=True, stop=True), reads=[Bsq, Brc], writes=[Bpss])
                    k.op('dve', lambda e: e.tensor_scalar(out=nr[:], in0=pss[:], scalar1=1e-24, scalar2=None, op0=ALU.max), reads=[Bpss], writes=[Bnr])
                    k.op('act', lambda e: e.activation(out=nr[:], in_=nr[:], func=AF.Ln), reads=[Bnr], writes=[Bnr])
                    k.op('act', lambda e: e.activation(out=nr[:], in_=nr[:], func=AF.Exp, scale=-0.5), reads=[Bnr], writes=[Bnr])
                    k.op('dve', lambda e: e.tensor_tensor(out=kk[:], in0=kkr[:], in1=nr[:], op=ALU.mult), reads=[Bkkr, Bnr], writes=[Bkk])
                    k.op('act', lambda e: e.activation(out=t_[:], in_=a[:], func=AF.Identity, scale=rv[:, 9, fc:fc + 1], bias=omk[:, fc:fc + 1]),
                         reads=[Ba, Brc], writes=[Bt_])
                    k.op('dve', lambda e: e.tensor_tensor(out=kp[:], in0=t_[:], in1=kf[:], op=ALU.mult), reads=[Bt_, Bkf], writes=[Bkp])
                    k.op('dve', lambda e: e.tensor_tensor(out=b_[:], in0=kk[:], in1=a[:], op=ALU.mult), reads=[Bkk, Ba], writes=[Bb_])
                    outs = []
                    sA, BsA = stgp.next()
                    k.op('dve', lambda e: e.scalar_tensor_tensor(out=sA[:], in0=kk[:], scalar=-1.0, in1=E3[:], op0=ALU.mult, op1=ALU.mult),
                         reads=[Bkk, BE3], writes=[BsA])
                    outs.append((0, sA, BsA))
                    sK, BsK = stgp.next()
                    k.op('dve', lambda e: e.tensor_tensor(out=sK[:], in0=kp[:], in1=E2[:], op=ALU.mult), reads=[Bkp, BE2], writes=[BsK])
                    outs.append((1, sK, BsK))
                    sB, BsB = stgp.next()
                    k.op('dve', lambda e: e.tensor_tensor(out=sB[:], in0=b_[:], in1=E2[:], op=ALU.mult), reads=[Bb_, BE2], writes=[BsB])
                    outs.append((2, sB, BsB))
                    sR, BsR = stgp.next()
                    k.op('dve', lambda e: e.tensor_tensor(out=sR[:], in0=rT[:, fc, :], in1=E1[:], op=ALU.mult), reads=[BrT, BE1], writes=[BsR])
                    outs.append((3, sR, BsR))
                    k.op('dve', lambda e: e.scalar_tensor_tensor(out=rkb[:], in0=rT[:, fc, :], scalar=rv[:, 10, fc:fc + 1], in1=kp[:],
                                                                 op0=ALU.mult, op1=ALU.mult), reads=[BrT, Bkp, Brc], writes=[Brkb])
                    pc, Bpc = bB
                    k.op('pe', lambda e: e.matmul(pc[:], lhsT=bd_b, rhs=rkb[:], start=True, stop=True), reads=[Brkb, Brc], writes=[Bpc])
                    sN, BsN = stgp.next()
                    k.op('dve', lambda e: e.tensor_tensor(out=sN[:], in0=pc[:], in1=vT[:, fc, :], op=ALU.mult), reads=[Bpc, BvT], writes=[BsN])
                    outs.append((5, sN, BsN))
                    sG, BsG = stgp.next()
                    k.op('act', lambda e: e.copy(out=sG[:], in_=pg[:]), reads=[Bpg], writes=[BsG])
                    outs.append((6, sG, BsG))
                    for (j, s_, Bs_) in outs:
                        k.dma('sp', rk_v[j][:, fc, ts], s_[:], reads=[Bs_], writes=[Brk])
                linear(k, Wd['w_k'], C.Bw, 16, 16, 128, xm, Bxt, TT, wp, pkpool, epiK)
        k.barrier()
        with ExitStack() as es:
            ldp = [Pool(k, es, f"rb_ld{j}", 2, [128, 16, 128], BF16) for j in range(7)]
            tokp = [sbt(k, es, f"rb_tok{j}", [128, 2048], BF16) for j in range(4)]
            Btok = [Buf(f"rb_tok{j}") for j in range(4)]
            RES = []
            ArkT = sbt(k, es, "rb_Ark", [128, 32, 128], BF16)
            ArbT = sbt(k, es, "rb_Arb", [128, 32, 128], BF16)
            BArk, BArb = Buf("rb_Ark"), Buf("rb_Arb")

            U0 = sbt(k, es, "rb_U0", [128, 2048], F32)
            BU0 = Buf("rb_U0")
            WT = sbt(k, es, "rb_WT", [128, 16, 128], BF16)
            BWT = Buf("rb_WT")
            Utok = sbt(k, es, "rb_Utok", [128, 2048], BF16)
            BUtok = Buf("rb_Utok")
            otok = sbt(k, es, "rb_otok", [128, 32, 64], F32)
            Botok = Buf("rb_otok")
            oc_ = sbt(k, es, "rb_oc", [128, 32, 64], F32)
            onb = sbt(k, es, "rb_on", [128, 2048], BF16)
            Bon = Buf("rb_on")
            st2 = sbt(k, es, "rb_st2", [128, 4, 32], F32)
            Bst2 = Buf("rb_st2")
            ST_f = sbt(k, es, "rb_STf", [128, 16, 64], F32)
            ST_b = sbt(k, es, "rb_STb", [128, 16, 128], BF16)
            stmp = sbt(k, es, "rb_stmp", [128, 16, 64], F32)
            BSTf, BSTb, Bstmp = Buf("rb_STf"), Buf("rb_STb"), Buf("rb_stmp")
            ev = Pool(k, es, "rb_ev", 2, [128, 8, 128], F32)
            ogT = sbt(k, es, "rb_ogT", [128, 16, TT], BF16)
            BogT = Buf("rb_ogT")
            wop = Pool(k, es, "rb_wo", 3, [128, 16 * 128], BF16)
            xcp = Pool(k, es, "rb_xc", 2, [128, TT], F32)
            pp = Pool(k, es, "rb_pp", 8, [128, 512], F32, psum=True)
            for u in range(2):
                g5_ = [Pool(k, es, f"rb_g{j}_{u}", 1, [128, 4, 128], BF16) for j in range(3)]
                pmp_ = Pool(k, es, f"rb_pm{u}", 4, [128, 4, 128], BF16)
                ttp_ = Pool(k, es, f"rb_tt{u}", 2, [128, 4, 128], BF16)
                Yp_ = Pool(k, es, f"rb_Y{u}", 1, [128, 4, 64], BF16)
                pps = Pool.__new__(Pool)
                pps.items = pp.items[4 * u:4 * u + 4]
                pps.i = 0
                RES.append((g5_, pmp_, ttp_, Yp_, pps))
            k.op('dve', lambda e: e.memset(ST_f[:], 0.0), writes=[BSTf])
            k.op('dve', lambda e: e.memset(ST_b[:], 0.0), writes=[BSTb])
            k.op('dve', lambda e: e.memset(Utok[:], 0.0), writes=[BUtok])
            def load_tile(ti_):
                X_ = []
                for j in range(7):
                    t_, B_ = ldp[j].next()
                    k.dma('sp', t_[:], rk_v[j][:, :, ti_ * 128:(ti_ + 1) * 128], reads=[Brk], writes=[B_])
                    X_.append((t_, B_))
                return X_
            Xn = load_tile(0)
            for ti in range(16):
                tc = slice(ti * 128, (ti + 1) * 128)
                X = Xn
                if ti + 1 < 16:
                    Xn = load_tile(ti + 1)
                (At, BAt), (Kt, BKt), (Bt, BBt), (Rt, BRt), (Vt, BVt), (BON, BBON), (GATE, BGATE) = X
                for j, (src, Bsrc) in enumerate([(Vt, BVt), (Kt, BKt), (Bt, BBt), (At, BAt)]):
                    for half in range(2):
                        pt, Bp = pp.next()
                        ptb = pt[:].bitcast(BF16)
                        for q in range(8):
                            k.op('pe', lambda e: e.transpose(out=ptb[:, q * 128:(q + 1) * 128], in_=src[:, half * 8 + q, :], identity=C.ident_bf[:]),
                                 reads=[Bsrc, C.Bconst], writes=[Bp])
                        k.op('act', lambda e: e.copy(out=tokp[j][:, half * 1024:(half + 1) * 1024], in_=ptb[:, 0:1024]), reads=[Bp], writes=[Btok[j]])
                Vtok, Ktok, Btk, Atok = tokp
                BVtok, BKtok, BBtok, BAtok = Btok
                Ark4 = ArkT[:].rearrange("p (f q) t -> p f q t", q=2)
                Arb4 = ArbT[:].rearrange("p (f q) t -> p f q t", q=2)
                U04 = U0[:].rearrange("p (f q i) -> p f q i", q=2, i=64)
                def group(gq, R):
                    g5, pmp, ttp, Yp, pp = R
                    par, gg = gq // 4, gq % 4
                    fs = slice(par * 64, par * 64 + 64)
                    fcl = [gg * 4 + j for j in range(4)]
                    hs = [2 * fc + par for fc in fcl]
                    M0, BM0 = g5[0].next()
                    P0, BP0 = g5[1].next()
                    Aak, BAak = g5[2].next()
                    prods = [((Bt, BBt), (At, BAt), Ms_b, M0[:], BM0), ((At, BAt), (Bt, BBt), MsT_b, P0[:], BP0),
                             ((Kt, BKt), (At, BAt), Ms_b, Aak[:], BAak), ((Kt, BKt), (Rt, BRt), Mi_b, Ark4[:, gg * 4:(gg + 1) * 4, par, :], BArk),
                             ((Bt, BBt), (Rt, BRt), Mi_b, Arb4[:, gg * 4:(gg + 1) * 4, par, :], BArb)]
                    for ((L, BL), (R_, BR), msk, dst, Bdst) in prods:
                        pt, Bp = pp.next()
                        for j, fc in enumerate(fcl):
                            k.op('pe', lambda e: e.matmul(pt[:, j * 128:(j + 1) * 128], lhsT=L[fs, fc, :], rhs=R_[fs, fc, :], start=True, stop=True),
                                 reads=[BL, BR], writes=[Bp])
                        k.op('dve', lambda e: e.tensor_tensor(out=dst, in0=pt[:].rearrange("p (a b) -> p a b", a=4),
                                                              in1=msk.unsqueeze(1).to_broadcast([128, 4, 128]), op=ALU.mult), reads=[Bp, Brc], writes=[Bdst])
                    pt, Bp = pp.next()
                    for j, h in enumerate(hs):
                        k.op('pe', lambda e: e.matmul(pt[:, j * 64:(j + 1) * 64], lhsT=Aak[:, j, :], rhs=Vtok[:, h * 64:(h + 1) * 64], start=True, stop=True),
                             reads=[BAak, BVtok], writes=[Bp])
                    Yg, BYg = Yp.next()
                    k.op('act', lambda e: e.copy(out=Yg[:], in_=pt[:, 0:256].rearrange("p (a b) -> p a b", a=4)), reads=[Bp], writes=[BYg])
                    TTt, BTT = ttp.next()
                    k.op('dve', lambda e: e.tensor_tensor(out=TTt[:], in0=M0[:], in1=C.ident_bf[:].unsqueeze(1).to_broadcast([128, 4, 128]), op=ALU.add),
                         reads=[BM0, C.Bconst], writes=[BTT])
                    Pk, BPk, Mk, BMk = P0, BP0, M0, BM0
                    for lev in range(1, 6):
                        pt, Bp = pp.next()
                        for j in range(4):
                            k.op('pe', lambda e: e.matmul(pt[:, j * 128:(j + 1) * 128], lhsT=Mk[:, j, :], rhs=Pk[:, j, :], start=True, stop=True),
                                 reads=[BMk, BPk], writes=[Bp])
                        Pn, BPn = pmp.next()
                        k.op('act', lambda e: e.copy(out=Pn[:], in_=pt[:].rearrange("p (a b) -> p a b", a=4)), reads=[Bp], writes=[BPn])
                        if lev < 5:
                            pt2, Bp2 = pp.next()
                            for j in range(4):
                                k.op('pe', lambda e: e.matmul(pt2[:, j * 128:(j + 1) * 128], lhsT=Pk[:, j, :], rhs=Mk[:, j, :], start=True, stop=True),
                                     reads=[BMk, BPk], writes=[Bp2])
                            Mn, BMn = pmp.next()
                            k.op('act', lambda e: e.copy(out=Mn[:], in_=pt2[:].rearrange("p (a b) -> p a b", a=4)), reads=[Bp2], writes=[BMn])
                        pt3, Bp3 = pp.next()
                        for j in range(4):
                            k.op('pe', lambda e: e.matmul(pt3[:, j * 128:(j + 1) * 128], lhsT=Pn[:, j, :], rhs=TTt[:, j, :], start=True, stop=True),
                                 reads=[BPn, BTT], writes=[Bp3])
                        TTn, BTTn = ttp.next()
                        k.op('dve', lambda e: e.tensor_tensor(out=TTn[:], in0=pt3[:].rearrange("p (a b) -> p a b", a=4), in1=TTt[:], op=ALU.add),
                             reads=[Bp3, BTT], writes=[BTTn])
                        TTt, BTT = TTn, BTTn
                        Pk, BPk = Pn, BPn
                        if lev < 5:
                            Mk, BMk = Mn, BMn
                    pt, Bp = pp.next()
                    for j in range(4):
                        k.op('pe', lambda e: e.matmul(pt[:, j * 64:(j + 1) * 64], lhsT=TTt[:, j, :], rhs=Yg[:, j, :], start=True, stop=True),
                             reads=[BTT, BYg], writes=[Bp])
                    k.op('act', lambda e: e.copy(out=U04[:, gg * 4:(gg + 1) * 4, par, :], in_=pt[:, 0:256].rearrange("p (a b) -> p a b", a=4)),
                         reads=[Bp], writes=[BU0])
                    pt, Bp = pp.next()
                    for j, h in enumerate(hs):
                        k.op('pe', lambda e: e.matmul(pt[fs, j * 128:(j + 1) * 128], lhsT=Atok[:, h * 64:(h + 1) * 64], rhs=TTt[:, j, :],
                                                      start=True, stop=True), reads=[BAtok, BTT], writes=[Bp])
                    k.op('act', lambda e: e.copy(out=WT[fs, gg * 4:(gg + 1) * 4, :], in_=pt[fs, :].rearrange("p (a b) -> p a b", a=4)), reads=[Bp], writes=[BWT])
                for gq in range(0, 8, 2):
                    chains = []
                    for u in range(2):
                        k.rec = []
                        group(gq + u, RES[u])
                        chains.append(k.rec)
                        k.rec = None
                    k.replay(chains)
                for c in range(2):
                    cs = slice(64 * c, 64 * c + 64)
                    pu = [pp.next() for _ in range(4)]
                    for fc in range(16):
                        k.op('pe', lambda e: e.matmul(pu[fc // 4][0][cs, (fc % 4) * 128:(fc % 4 + 1) * 128], lhsT=WT[:, fc, cs], rhs=ST_b[:, fc, :],
                                                      start=True, stop=True), reads=[BWT, BSTb], writes=[pu[fc // 4][1]])
                    for q in range(4):
                        k.op('dve', lambda e: e.tensor_tensor(out=Utok[cs, q * 512:(q + 1) * 512], in0=pu[q][0][cs, :], in1=U0[cs, q * 512:(q + 1) * 512], op=ALU.add),
                             reads=[pu[q][1], BU0], writes=[BUtok])
                    po = [pp.next() for _ in range(4)]
                    for fc in range(16):
                        k.op('pe', lambda e: e.matmul(po[fc // 4][0][cs, (fc % 4) * 128:(fc % 4 + 1) * 128], lhsT=Rt[:, fc, cs], rhs=ST_b[:, fc, :],
                                                      start=True, stop=False), reads=[BRt, BSTb], writes=[po[fc // 4][1]])
                        for par in range(2):
                            h = 2 * fc + par
                            hc = slice(h * 64, (h + 1) * 64)
                            o_ap = po[fc // 4][0][cs, (h % 8) * 64:(h % 8 + 1) * 64]
                            k.op('pe', lambda e: e.matmul(o_ap, lhsT=ArkT[:, h, cs], rhs=Vtok[:, hc], start=False, stop=False),
                                 reads=[BArk, BVtok], writes=[po[fc // 4][1]])
                            k.op('pe', lambda e: e.matmul(o_ap, lhsT=ArbT[:, h, cs], rhs=Utok[:, hc], start=False, stop=(par == 1)),
                                 reads=[BArb, BUtok], writes=[po[fc // 4][1]])
                    for q in range(4):
                        k.op('act', lambda e: e.copy(out=otok[cs, q * 8:(q + 1) * 8, :], in_=po[q][0][cs, :].rearrange("p (a b) -> p a b", a=8)),
                             reads=[po[q][1]], writes=[Botok])
                    pS = [pp.next() for _ in range(2)]
                    for h in range(32):
                        fc, fs = h // 2, slice((h % 2) * 64, (h % 2) * 64 + 64)
                        s_ap = pS[fc // 8][0][fs, (fc % 8) * 64:(fc % 8 + 1) * 64]
                        hc = slice(h * 64, (h + 1) * 64)
                        k.op('pe', lambda e: e.matmul(s_ap, lhsT=Ktok[cs, hc], rhs=Vtok[cs, hc], start=True, stop=False),
                             reads=[BKtok, BVtok], writes=[pS[fc // 8][1]])
                        k.op('pe', lambda e: e.matmul(s_ap, lhsT=Btk[cs, hc], rhs=Utok[cs, hc], start=False, stop=True),
                             reads=[BBtok, BUtok], writes=[pS[fc // 8][1]])
                    ci = ti * 2 + c
                    for q in range(2):
                        k.op('dve', lambda e: e.tensor_tensor(out=stmp[:, q * 8:(q + 1) * 8, :], in0=pS[q][0][:].rearrange("p (a b) -> p a b", a=8),
                                                              in1=ST_f[:, q * 8:(q + 1) * 8, :], op=ALU.add), reads=[pS[q][1], BSTf], writes=[Bstmp])
                    k.op('dve', lambda e: e.tensor_tensor(out=ST_f[:], in0=stmp[:], in1=gam[:, :, ci:ci + 1].to_broadcast([128, 16, 64]), op=ALU.mult),
                         reads=[Bstmp, Bgam], writes=[BSTf])
                    k.op('act', lambda e: e.copy(out=ST_b[0:64, :, 0:64], in_=ST_f[0:64, :, :]), reads=[BSTf], writes=[BSTb])
                    k.op('act', lambda e: e.copy(out=ST_b[64:128, :, 64:128], in_=ST_f[64:128, :, :]), reads=[BSTf], writes=[BSTb])
                k.op('dve', lambda e: e.tensor_reduce(out=st2[:, 0, :], in_=otok[:], axis=AX.X, op=ALU.add), reads=[Botok], writes=[Bst2])
                k.op('dve', lambda e: e.tensor_scalar(out=st2[:, 1, :], in0=st2[:, 0, :], scalar1=-1.0 / 64, scalar2=None, op0=ALU.mult), reads=[Bst2], writes=[Bst2])
                k.op('dve', lambda e: e.tensor_tensor(out=oc_[:], in0=otok[:], in1=st2[:, 1, :].unsqueeze(2).to_broadcast([128, 32, 64]), op=ALU.add),
                     reads=[Botok, Bst2], writes=[Bon])
                k.op('act', lambda e: e.activation(out=otok[:], in_=oc_[:], func=AF.Square), reads=[Bon], writes=[Botok])
                k.op('dve', lambda e: e.tensor_reduce(out=st2[:, 2, :], in_=otok[:], axis=AX.X, op=ALU.add), reads=[Botok], writes=[Bst2])
                k.op('act', lambda e: e.activation(out=st2[:, 3, :], in_=st2[:, 2, :], func=AF.Sqrt, scale=1.0 / 64, bias=gne[:]), reads=[Bst2, Brc], writes=[Bst2])
                k.op('dve', lambda e: e.reciprocal(out=st2[:, 3, :], in_=st2[:, 3, :]), reads=[Bst2], writes=[Bst2])
                k.op('dve', lambda e: e.tensor_tensor(out=onb[:].rearrange("p (a b) -> p a b", a=32), in0=oc_[:],
                                                      in1=st2[:, 3, :].unsqueeze(2).to_broadcast([128, 32, 64]), op=ALU.mult), reads=[Bon, Bst2], writes=[Bon])
                tsub = slice((ti % 4) * 128, (ti % 4 + 1) * 128)
                for half in range(2):
                    pt, Bp = pp.next()
                    ptb = pt[:].bitcast(BF16)
                    for q in range(8):
                        fc = half * 8 + q
                        k.op('pe', lambda e: e.transpose(out=ptb[:, q * 128:(q + 1) * 128], in_=onb[:, fc * 128:(fc + 1) * 128], identity=C.ident_bf[:]),
                             reads=[Bon, C.Bconst], writes=[Bp])
                    e_, Be_ = ev.next()
                    hs_ = slice(half * 8, half * 8 + 8)
                    k.op('dve', lambda e: e.tensor_tensor(out=e_[:], in0=ptb[:, 0:1024].rearrange("p (a b) -> p a b", a=8),
                                                          in1=rv[:, 11, hs_].unsqueeze(2).to_broadcast([128, 8, 128]), op=ALU.mult), reads=[Bp, Brc], writes=[Be_])
                    k.op('dve', lambda e: e.tensor_tensor(out=e_[:], in0=e_[:], in1=rv[:, 12, hs_].unsqueeze(2).to_broadcast([128, 8, 128]), op=ALU.add),
                         reads=[Be_, Brc], writes=[Be_])
                    k.op('dve', lambda e: e.tensor_tensor(out=e_[:], in0=e_[:], in1=BON[:, hs_, :], op=ALU.add), reads=[Be_, BBON], writes=[Be_])
                    k.op('dve', lambda e: e.tensor_tensor(out=ogT[:, hs_, tsub], in0=e_[:], in1=GATE[:, hs_, :], op=ALU.mult), reads=[Be_, BGATE], writes=[BogT])
                if ti % 4 == 3:
                    tsl = slice((ti // 4) * TT, (ti // 4 + 1) * TT)

                    def epiO(g, pt, Bp):
                        xc_, Bxc_ = xcp.next()
                        k.dma('sp', xc_[:], xin_v[:, g, tsl], reads=[Bxin], writes=[Bxc_])
                        k.op('dve', lambda e: e.scalar_tensor_tensor(out=xc_[:], in0=pt[:], scalar=mod[:, 32 + g:33 + g], in1=xc_[:],
                                                                     op0=ALU.mult, op1=ALU.add), reads=[Bp, Bxc_, C.Bmod], writes=[Bxc_])
                        k.dma('sp', xout_v[:, g, tsl], xc_[:], reads=[Bxc_], writes=[Bxout])
                    linear(k, Wd['w_o'], C.Bw, 16, 16, 128, lambda kc: ogT[:, kc, :], [BogT], TT, wop, pp, epiO)
    k.barrier()
```
